# Optimizing a Trainium2 kernel written in Bass

```python
import jax, jax.numpy as jnp
from jax import lax
import numpy as np

D_MODEL = 1024
BATCH = 32
SEQ = 2048
DEPTH = 1

N_META = 16
GRID_W = 64
Q_BLOCK = 128
ROPE_THETA = 10000.0
RMS_EPS = 1e-6
LN_EPS = 1e-5
D_FF = 2816

A_HEADS = 8
A_KV_HEADS = 2
A_HEAD_DIM = 64
A_GROUP = A_HEADS // A_KV_HEADS
B_HEADS = 8
B_Q_RANK = 256
B_KV_RANK = 128
B_NOPE_DIM = 64
B_ROPE_DIM = 32
B_V_DIM = 64
B_QK_DIM = B_NOPE_DIM + B_ROPE_DIM

A_WIDTH = A_HEADS * A_HEAD_DIM
B_WIDTH = B_HEADS * B_V_DIM
MIX_WIDTH = A_WIDTH + B_WIDTH
IN_SPLITS = (A_HEADS * A_HEAD_DIM, A_KV_HEADS * A_HEAD_DIM, A_KV_HEADS * A_HEAD_DIM, B_Q_RANK, B_KV_RANK, B_ROPE_DIM)
IN_WIDTH = sum(IN_SPLITS)
IN_OFFSETS = [int(o) for o in np.cumsum(IN_SPLITS)[:-1]]

DEEPNORM_ALPHA = (2.0 * DEPTH) ** 0.25
DEEPNORM_BETA = (8.0 * DEPTH) ** -0.25

kernel_name = "hybrid_gqa_mla_macaron_deepnorm_encoder"


def rms_norm(x, g):
    xf = x.astype(jnp.float32)
    y = xf * lax.rsqrt(jnp.mean(xf * xf, axis=-1, keepdims=True) + RMS_EPS)
    return (y * g.astype(jnp.float32)).astype(x.dtype)


def layer_norm(x, g, b):
    xf = x.astype(jnp.float32)
    mu = jnp.mean(xf, axis=-1, keepdims=True)
    var = jnp.mean(jnp.square(xf - mu), axis=-1, keepdims=True)
    y = (xf - mu) * lax.rsqrt(var + LN_EPS)
    return (y * g.astype(jnp.float32) + b.astype(jnp.float32)).astype(x.dtype)


def swiglu(x, w1, w3, w2):
    return (jax.nn.silu(x @ w1) * (x @ w3)) @ w2


def axial_rope_tables(n_tokens, rot_dim, dtype):
    rows_count = n_tokens // GRID_W
    rows = jnp.repeat(jnp.arange(rows_count, dtype=jnp.int32), GRID_W)
    cols = jnp.tile(jnp.arange(GRID_W, dtype=jnp.int32), rows_count)
    axis_dim = rot_dim // 2
    inv_freq = ROPE_THETA ** (-jnp.arange(0, axis_dim, 2, dtype=jnp.float32) / axis_dim)
    ang = jnp.concatenate([rows.astype(jnp.float32)[:, None] * inv_freq,
                           cols.astype(jnp.float32)[:, None] * inv_freq], axis=-1)
    ang = jnp.concatenate([jnp.zeros((N_META, rot_dim // 2), jnp.float32), ang], axis=0)
    return jnp.cos(ang).astype(dtype), jnp.sin(ang).astype(dtype)


def apply_rope(x, cos, sin):
    half = x.shape[-1] // 2
    x1, x2 = x[..., :half], x[..., half:]
    c, s = cos[None, :, None, :], sin[None, :, None, :]
    return jnp.concatenate([x1 * c - x2 * s, x1 * s + x2 * c], axis=-1)


def block_attention(q, k, v, scale):
    def attend(qb):
        s = jnp.einsum("bqhgd,bkhd->bhgqk", qb, k).astype(jnp.float32) * scale
        p = jax.nn.softmax(s, axis=-1).astype(v.dtype)
        return jnp.einsum("bhgqk,bkhd->bqhgd", p, v)

    bsz, length, hkv, grp, dk = q.shape
    n_real = length - N_META
    o_meta = attend(q[:, :N_META])
    q_real = q[:, N_META:].reshape(bsz, n_real // Q_BLOCK, Q_BLOCK, hkv, grp, dk)
    o_real = lax.map(attend, jnp.moveaxis(q_real, 1, 0))
    o_real = jnp.moveaxis(o_real, 0, 1).reshape(bsz, n_real, hkv, grp, v.shape[-1])
    return jnp.concatenate([o_meta, o_real], axis=1)


def parallel_mixer(h, cos_a, sin_a, cos_b, sin_b, w_in, q_norm_a, k_norm_a, cq_norm, ckv_norm,
                   w_uq, w_ukv, out_norm_a, out_norm_b, w_out):
    bsz, length, _ = h.shape
    proj = h @ w_in
    q_a, k_a, v_a, c_q, c_kv, k_pe = jnp.split(proj, IN_OFFSETS, axis=-1)

    q_a = apply_rope(rms_norm(q_a.reshape(bsz, length, A_HEADS, A_HEAD_DIM), q_norm_a), cos_a, sin_a)
    k_a = apply_rope(rms_norm(k_a.reshape(bsz, length, A_KV_HEADS, A_HEAD_DIM), k_norm_a), cos_a, sin_a)
    v_a = v_a.reshape(bsz, length, A_KV_HEADS, A_HEAD_DIM)
    q_a = q_a.reshape(bsz, length, A_KV_HEADS, A_GROUP, A_HEAD_DIM)
    o_a = block_attention(q_a, k_a, v_a, A_HEAD_DIM ** -0.5).reshape(bsz, length, A_WIDTH)

    q_b = (rms_norm(c_q, cq_norm) @ w_uq).reshape(bsz, length, B_HEADS, B_QK_DIM)
    q_nope, q_pe = q_b[..., :B_NOPE_DIM], q_b[..., B_NOPE_DIM:]
    q_b = jnp.concatenate([q_nope, apply_rope(q_pe, cos_b, sin_b)], axis=-1)
    kv_b = (rms_norm(c_kv, ckv_norm) @ w_ukv).reshape(bsz, length, B_HEADS, B_NOPE_DIM + B_V_DIM)
    k_nope, v_b = kv_b[..., :B_NOPE_DIM], kv_b[..., B_NOPE_DIM:]
    k_pe = apply_rope(k_pe[:, :, None, :], cos_b, sin_b)
    k_b = jnp.concatenate([k_nope, jnp.broadcast_to(k_pe, (bsz, length, B_HEADS, B_ROPE_DIM))], axis=-1)
    q_b = q_b.reshape(bsz, length, B_HEADS, 1, B_QK_DIM)
    o_b = block_attention(q_b, k_b, v_b, B_QK_DIM ** -0.5).reshape(bsz, length, B_WIDTH)

    o = jnp.concatenate([rms_norm(o_a, out_norm_a), rms_norm(o_b, out_norm_b)], axis=-1)
    return o @ w_out


def _normal(k, shape, scale):
    return jax.random.normal(k, shape, jnp.float32) * scale


def setup_inputs(seed: int = 0) -> dict:
    key = jax.random.key(seed)
    ks = jax.random.split(key, 24)
    D, L = D_MODEL, DEPTH
    gain = lambda k, n: 1.0 + _normal(k, (L, n), 0.02)
    bias = lambda k, n: _normal(k, (L, n), 0.02)
    return {
        "x": _normal(ks[0], (BATCH, SEQ, D), 1.0),
        "meta_tokens": _normal(ks[1], (N_META, D), 1.0),
        "ffn1_w1": _normal(ks[2], (L, D, D_FF), D ** -0.5),
        "ffn1_w3": _normal(ks[3], (L, D, D_FF), D ** -0.5),
        "ffn1_w2": _normal(ks[4], (L, D_FF, D), DEEPNORM_BETA * D_FF ** -0.5),
        "ln1_g": gain(ks[5], D),
        "ln1_b": bias(ks[6], D),
        "w_in": _normal(ks[7], (L, D, IN_WIDTH), D ** -0.5),
        "q_norm_a": gain(ks[8], A_HEAD_DIM),
        "k_norm_a": gain(ks[9], A_HEAD_DIM),
        "cq_norm": gain(ks[10], B_Q_RANK),
        "ckv_norm": gain(ks[11], B_KV_RANK),
        "w_uq": _normal(ks[12], (L, B_Q_RANK, B_HEADS * B_QK_DIM), B_Q_RANK ** -0.5),
        "w_ukv": _normal(ks[13], (L, B_KV_RANK, B_HEADS * (B_NOPE_DIM + B_V_DIM)), B_KV_RANK ** -0.5),
        "out_norm_a": gain(ks[14], A_WIDTH),
        "out_norm_b": gain(ks[15], B_WIDTH),
        "w_out": _normal(ks[16], (L, MIX_WIDTH, D), DEEPNORM_BETA * MIX_WIDTH ** -0.5),
        "ln2_g": gain(ks[17], D),
        "ln2_b": bias(ks[18], D),
        "ffn2_w1": _normal(ks[19], (L, D, D_FF), D ** -0.5),
        "ffn2_w3": _normal(ks[20], (L, D, D_FF), D ** -0.5),
        "ffn2_w2": _normal(ks[21], (L, D_FF, D), DEEPNORM_BETA * D_FF ** -0.5),
        "ln3_g": gain(ks[22], D),
        "ln3_b": bias(ks[23], D),
    }


def reference(x, meta_tokens, ffn1_w1, ffn1_w3, ffn1_w2, ln1_g, ln1_b, w_in, q_norm_a, k_norm_a,
              cq_norm, ckv_norm, w_uq, w_ukv, out_norm_a, out_norm_b, w_out, ln2_g, ln2_b,
              ffn2_w1, ffn2_w3, ffn2_w2, ln3_g, ln3_b):
    bsz, n_tok, d = x.shape
    meta = jnp.broadcast_to(meta_tokens[None].astype(x.dtype), (bsz, N_META, d))
    h = jnp.concatenate([meta, x], axis=1)
    cos_a, sin_a = axial_rope_tables(n_tok, A_HEAD_DIM, x.dtype)
    cos_b, sin_b = axial_rope_tables(n_tok, B_ROPE_DIM, x.dtype)
    alpha = DEEPNORM_ALPHA
    for i in range(DEPTH):
        h = layer_norm(alpha * h + 0.5 * swiglu(h, ffn1_w1[i], ffn1_w3[i], ffn1_w2[i]), ln1_g[i], ln1_b[i])
        mix = parallel_mixer(h, cos_a, sin_a, cos_b, sin_b, w_in[i], q_norm_a[i], k_norm_a[i],
                             cq_norm[i], ckv_norm[i], w_uq[i], w_ukv[i], out_norm_a[i], out_norm_b[i], w_out[i])
        h = layer_norm(alpha * h + mix, ln2_g[i], ln2_b[i])
        h = layer_norm(alpha * h + 0.5 * swiglu(h, ffn2_w1[i], ffn2_w3[i], ffn2_w2[i]), ln3_g[i], ln3_b[i])
    return h[:, N_META:]
```

```python
import numpy as np
from contextlib import ExitStack
import concourse.bass as bass
import concourse.mybir as mybir
from concourse.bass_utils import run_bass_kernel_spmd

F32 = mybir.dt.float32
BF16 = mybir.dt.bfloat16
AF = mybir.ActivationFunctionType
ALU = mybir.AluOpType
AX = mybir.AxisListType

D = 1024
FF = 2816
NFC = 22
SEQ = 2048
NMETA = 16
LK = SEQ + NMETA
NSEQ = 4
ALPHA = 2.0 ** 0.25
LN_EPS = 1e-5
RMS_EPS = 1e-6
SC_A = 64 ** -0.5
SC_B = 96 ** -0.5


class Buf:
    __slots__ = ("name", "w", "r", "psum")

    def __init__(self, name, psum=False):
        self.name = name
        self.w = None
        self.r = []
        self.psum = psum


class Op:
    __slots__ = ("eng", "fn", "waits", "needs_inc", "seq", "dma_sem")

    def __init__(self, eng, fn):
        self.eng = eng
        self.fn = fn
        self.waits = []
        self.needs_inc = False
        self.seq = None
        self.dma_sem = None


class Prog:
    def __init__(self, nc):
        self.nc = nc
        self.ops = {e: [] for e in ("pe", "act", "dve", "pool", "sp")}
        self.dma_sems = {}

    def _deps(self, eng, reads, writes, is_dma=False):
        deps = []
        for b in reads:
            if b.w is not None:
                deps.append((b.w, True))
            if b.psum:
                for t in b.r:
                    if t[0] == "c" and t[1].eng != eng:
                        deps.append((t, True))
        for b in writes:
            if b.w is not None:
                deps.append((b.w, False))
            for t in b.r:
                deps.append((t, False))
        out = []
        for t, raw in deps:
            if t[0] == "c":
                o = t[1]
                if o.eng == eng and not is_dma:
                    if eng == "pe":
                        continue
                o.needs_inc = True
            out.append(t)
        return out

    def _commit(self, tok, reads, writes):
        for b in reads:
            b.r.append(tok)
        for b in writes:
            b.w = tok
            b.r = []

    def op(self, eng, fn, reads=(), writes=()):
        o = Op(eng, fn)
        o.waits = self._deps(eng, reads, writes)
        self.ops[eng].append(o)
        self._commit(("c", o), reads, writes)
        return o

    def dma(self, q, fn, sem, reads=(), writes=(), final=False):
        o = Op(q, fn)
        o.waits = [t for t in self._deps(q, reads, writes, True) if not (t[0] == "d" and t[1] == sem and t[2] is None)]
        ent = self.dma_sems.setdefault(sem, [None, 0])
        ent[1] += 16
        o.dma_sem = sem
        self.ops[q].append(o)
        self._commit(("d", sem, None if final else ent[1]), reads, writes)
        return o

    def emit(self):
        nc = self.nc
        with ExitStack() as st:
            esem = {e: st.enter_context(nc.semaphore("s_" + e)) for e in self.ops}
            for name, ent in self.dma_sems.items():
                ent[0] = st.enter_context(nc.semaphore("d_" + name))
            for e, lst in self.ops.items():
                c = 0
                for o in lst:
                    if o.dma_sem is None and o.needs_inc:
                        c += 1
                        o.seq = c
            block = st.enter_context(nc.Block())
            starters = {"pe": block.tensor, "act": block.scalar, "dve": block.vector,
                        "pool": block.gpsimd, "sp": block.sync}

            def run_engine(e):
                lst = self.ops[e]

                def body(eng):
                    seen = {}
                    for o in lst:
                        for t in o.waits:
                            if t[0] == "c":
                                key, val, sem = t[1].eng, t[1].seq, esem[t[1].eng]
                            else:
                                ent = self.dma_sems[t[1]]
                                key, sem = "d:" + t[1], ent[0]
                                val = ent[1] if t[2] is None else t[2]
                            if seen.get(key, 0) >= val:
                                continue
                            seen[key] = val
                            eng.wait_ge(sem, val)
                        ins = o.fn(eng)
                        if o.dma_sem is not None:
                            ins.then_inc(self.dma_sems[o.dma_sem][0], 16)
                        elif o.needs_inc:
                            ins.then_inc(esem[e], 1)
                    if e == "sp":
                        for ent in self.dma_sems.values():
                            eng.wait_ge(ent[0], ent[1])
                starters[e](body)

            for e in ("sp", "pool", "act", "dve", "pe"):
                run_engine(e)


def build_program(nseq=NSEQ, nch=4, nfc=22, stop=99):
    SEQ_ = nch * 512
    LK_ = SEQ_ + 128
    NKT = nch * 4
    FF_ = nfc * 128
    nc = bass.Bass("TRN2", target_bir_lowering=False, dynamic_dma_scratch_size=8192)
    P = Prog(nc)

    def din(name, shape):
        return nc.dram_tensor(name, list(shape), F32, kind="ExternalInput").ap()

    x_d = din("x", [nseq, SEQ_, D])
    meta_d = din("meta", [NMETA, D])
    fw = {}
    for f in (1, 2):
        fw[f] = (din(f"f{f}w1", [D, FF_]), din(f"f{f}w3", [D, FF_]), din(f"f{f}w2", [FF_, D]))
    w_in_d = din("w_in", [D, 1184])
    w_uq_d = din("w_uq", [256, 768])
    w_ukv_d = din("w_ukv", [128, 1024])
    w_out_d = din("w_out", [D, D])
    ln_d = {i: (din(f"ln{i}_g", [1, D]), din(f"ln{i}_b", [1, D])) for i in (1, 2, 3)}
    qn_d = din("q_norm_a", [1, 64])
    kn_d = din("k_norm_a", [1, 64])
    cqn_d = din("cq_norm", [1, 256])
    ckvn_d = din("ckv_norm", [1, 128])
    ona_d = din("out_norm_a", [1, 512])
    onb_d = din("out_norm_b", [1, 512])
    ropeA_d = din("ropeA", [SEQ_, 64])
    ropeB_d = din("ropeB", [SEQ_, 32])
    ident_d = din("ident", [128, 128])
    out_d = nc.dram_tensor("out", [nseq, SEQ_, D], F32, kind="ExternalOutput").ap()

    def dscr(name, shape, dt=BF16):
        return nc.dram_tensor(name, list(shape), dt, kind="Internal").ap()

    w13s = {f: dscr(f"w13s{f}", [nfc, 128, 2, 8, 128]) for f in (1, 2)}
    w2s = {f: dscr(f"w2s{f}", [FF_, D]) for f in (1, 2)}
    winq_s = dscr("winq_s", [D, 768])
    wuq_s = dscr("wuq_s", [256, 768])
    wout_s = dscr("wout_s", [D, D])
    h1_s = dscr("h1_s", [SEQ_, D], F32)

    with ExitStack() as st:
        def sb(name, shape, dt):
            return st.enter_context(nc.sbuf_tensor(name, list(shape), dt))

        def ps(name, shape, dt):
            return st.enter_context(nc.psum_tensor(name, list(shape), dt))

        KT_A = sb("KT_A", [128, LK_], BF16)
        KT_B = sb("KT_B", [128, 8, LK_], BF16)
        V_A = sb("V_A", [128, NKT + 1, 2, 65], BF16)
        V_B = sb("V_B", [128, NKT + 1, 8, 65], BF16)
        hc = sb("hc", [128, 4, D], F32)
        aT = sb("aT", [128, 8, 512], BF16)
        G = sb("G", [128, 24, 512], BF16)
        o_tm = G[:].rearrange("p a b -> p (a b)")[:, 0:8192].bitcast(F32).rearrange("p (t c) -> p t c", t=4)
        on_tm = G[:].rearrange("p a b -> p (a b)")[:, 8192:12288].rearrange("p (t c) -> p t c", t=4)
        NR13, NR2 = 3, 4
        r13 = [sb(f"r13_{i}", [128, 2, 8, 128], BF16) for i in range(NR13)]
        r2 = [sb(f"r2_{i}", [128, 1024], BF16) for i in range(NR2)]
        QT_A = sb("QT_A", [128, 8, 512], BF16)
        QT_B = sb("QT_B", [128, 8, 512], BF16)
        NPT = 3
        PT = [sb(f"PT{i}", [128, 512], BF16) for i in range(NPT)]
        lnp = [sb(f"lnp{i}", [128, 2, D], F32) for i in range(2)]
        ropeA = sb("ropeA_t", [128, NKT, 64], F32)
        ropeB = sb("ropeB_t", [128, NKT, 32], F32)
        gq = sb("gq", [128, 64], F32)
        gk = sb("gk", [128, 64], F32)
        gcq = sb("gcq", [128, 256], F32)
        gckv = sb("gckv", [128, 128], F32)
        gout = sb("gout", [128, 1024], F32)
        identf = sb("identf", [128, 128], F32)
        identb = sb("identb", [128, 128], BF16)
        sel = sb("sel", [128, 96], BF16)
        epsL = sb("epsL", [128, 1], F32)
        epsR = sb("epsR", [128, 1], F32)
        winkv = sb("winkv", [128, 8, 416], BF16)
        wukp = sb("wukp", [128, 8, 96], BF16)
        wuv = sb("wuv", [128, 8, 64], BF16)
        sa = [sb(f"sa{i}", [128, 512], F32) for i in range(2)]
        OTs = [sb(f"OTs{i}", [128, 512], F32) for i in range(2)]
        sq = sb("sq", [128, 768], F32)
        nrm = sb("nrm", [128, 512], F32)
        rt = [sb(f"rt{i}", [128, 256], F32) for i in range(4)]
        rotb = sb("rotb", [128, 768], BF16)
        cnb = sb("cnb", [128, 256], BF16)
        ckvT = sb("ckvT", [128, 512], BF16)
        kpeT = sb("kpeT", [128, 512], BF16)
        cqT = sb("cqT", [128, 2, 512], BF16)
        stats = sb("stats", [128, 4, 2, 6], F32)
        mv = sb("mv", [128, 4, 2], F32)
        rstd = sb("rstd", [128, 4], F32)
        nmr = sb("nmr", [128, 4], F32)
        ss = sb("ss", [128, 16], F32)
        rs = sb("rs", [128, 16], F32)
        rcp = sb("rcp", [128, 4], F32)

        PS = [ps(f"ps{i}", [128, 1024], F32) for i in range(4)]

        def bank(i):
            return PS[i // 2][:, (i % 2) * 512:(i % 2) * 512 + 512]

        def bank_bf(i):
            return bank(i).bitcast(BF16)

        PJ = PS[3]
        pb = [Buf(f"bank{i}", psum=True) for i in range(8)]

        B = {}

        def bf(name):
            if name not in B:
                B[name] = Buf(name)
            return B[name]

        KA = [bf(f"KA{i}") for i in range(NKT + 1)]
        KB = [bf(f"KB{i}") for i in range(NKT + 1)]
        VA = [bf(f"VA{i}") for i in range(NKT + 1)]
        VB = [bf(f"VB{i}") for i in range(NKT + 1)]
        hcB = [bf(f"hc{i}") for i in range(4)]
        aTB = [bf(f"aT{i}") for i in range(4)]
        GB = [bf(f"G{i}") for i in range(24)]
        r13B = [bf(f"r13_{i}") for i in range(NR13)]
        r2B = [bf(f"r2_{i}") for i in range(NR2)]
        PTB = [bf(f"PT{i}") for i in range(NPT)]
        lnpB = [bf(f"lnp{i}") for i in range(2)]
        saB = [bf(f"sa{i}") for i in range(2)]
        OTsB = [bf(f"OTs{i}") for i in range(2)]
        h1sB = [bf(f"h1s{i}") for i in range(NKT)]
        cB = bf("consts")
        qaB, qbB = bf("QT_A"), bf("QT_B")
        cvA, cvB, cvC = bf("cvA"), bf("cvB"), bf("cvC")

        P.dma("sp", lambda e: e.dma_start(out=identf[:], in_=ident_d), "c0", writes=[cB], final=True)
        P.dma("sp", lambda e: e.dma_start(out=ropeA[:], in_=ropeA_d.rearrange("(t p) c -> p t c", p=128)), "c0", writes=[cB], final=True)
        P.dma("sp", lambda e: e.dma_start(out=ropeB[:], in_=ropeB_d.rearrange("(t p) c -> p t c", p=128)), "c0", writes=[cB], final=True)
        for tile_, src in ((gq, qn_d), (gk, kn_d), (gcq, cqn_d), (gckv, ckvn_d)):
            P.dma("sp", lambda e, tile_=tile_, src=src: e.dma_start(out=tile_[:], in_=src.partition_broadcast(128)), "c0", writes=[cB], final=True)
        P.dma("sp", lambda e: e.dma_start(out=gout[:, 0:512], in_=ona_d.partition_broadcast(128)), "c0", writes=[cB], final=True)
        P.dma("sp", lambda e: e.dma_start(out=gout[:, 512:1024], in_=onb_d.partition_broadcast(128)), "c0", writes=[cB], final=True)
        c2 = bf("consts2")
        P.op("dve", lambda e: e.tensor_copy(out=identb[:], in_=identf[:]), reads=[cB], writes=[c2])
        P.op("pool", lambda e: e.memset(sel[:], 0.0), writes=[c2])
        P.op("dve", lambda e: e.tensor_copy(out=sel[0:32, 64:96], in_=identf[0:32, 0:32]), reads=[cB, c2], writes=[c2])
        P.op("pool", lambda e: e.memset(kpeT[:], 0.0), writes=[bf("kpeT")])
        P.op("pool", lambda e: e.memset(epsL[:], LN_EPS), writes=[c2])
        P.op("pool", lambda e: e.memset(epsR[:], RMS_EPS), writes=[c2])
        P.op("pool", lambda e: e.memset(V_A[:], 1.0), writes=VA)
        P.op("pool", lambda e: e.memset(V_B[:], 1.0), writes=VB)
        P.op("pool", lambda e: e.memset(V_A[:, NKT], 0.0), writes=[VA[NKT]])
        P.op("pool", lambda e: e.memset(V_B[:, NKT], 0.0), writes=[VB[NKT]])
        P.op("pool", lambda e: e.memset(V_A[0:NMETA, NKT, :, 64:65], 1.0), writes=[VA[NKT]])
        P.op("pool", lambda e: e.memset(V_B[0:NMETA, NKT, :, 64:65], 1.0), writes=[VB[NKT]])
        P.op("pool", lambda e: e.memset(KT_A[:, SEQ_:SEQ_ + 128], 0.0), writes=[KA[NKT]])
        P.op("pool", lambda e: e.memset(KT_B[:, :, SEQ_:SEQ_ + 128], 0.0), writes=[KB[NKT]])
        P.op("pool", lambda e: e.memset(QT_A[:], 0.0), writes=[bf("QT_A")])
        P.op("pool", lambda e: e.memset(wukp[:], 0.0), writes=[cvA])
        P.dma("pool", lambda e: e.dma_start(out=winkv[:, :, 0:256], in_=w_in_d[:, 512:768].rearrange("(kc p) n -> p kc n", p=128)), "cvA", writes=[cvA], final=True)
        P.dma("pool", lambda e: e.dma_start(out=winkv[:, :, 256:416], in_=w_in_d[:, 1024:1184].rearrange("(kc p) n -> p kc n", p=128)), "cvA", writes=[cvA], final=True)
        ukv4 = w_ukv_d.rearrange("p (h two d) -> p h two d", h=8, two=2)
        P.dma("pool", lambda e: e.dma_start(out=wukp[:, :, 0:64], in_=ukv4[:, :, 0, :]), "cvA", writes=[cvA], final=True)
        P.dma("pool", lambda e: e.dma_start(out=wuv[:], in_=ukv4[:, :, 1, :]), "cvA", writes=[cvA], final=True)

        def conv_ffn(f, semname, buf):
            w1, w3, w2 = fw[f]
            for fc in range(nfc):
                for m, w in enumerate((w1, w3)):
                    src = w.rearrange("(kc p) (fc j) -> fc p kc j", p=128, j=128)[fc]
                    P.dma("pool", lambda e, src=src, fc=fc, m=m: e.dma_start(out=w13s[f][fc, :, m, :, :], in_=src),
                          semname, writes=[buf], final=True)
            for fc in range(nfc):
                P.dma("pool", lambda e, fc=fc: e.dma_start(out=w2s[f][fc * 128:(fc + 1) * 128, :], in_=w2[fc * 128:(fc + 1) * 128, :]),
                      semname, writes=[buf], final=True)

        conv_ffn(1, "cvA", cvA)
        for kc in range(8):
            P.dma("pool", lambda e, kc=kc: e.dma_start(out=winq_s[kc * 128:(kc + 1) * 128, 0:512], in_=w_in_d[kc * 128:(kc + 1) * 128, 0:512]), "cvB", writes=[cvB], final=True)
            P.dma("pool", lambda e, kc=kc: e.dma_start(out=winq_s[kc * 128:(kc + 1) * 128, 512:768], in_=w_in_d[kc * 128:(kc + 1) * 128, 768:1024]), "cvB", writes=[cvB], final=True)
        for kc in range(8):
            P.dma("pool", lambda e, kc=kc: e.dma_start(out=wout_s[kc * 128:(kc + 1) * 128, :], in_=w_out_d[kc * 128:(kc + 1) * 128, :]), "cvB", writes=[cvB], final=True)
        conv_ffn(2, "cvC", cvC)

        class Stream:
            def __init__(self, slots, bufs, semprefix):
                self.slots, self.bufs, self.pref = slots, bufs, semprefix
                self.items = []
                self.issued = 0
                self.taken = 0

            def _issue(self):
                i = self.issued
                fn, rd = self.items[i]
                s = i % len(self.slots)
                P.dma("sp", lambda e, fn=fn, s=s: fn(e, self.slots[s]), f"{self.pref}{s}", reads=[rd], writes=[self.bufs[s]])
                self.issued += 1

            def take(self):
                i = self.taken
                while self.issued < min(len(self.items), i + len(self.slots)):
                    self._issue()
                self.taken += 1
                s = i % len(self.slots)
                return self.slots[s], self.bufs[s]

        S13 = Stream(r13, r13B, "r13_")
        S2 = Stream(r2, r2B, "r2_")
        cvbuf = {1: cvA, 2: cvC}

        def sched_ffn(f):
            for fc in range(nfc):
                S13.items.append((lambda e, slot, fc=fc, f=f: e.dma_start(out=slot[:], in_=w13s[f][fc]), cvbuf[f]))
            for half in range(2):
                for fp in range(nfc // 2):
                    src = w2s[f][fp * 256:(fp + 1) * 256, half * 512:(half + 1) * 512].rearrange("(a p) n -> p a n", p=128)
                    S2.items.append((lambda e, slot, src=src: e.dma_start(out=slot[:].rearrange("p (a n) -> p a n", a=2), in_=src), cvbuf[f]))

        def sched_mixB_q():
            for tp in range(2):
                for kc in range(8):
                    S2.items.append((lambda e, slot, kc=kc: e.dma_start(out=slot[:, 0:768], in_=winq_s[kc * 128:(kc + 1) * 128, :]), cvB))

        def sched_mixB_o():
            for tp in range(2):
                for kc in range(8):
                    S2.items.append((lambda e, slot, kc=kc: e.dma_start(out=slot[:], in_=wout_s[kc * 128:(kc + 1) * 128, :]), cvB))

        sched_ffn(1)
        for s in range(nseq):
            for c in range(nch):
                sched_ffn(1)
            for c in range(nch):
                sched_mixB_q()
                sched_mixB_o()
                sched_ffn(2)

        ln_cur = [None, None]

        def ensure_ln(i, slot):
            if ln_cur[slot] == i:
                return
            ln_cur[slot] = i
            g_d, b_d = ln_d[i]
            P.dma("sp", lambda e: e.dma_start(out=lnp[slot][:, 0, :], in_=g_d.partition_broadcast(128)), f"lnp{slot}", writes=[lnpB[slot]])
            P.dma("sp", lambda e: e.dma_start(out=lnp[slot][:, 1, :], in_=b_d.partition_broadcast(128)), f"lnp{slot}", writes=[lnpB[slot]])

        evac_rr = [0]

        def evac(out, in_, reads, writes):
            evac_rr[0] ^= 1
            if evac_rr[0]:
                P.op("dve", lambda e: e.tensor_copy(out=out, in_=in_), reads=reads, writes=writes)
            else:
                P.op("act", lambda e: e.activation(out=out, in_=in_, func=AF.Copy), reads=reads, writes=writes)

        def transposes_to_aT(nt, R, tiles=None):
            for t in (range(nt) if tiles is None else tiles):
                for kc in range(8):
                    P.op("pe", lambda e, t=t, kc=kc: e.transpose(out=PJ[:, kc * 128:kc * 128 + R], in_=hc[0:R, t, kc * 128:(kc + 1) * 128], identity=identf[0:R, 0:R]),
                         reads=[hcB[t], cB], writes=[pb[6 + kc // 4]])
                pj3 = PJ[:].rearrange("p (k c) -> p k c", k=8)
                evac(aT[:, 0:4, t * 128:t * 128 + R], pj3[:, 0:4, 0:R], [pb[6]], [aTB[t]])
                evac(aT[:, 4:8, t * 128:t * 128 + R], pj3[:, 4:8, 0:R], [pb[7]], [aTB[t]])

        def ffn(f, T, R, nt, ln_fused=False):
            for t in range(nt):
                P.op("act", lambda e, t=t: e.activation(out=hc[0:R, t, :], in_=hc[0:R, t, :], func=AF.Copy, scale=ALPHA),
                     reads=[hcB[t]], writes=[hcB[t]])
            for fc in range(nfc):
                slot, sbuf_ = S13.take()
                ua, ub = (0, 1) if fc % 2 == 0 else (2, 3)
                for m, bk in ((0, ua), (1, ub)):
                    for kc in range(8):
                        P.op("pe", lambda e, slot=slot, m=m, bk=bk, kc=kc: e.matmul(bank(bk)[:, 0:T], lhsT=slot[:, m, kc, :], rhs=aT[:, kc, 0:T], start=(kc == 0), stop=(kc == 7)),
                             reads=[sbuf_] + aTB[0:nt], writes=[pb[bk]])
                si = fc % 2
                P.op("act", lambda e, si=si, ua=ua: e.activation(out=sa[si][:, 0:T], in_=bank(ua)[:, 0:T], func=AF.Silu), reads=[pb[ua]], writes=[saB[si]])
                P.op("dve", lambda e, si=si, ub=ub, fc=fc: e.tensor_tensor(out=G[:, fc, 0:T], in0=sa[si][:, 0:T], in1=bank(ub)[:, 0:T], op=ALU.mult),
                     reads=[saB[si], pb[ub]], writes=[GB[fc]])
            for half in range(2):
                for fp in range(nfc // 2):
                    slot, sbuf_ = S2.take()
                    for a in range(2):
                        fc = fp * 2 + a
                        for t in range(nt):
                            P.op("pe", lambda e, slot=slot, a=a, fc=fc, t=t: e.matmul(bank(4 + t)[0:R, :], lhsT=G[:, fc, t * 128:t * 128 + R], rhs=slot[:, a * 512:(a + 1) * 512], start=(fc == 0), stop=(fc == nfc - 1)),
                                 reads=[sbuf_, GB[fc]], writes=[pb[4 + t]])
                deferred = []
                for t in range(nt):
                    def ev(t=t, half=half):
                        P.op("dve", lambda e: e.scalar_tensor_tensor(out=hc[0:R, t, half * 512:(half + 1) * 512], in0=bank(4 + t)[0:R, :], scalar=0.5, in1=hc[0:R, t, half * 512:(half + 1) * 512], op0=ALU.mult, op1=ALU.add),
                             reads=[pb[4 + t], hcB[t]], writes=[hcB[t]])
                    if ln_fused and half == 1:
                        deferred.append(ev)
                    else:
                        ev()
                        if ln_fused:
                            P.op("dve", lambda e, t=t: e.bn_stats(out=stats[0:R, t, 0, :], in_=hc[0:R, t, 0:512]), reads=[hcB[t]], writes=[stB[t]])
            return deferred

        sB = {n: bf(n) for n in ("stats", "mv", "rstd", "nmr", "sq", "sq2", "ss", "ss2", "rs", "nrm", "rt0", "rt1", "rt2", "rt3", "rotb", "cnb", "ckvT", "kpeT", "cqT", "rcp")}

        def layernorm(slot, R, nt):
            for t in range(nt):
                for h in range(2):
                    P.op("dve", lambda e, t=t, h=h: e.bn_stats(out=stats[0:R, t, h, :], in_=hc[0:R, t, h * 512:(h + 1) * 512]), reads=[hcB[t]], writes=[sB["stats"]])
                P.op("dve", lambda e, t=t: e.bn_aggr(out=mv[0:R, t, :], in_=stats[0:R, t].rearrange("p a b -> p (a b)")), reads=[sB["stats"]], writes=[sB["mv"]])
            P.op("act", lambda e: e.activation(out=rstd[0:R, 0:nt], in_=mv[0:R, 0:nt, 1], func=AF.Sqrt, bias=epsL[0:R, :], scale=1.0), reads=[sB["mv"], c2], writes=[sB["rstd"]])
            P.op("dve", lambda e: e.reciprocal(out=rstd[0:R, 0:nt], in_=rstd[0:R, 0:nt]), reads=[sB["rstd"]], writes=[sB["rstd"]])
            P.op("dve", lambda e: e.scalar_tensor_tensor(out=nmr[0:R, 0:nt], in0=mv[0:R, 0:nt, 0], scalar=-1.0, in1=rstd[0:R, 0:nt], op0=ALU.mult, op1=ALU.mult),
                 reads=[sB["mv"], sB["rstd"]], writes=[sB["nmr"]])
            for t in range(nt):
                P.op("act", lambda e, t=t: e.activation(out=hc[0:R, t, :], in_=hc[0:R, t, :], func=AF.Identity, scale=rstd[0:R, t:t + 1], bias=nmr[0:R, t:t + 1]),
                     reads=[hcB[t], sB["rstd"], sB["nmr"]], writes=[hcB[t]])
                P.op("dve", lambda e, t=t: e.tensor_tensor(out=hc[0:R, t, :], in0=hc[0:R, t, :], in1=lnp[slot][0:R, 0, :], op=ALU.mult), reads=[hcB[t], lnpB[slot]], writes=[hcB[t]])
                P.op("dve", lambda e, t=t: e.tensor_tensor(out=hc[0:R, t, :], in0=hc[0:R, t, :], in1=lnp[slot][0:R, 1, :], op=ALU.add), reads=[hcB[t], lnpB[slot]], writes=[hcB[t]])

        stB = [bf(f"st{i}") for i in range(4)]
        mvB = [bf(f"mv{i}") for i in range(4)]
        rsB = [bf(f"rsd{i}") for i in range(4)]

        def ln_pipeline(slot, R, nt, pre=None, post=None, have_h0=False):
            def S1(t):
                if pre:
                    pre[t]()
                for h in ((1,) if have_h0 else (0, 1)):
                    P.op("dve", lambda e, h=h: e.bn_stats(out=stats[0:R, t, h, :], in_=hc[0:R, t, h * 512:(h + 1) * 512]), reads=[hcB[t]], writes=[stB[t]])
                P.op("dve", lambda e: e.bn_aggr(out=mv[0:R, t, :], in_=stats[0:R, t].rearrange("p a b -> p (a b)")), reads=[stB[t]], writes=[mvB[t]])
                P.op("act", lambda e: e.activation(out=rstd[0:R, t:t + 1], in_=mv[0:R, t, 1:2], func=AF.Sqrt, bias=epsL[0:R, :], scale=1.0), reads=[mvB[t], c2], writes=[rsB[t]])

            def S2(t):
                P.op("dve", lambda e: e.reciprocal(out=rstd[0:R, t:t + 1], in_=rstd[0:R, t:t + 1]), reads=[rsB[t]], writes=[rsB[t]])
                P.op("dve", lambda e: e.scalar_tensor_tensor(out=nmr[0:R, t:t + 1], in0=mv[0:R, t, 0:1], scalar=-1.0, in1=rstd[0:R, t:t + 1], op0=ALU.mult, op1=ALU.mult),
                     reads=[mvB[t], rsB[t]], writes=[rsB[t]])
                P.op("act", lambda e: e.activation(out=hc[0:R, t, :], in_=hc[0:R, t, :], func=AF.Identity, scale=rstd[0:R, t:t + 1], bias=nmr[0:R, t:t + 1]),
                     reads=[hcB[t], rsB[t]], writes=[hcB[t]])

            def S3(t):
                P.op("dve", lambda e: e.tensor_tensor(out=hc[0:R, t, :], in0=hc[0:R, t, :], in1=lnp[slot][0:R, 0, :], op=ALU.mult), reads=[hcB[t], lnpB[slot]], writes=[hcB[t]])
                P.op("dve", lambda e: e.tensor_tensor(out=hc[0:R, t, :], in0=hc[0:R, t, :], in1=lnp[slot][0:R, 1, :], op=ALU.add), reads=[hcB[t], lnpB[slot]], writes=[hcB[t]])

            for i in range(nt + 3):
                for k, st in enumerate((S1, S2, S3, post)):
                    t = i - k
                    if st is not None and 0 <= t < nt:
                        st(t)

        def rms_rstd(R, n, inv_n):
            P.op("act", lambda e: e.activation(out=rs[0:R, 0:n], in_=ss[0:R, 0:n], func=AF.Sqrt, bias=epsR[0:R, :], scale=inv_n), reads=[sB["ss"], sB["ss2"], c2], writes=[sB["rs"]])
            P.op("dve", lambda e: e.reciprocal(out=rs[0:R, 0:n], in_=rs[0:R, 0:n]), reads=[sB["rs"]], writes=[sB["rs"]])

        def passA(src_fn, T, R, nt, kbase, rope_t0, store_h1):
            kcol0 = kbase * 128
            for t in range(nt):
                P.dma("sp", lambda e, t=t: e.dma_start(out=hc[0:R, t, :], in_=src_fn(t)), f"ldhc{t}", writes=[hcB[t]])
            transposes_to_aT(nt, R)
            ensure_ln(1, 0)
            dfr = ffn(1, T, R, nt, ln_fused=True)

            def postA(t):
                if store_h1:
                    P.dma("sp", lambda e: e.dma_start(out=h1_s[(kbase + t) * 128:(kbase + t + 1) * 128, :], in_=hc[:, t, :]), f"sthc{t}", reads=[hcB[t]], writes=[h1sB[kbase + t]])
                transposes_to_aT(nt, R, tiles=[t])
            ln_pipeline(0, R, nt, pre=dfr, post=postA, have_h0=True)
            for t in range(nt):
                kt = kbase + t
                for kc in range(8):
                    P.op("pe", lambda e, t=t, kc=kc: e.matmul(PJ[0:R, 0:416], lhsT=aT[:, kc, t * 128:t * 128 + R], rhs=winkv[:, kc, :], start=(kc == 0), stop=(kc == 7)),
                         reads=[aTB[t], cvA], writes=[pb[6]])
                P.op("act", lambda e: e.activation(out=sq[0:R, 0:128], in_=PJ[0:R, 0:128], func=AF.Square), reads=[pb[6]], writes=[sB["sq"]])
                P.op("act", lambda e: e.activation(out=sq[0:R, 256:384], in_=PJ[0:R, 256:384], func=AF.Square, scale=0.5 ** 0.5, accum_out=ss[0:R, 2:3]), reads=[pb[6]], writes=[sB["sq2"], sB["ss2"]])
                P.op("dve", lambda e: e.tensor_reduce(out=ss[0:R, 0:2], in_=sq[0:R, 0:128].rearrange("p (h d) -> p h d", d=64), axis=AX.X, op=ALU.add), reads=[sB["sq"]], writes=[sB["ss"]])
                rms_rstd(R, 3, 1.0 / 64)
                for h in range(2):
                    P.op("dve", lambda e, h=h: e.scalar_tensor_tensor(out=nrm[0:R, h * 64:(h + 1) * 64], in0=PJ[0:R, h * 64:(h + 1) * 64], scalar=rs[0:R, h:h + 1], in1=gk[0:R, :], op0=ALU.mult, op1=ALU.mult),
                         reads=[pb[6], sB["rs"], cB], writes=[sB["nrm"]])
                s3 = nrm[0:R, 0:128].rearrange("p (h d) -> p h d", h=2)
                d3 = rotb[0:R, 0:128].rearrange("p (h d) -> p h d", h=2)
                if rope_t0 is None:
                    P.op("dve", lambda e, s3=s3, d3=d3: e.tensor_copy(out=d3, in_=s3), reads=[sB["nrm"]], writes=[sB["rotb"]])
                else:
                    tab = ropeA[0:R, rope_t0 + t, :]
                    rope4(s3[:, :, 0:32], s3[:, :, 32:64], tab[:, 0:32].unsqueeze(1).broadcast_to([R, 2, 32]), tab[:, 32:64].unsqueeze(1).broadcast_to([R, 2, 32]),
                          d3[:, :, 0:32], d3[:, :, 32:64], lambda i: rt[i][0:R, 0:64].rearrange("p (h d) -> p h d", h=2), [sB["nrm"]])
                P.op("pe", lambda e: e.transpose(out=bank_bf(5)[:, 0:R], in_=rotb[0:R, 0:128], identity=identb[0:R, 0:R]), reads=[sB["rotb"], c2], writes=[pb[5]])
                evac(KT_A[:, kt * 128:kt * 128 + R], bank_bf(5)[:, 0:R], [pb[5]], [KA[kt]])
                P.op("act", lambda e, kt=kt: e.activation(out=V_A[0:R, kt, :, 0:64], in_=PJ[0:R, 128:256].rearrange("p (h d) -> p h d", h=2), func=AF.Copy), reads=[pb[6]], writes=[VA[kt]])
                P.op("dve", lambda e: e.scalar_tensor_tensor(out=cnb[0:R, 0:128], in0=PJ[0:R, 256:384], scalar=rs[0:R, 2:3], in1=gckv[0:R, :], op0=ALU.mult, op1=ALU.mult),
                     reads=[pb[6], sB["rs"], cB], writes=[sB["cnb"]])
                P.op("pe", lambda e: e.transpose(out=bank_bf(5)[:, 128:128 + R], in_=cnb[0:R, 0:128], identity=identb[0:R, 0:R]), reads=[sB["cnb"], c2], writes=[pb[5]])
                evac(ckvT[:, t * 128:t * 128 + R], bank_bf(5)[:, 128:128 + R], [pb[5]], [sB["ckvT"]])
                P.op("dve", lambda e: e.tensor_copy(out=nrm[0:R, 128:160], in_=PJ[0:R, 384:416]), reads=[pb[6]], writes=[sB["nrm"]])
                if rope_t0 is None:
                    P.op("dve", lambda e: e.tensor_copy(out=rotb[0:R, 128:160], in_=nrm[0:R, 128:160]), reads=[sB["nrm"]], writes=[sB["rotb"]])
                else:
                    tab = ropeB[0:R, rope_t0 + t, :]
                    rope4(nrm[0:R, 128:144], nrm[0:R, 144:160], tab[:, 0:16], tab[:, 16:32], rotb[0:R, 128:144], rotb[0:R, 144:160],
                          lambda i: rt[i][0:R, 0:16], [sB["nrm"]])
                P.op("pe", lambda e: e.transpose(out=bank_bf(5)[0:32, 256:256 + R], in_=rotb[0:R, 128:160], identity=identb[0:R, 0:R]), reads=[sB["rotb"], c2], writes=[pb[5]])
                evac(kpeT[0:32, t * 128:t * 128 + R], bank_bf(5)[0:32, 256:256 + R], [pb[5]], [sB["kpeT"]])
            for h in range(8):
                bk = 4 + (h % 2)
                P.op("pe", lambda e, h=h, bk=bk: e.matmul(bank(bk)[0:96, 0:T], lhsT=wukp[:, h, :], rhs=ckvT[:, 0:T], start=True, stop=False), reads=[cvA, sB["ckvT"]], writes=[pb[bk]])
                P.op("pe", lambda e, h=h, bk=bk: e.matmul(bank(bk)[0:96, 0:T], lhsT=sel[:, :], rhs=kpeT[:, 0:T], start=False, stop=True), reads=[c2, sB["kpeT"]], writes=[pb[bk]])
                evac(KT_B[0:96, h, kcol0:kcol0 + T], bank(bk)[0:96, 0:T], [pb[bk]], KB[kbase:kbase + nt])
            for t in range(nt):
                kt = kbase + t
                P.op("pe", lambda e, t=t: e.matmul(PJ[0:R, 0:512], lhsT=ckvT[:, t * 128:t * 128 + R], rhs=wuv[:].rearrange("p h d -> p (h d)"), start=True, stop=True), reads=[sB["ckvT"], cvA], writes=[pb[6]])
                evac(V_B[0:R, kt, :, 0:64], PJ[0:R, 0:512].rearrange("p (h d) -> p h d", h=8), [pb[6]], [VB[kt]])

        def attention_head(QT_ap, KT_fn, V_fn, Kbufs, Vbufs, dk, scale, hb, ocol, qbufs, prev_tail):
            otb = 4 + hb
            NKC = NKT + 1
            LOOK = 2

            def st_mm(kc):
                bk = kc % 4
                P.op("pe", lambda e: e.matmul(bank(bk)[:, :], lhsT=KT_fn(kc), rhs=QT_ap, start=True, stop=True), reads=[Kbufs[kc]] + qbufs, writes=[pb[bk]])

            def exp_pv(kc):
                bk = kc % 4
                pi = kc % NPT
                P.op("act", lambda e: e.activation(out=PT[pi][:, :], in_=bank(bk)[:, :], func=AF.Exp, scale=scale), reads=[pb[bk]], writes=[PTB[pi]])
                P.op("pe", lambda e: e.matmul(bank(otb)[0:65, :], lhsT=V_fn(kc), rhs=PT[pi][:, :], start=(kc == 0), stop=(kc == NKC - 1)), reads=[Vbufs[kc], PTB[pi]], writes=[pb[otb]])

            for kc in range(min(LOOK, NKC)):
                st_mm(kc)
            for kc in range(NKC):
                if kc + LOOK < NKC:
                    st_mm(kc + LOOK)
                exp_pv(kc)
                if kc == 2 and prev_tail is not None:
                    prev_tail()

            def tail():
                P.op("dve", lambda e: e.tensor_copy(out=OTs[hb][0:65, :], in_=bank(otb)[0:65, :]), reads=[pb[otb]], writes=[OTsB[hb]])
                for t in range(4):
                    P.op("pe", lambda e, t=t: e.transpose(out=bank(6)[:, t * 128:t * 128 + 65], in_=OTs[hb][0:65, t * 128:(t + 1) * 128], identity=identf[0:65, 0:65]), reads=[OTsB[hb], cB], writes=[pb[6]])
                po = bank(6).rearrange("p (t c) -> p t c", t=4)
                P.op("dve", lambda e: e.reciprocal(out=rcp[:, 0:4], in_=po[:, :, 64]), reads=[pb[6]], writes=[sB["rcp"]])
                for t in range(4):
                    P.op("dve", lambda e, t=t: e.tensor_scalar(out=o_tm[:, t, ocol:ocol + 64], in0=po[:, t, 0:64], scalar1=rcp[:, t:t + 1], scalar2=None, op0=ALU.mult),
                         reads=[pb[6], sB["rcp"]], writes=GB[4 * t:4 * t + 4])
            return tail

        wuq = sb("wuq", [128, 2, 768], BF16)
        P.dma("pool", lambda e: e.dma_start(out=wuq[:], in_=w_uq_d.rearrange("(kc p) n -> p kc n", p=128)), "cvB", writes=[cvB], final=True)

        def passB(s, c):
            q0 = c * 512
            for t in range(4):
                P.dma("sp", lambda e, t=t: e.dma_start(out=hc[:, t, :], in_=h1_s[q0 + t * 128:q0 + (t + 1) * 128, :]), f"ldhc{t}", reads=[h1sB[c * 4 + t]], writes=[hcB[t]])
            transposes_to_aT(4, 128)
            for t in range(4):
                P.op("act", lambda e, t=t: e.activation(out=hc[:, t, :], in_=hc[:, t, :], func=AF.Copy, scale=ALPHA), reads=[hcB[t]], writes=[hcB[t]])
            if stop < 3.1:
                return
            for tp in range(2):
                for kc in range(8):
                    slot, sbuf_ = S2.take()
                    for j in range(2):
                        t = tp * 2 + j
                        P.op("pe", lambda e, slot=slot, t=t, j=j, kc=kc: e.matmul(PS[j][:, 0:512], lhsT=aT[:, kc, t * 128:(t + 1) * 128], rhs=slot[:, 0:512], start=(kc == 0), stop=(kc == 7)),
                             reads=[sbuf_, aTB[t]], writes=[pb[2 * j]])
                        P.op("pe", lambda e, slot=slot, t=t, j=j, kc=kc: e.matmul(PS[j][:, 512:768], lhsT=aT[:, kc, t * 128:(t + 1) * 128], rhs=slot[:, 512:768], start=(kc == 0), stop=(kc == 7)),
                             reads=[sbuf_, aTB[t]], writes=[pb[2 * j + 1]])
                if stop < 3.11:
                    continue
                for j in range(2):
                    t = tp * 2 + j
                    pq = PS[j]
                    pbs = [pb[2 * j], pb[2 * j + 1]]
                    P.op("act", lambda e, pq=pq: e.activation(out=sq[:, 0:512], in_=pq[:, 0:512], func=AF.Square), reads=[pbs[0]], writes=[sB["sq"]])
                    P.op("act", lambda e, pq=pq: e.activation(out=sq[:, 512:768], in_=pq[:, 512:768], func=AF.Square, scale=0.5, accum_out=ss[:, 8:9]), reads=[pbs[1]], writes=[sB["sq2"], sB["ss2"]])
                    P.op("dve", lambda e: e.tensor_reduce(out=ss[:, 0:8], in_=sq[:, 0:512].rearrange("p (h d) -> p h d", d=64), axis=AX.X, op=ALU.add), reads=[sB["sq"]], writes=[sB["ss"]])
                    if stop < 3.12:
                        continue
                    rms_rstd(128, 9, 1.0 / 64)
                    P.op("dve", lambda e, pq=pq: e.tensor_tensor(out=nrm[:, 0:512].rearrange("p (h d) -> p h d", h=8), in0=pq[:, 0:512].rearrange("p (h d) -> p h d", h=8),
                                                                 in1=rs[:, 0:8].unsqueeze(2).broadcast_to([128, 8, 64]), op=ALU.mult), reads=[pbs[0], sB["rs"]], writes=[sB["nrm"]])
                    P.op("dve", lambda e: e.tensor_tensor(out=nrm[:, 0:512].rearrange("p (h d) -> p h d", h=8), in0=nrm[:, 0:512].rearrange("p (h d) -> p h d", h=8),
                                                          in1=gq[:, :].unsqueeze(1).broadcast_to([128, 8, 64]), op=ALU.mult), reads=[sB["nrm"], cB], writes=[sB["nrm"]])
                    if stop < 3.13:
                        continue
                    src4 = nrm[:, 0:512].rearrange("p (j q d) -> p j q d", j=2, q=4)
                    dst4 = rotb[:, 0:512].rearrange("p (q j d) -> p j q d", q=4, j=2)
                    tab = ropeA[:, c * 4 + t, :]
                    cs = tab[:, 0:32].unsqueeze(1).unsqueeze(1).broadcast_to([128, 2, 4, 32])
                    sn = tab[:, 32:64].unsqueeze(1).unsqueeze(1).broadcast_to([128, 2, 4, 32])
                    rope4(src4[:, :, :, 0:32], src4[:, :, :, 32:64], cs, sn, dst4[:, :, :, 0:32], dst4[:, :, :, 32:64],
                          lambda i: rt[i][:, 0:256].rearrange("p (j q d) -> p j q d", j=2, q=4), [sB["nrm"]])
                    if stop < 3.14:
                        continue
                    for p4 in range(4):
                        P.op("pe", lambda e, p4=p4: e.transpose(out=bank_bf(5)[:, p4 * 128:(p4 + 1) * 128], in_=rotb[:, p4 * 128:(p4 + 1) * 128], identity=identb[:, :]), reads=[sB["rotb"], c2], writes=[pb[5]])
                    evac(QT_A[0:64, 0:4, t * 128:(t + 1) * 128], bank_bf(5)[0:64, 0:512].rearrange("p (q n) -> p q n", q=4), [pb[5]], [qaB])
                    evac(QT_A[64:128, 4:8, t * 128:(t + 1) * 128], bank_bf(5)[64:128, 0:512].rearrange("p (q n) -> p q n", q=4), [pb[5]], [qaB])
                    if stop < 3.15:
                        continue
                    P.op("dve", lambda e, pq=pq: e.scalar_tensor_tensor(out=cnb[:, 0:256], in0=pq[:, 512:768], scalar=rs[:, 8:9], in1=gcq[:, :], op0=ALU.mult, op1=ALU.mult),
                         reads=[pbs[1], sB["rs"], cB], writes=[sB["cnb"]])
                    if stop < 3.16:
                        continue
                    for k2 in range(2):
                        P.op("pe", lambda e, k2=k2: e.transpose(out=bank_bf(5)[:, 512 + k2 * 128:512 + (k2 + 1) * 128], in_=cnb[:, k2 * 128:(k2 + 1) * 128], identity=identb[:, :]), reads=[sB["cnb"], c2], writes=[pb[5]])
                    if stop < 3.17:
                        continue
                    if stop == 3.19:
                        P.op("dve", lambda e, t=t: e.tensor_copy(out=ckvT[:, t * 128:(t + 1) * 128], in_=cnb[:, 0:128]), reads=[sB["cnb"]], writes=[sB["ckvT"]])
                        continue
                    if stop == 3.18:
                        P.op("dve", lambda e, t=t: e.tensor_copy(out=cqT[:, 0, t * 128:(t + 1) * 128], in_=cnb[:, 0:128]), reads=[sB["cnb"]], writes=[sB["cqT"]])
                        continue
                    for k2 in range(2):
                        P.op("dve", lambda e, t=t, k2=k2: e.tensor_copy(out=cqT[:, k2, t * 128:(t + 1) * 128], in_=bank_bf(5)[:, 512 + k2 * 128:512 + (k2 + 1) * 128]), reads=[pb[5]], writes=[sB["cqT"]])
            if stop < 3.2:
                return
            for t in range(4):
                for (lo, hi, bk) in ((0, 512, 6), (512, 768, 7)):
                    for k2 in range(2):
                        P.op("pe", lambda e, t=t, lo=lo, hi=hi, k2=k2: e.matmul(PJ[:, lo:hi], lhsT=cqT[:, k2, t * 128:(t + 1) * 128], rhs=wuq[:, k2, lo:hi], start=(k2 == 0), stop=(k2 == 1)),
                             reads=[sB["cqT"], cvB], writes=[pb[bk]])
                if stop < 3.21:
                    continue
                P.op("act", lambda e: e.activation(out=sq[:, 0:768], in_=PJ[:, 0:768], func=AF.Copy), reads=[pb[6], pb[7]], writes=[sB["sq"], sB["sq2"]])
                qb3 = sq[:, 0:768].rearrange("p (h d) -> p h d", h=8)
                rb3 = rotb[:, 0:768].rearrange("p (h d) -> p h d", h=8)
                P.op("dve", lambda e, qb3=qb3, rb3=rb3: e.tensor_copy(out=rb3[:, :, 0:64], in_=qb3[:, :, 0:64]), reads=[sB["sq"], sB["sq2"]], writes=[sB["rotb"]])
                if stop < 3.22:
                    continue
                tab = ropeB[:, c * 4 + t, :]
                cs = tab[:, 0:16].unsqueeze(1).broadcast_to([128, 8, 16])
                sn = tab[:, 16:32].unsqueeze(1).broadcast_to([128, 8, 16])
                rope4(qb3[:, :, 64:80], qb3[:, :, 80:96], cs, sn, rb3[:, :, 64:80], rb3[:, :, 80:96],
                      lambda i: rt[i][:, 0:128].rearrange("p (h d) -> p h d", h=8), [sB["sq"], sB["sq2"]], pool_ok=False)
                if stop < 3.23:
                    continue
                for h in range(8):
                    P.op("pe", lambda e, h=h: e.transpose(out=bank_bf(5)[0:96, h * 128:(h + 1) * 128], in_=rotb[:, h * 96:(h + 1) * 96], identity=identb[:, :]), reads=[sB["rotb"], c2], writes=[pb[5]])
                if stop < 3.24:
                    continue
                evac(QT_B[0:96, :, t * 128:(t + 1) * 128], bank_bf(5)[0:96, :].rearrange("p (h n) -> p h n", h=8), [pb[5]], [qbB])
            if stop < 3.3:
                return
            hbc = [0]
            tail = None
            for h in range(8):
                tail = attention_head(QT_A[:, h, :],
                                      lambda kc: KT_A[:, kc * 128:(kc + 1) * 128],
                                      lambda kc, g=h // 4: V_A[:, kc, g, :], KA, VA, 64, SC_A, hbc[0], h * 64, [qaB], tail)
                hbc[0] ^= 1
            for h in range(8):
                tail = attention_head(QT_B[0:96, h, :],
                                      lambda kc, h=h: KT_B[0:96, h, kc * 128:(kc + 1) * 128],
                                      lambda kc, h=h: V_B[:, kc, h, :], KB, VB, 96, SC_B, hbc[0], 512 + h * 64, [qbB], tail)
                hbc[0] ^= 1
            tail()
            if stop < 3.4:
                return
            for t in range(4):
                gbt = GB[4 * t:4 * t + 4]
                P.op("act", lambda e, t=t: e.activation(out=sq[:, 0:512], in_=o_tm[:, t, 0:512], func=AF.Square, accum_out=ss[:, 2 * t:2 * t + 1]), reads=gbt, writes=[sB["sq"], sB["ss"]])
                P.op("act", lambda e, t=t: e.activation(out=nrm[:, 0:512], in_=o_tm[:, t, 512:1024], func=AF.Square, accum_out=ss[:, 2 * t + 1:2 * t + 2]), reads=gbt, writes=[sB["nrm"], sB["ss"]])
            rms_rstd(128, 8, 1.0 / 512)
            for t in range(4):
                gbt = GB[4 * t:4 * t + 4]
                for g in range(2):
                    P.op("dve", lambda e, t=t, g=g: e.scalar_tensor_tensor(out=on_tm[:, t, g * 512:(g + 1) * 512], in0=o_tm[:, t, g * 512:(g + 1) * 512], scalar=rs[:, 2 * t + g:2 * t + g + 1], in1=gout[:, g * 512:(g + 1) * 512], op0=ALU.mult, op1=ALU.mult),
                         reads=gbt + [sB["rs"], cB], writes=GB[16 + 2 * t:18 + 2 * t])
                for kc in range(8):
                    P.op("pe", lambda e, t=t, kc=kc: e.transpose(out=bank_bf(5 + (t % 2))[:, kc * 128:(kc + 1) * 128], in_=on_tm[:, t, kc * 128:(kc + 1) * 128], identity=identb[:, :]), reads=GB[16 + 2 * t:18 + 2 * t] + [c2], writes=[pb[5 + (t % 2)]])
                evac(aT[:, :, t * 128:(t + 1) * 128], bank_bf(5 + (t % 2))[:, :].rearrange("p (k n) -> p k n", k=8), [pb[5 + (t % 2)]], [aTB[t]])
            if stop < 3.5:
                return
            for tp in range(2):
                for kc in range(8):
                    slot, sbuf_ = S2.take()
                    for j in range(2):
                        t = tp * 2 + j
                        for half in range(2):
                            P.op("pe", lambda e, slot=slot, t=t, j=j, kc=kc, half=half: e.matmul(PS[j][:, half * 512:(half + 1) * 512], lhsT=aT[:, kc, t * 128:(t + 1) * 128], rhs=slot[:, half * 512:(half + 1) * 512], start=(kc == 0), stop=(kc == 7)),
                                 reads=[sbuf_, aTB[t]], writes=[pb[2 * j + half]])
                for j in range(2):
                    t = tp * 2 + j
                    for half in range(2):
                        P.op("dve", lambda e, t=t, j=j, half=half: e.tensor_tensor(out=hc[:, t, half * 512:(half + 1) * 512], in0=PS[j][:, half * 512:(half + 1) * 512], in1=hc[:, t, half * 512:(half + 1) * 512], op=ALU.add),
                             reads=[pb[2 * j + half], hcB[t]], writes=[hcB[t]])
            if stop < 3.6:
                return
            ensure_ln(2, 0)
            ln_pipeline(0, 128, 4, post=lambda t: transposes_to_aT(4, 128, tiles=[t]))
            ensure_ln(3, 1)
            dfr = ffn(2, 512, 128, 4, ln_fused=True)

            def postB(t):
                P.dma("sp", lambda e: e.dma_start(out=out_d[s, q0 + t * 128:q0 + (t + 1) * 128, :], in_=hc[:, t, :]), f"sthc{t}", reads=[hcB[t]])
            ln_pipeline(1, 128, 4, pre=dfr, post=postB, have_h0=True)

        pass

        def rope4(x1, x2, cs, sn, d1, d2, tvf, rds, pool_ok=False):
            tv = [tvf(i) for i in range(4)]
            e2 = "pool" if pool_ok else "dve"
            P.op("dve", lambda e: e.tensor_tensor(out=tv[0], in0=x1, in1=cs, op=ALU.mult), reads=rds + [cB], writes=[sB["rt0"]])
            P.op(e2, lambda e: e.tensor_tensor(out=tv[1], in0=x2, in1=sn, op=ALU.mult), reads=rds + [cB], writes=[sB["rt1"]])
            P.op("dve", lambda e: e.tensor_tensor(out=tv[2], in0=x1, in1=sn, op=ALU.mult), reads=rds + [cB], writes=[sB["rt2"]])
            P.op(e2, lambda e: e.tensor_tensor(out=tv[3], in0=x2, in1=cs, op=ALU.mult), reads=rds + [cB], writes=[sB["rt3"]])
            P.op("dve", lambda e: e.tensor_tensor(out=d1, in0=tv[0], in1=tv[1], op=ALU.subtract), reads=[sB["rt0"], sB["rt1"]], writes=[sB["rotb"]])
            P.op("dve", lambda e: e.tensor_tensor(out=d2, in0=tv[2], in1=tv[3], op=ALU.add), reads=[sB["rt2"], sB["rt3"]], writes=[sB["rotb"]])

        if stop >= 1:
            passA(lambda t: meta_d, NMETA, NMETA, 1, NKT, None, False)
        for s in range(nseq):
            for c in range(nch):
                if stop >= 2:
                    passA(lambda t, s=s, c=c: x_d[s, c * 512 + t * 128:c * 512 + (t + 1) * 128, :], 512, 128, 4, c * 4, c * 4, True)
            for c in range(nch):
                if stop >= 3:
                    passB(s, c)
        if stop >= 99:
            assert S13.taken == len(S13.items) and S2.taken == len(S2.items), (S13.taken, len(S13.items), S2.taken, len(S2.items))
        P.emit()
    return nc


_CACHE = {}


def _rope_tables(SEQ=SEQ):
    def tab(rot_dim):
        axis_dim = rot_dim // 2
        inv = (10000.0 ** (-np.arange(0, axis_dim, 2, dtype=np.float32) / np.float32(axis_dim))).astype(np.float32)
        rows = np.repeat(np.arange(SEQ // 64, dtype=np.float32), 64)
        cols = np.tile(np.arange(64, dtype=np.float32), SEQ // 64)
        ang = np.concatenate([rows[:, None] * inv[None, :], cols[:, None] * inv[None, :]], axis=-1).astype(np.float32)
        return np.concatenate([np.cos(ang), np.sin(ang)], axis=-1).astype(np.float32)
    return tab(64), tab(32)


def kernel(**inputs):
    n = 8
    if "nc" not in _CACHE:
        _CACHE["nc"] = build_program()
    nc = _CACHE["nc"]
    x = np.ascontiguousarray(inputs["x"], dtype=np.float32)
    ropeA, ropeB = _rope_tables()
    shared = {
        "meta": np.ascontiguousarray(inputs["meta_tokens"], dtype=np.float32),
        "w_in": np.ascontiguousarray(inputs["w_in"][0]),
        "w_uq": np.ascontiguousarray(inputs["w_uq"][0]),
        "w_ukv": np.ascontiguousarray(inputs["w_ukv"][0]),
        "w_out": np.ascontiguousarray(inputs["w_out"][0]),
        "ropeA": ropeA, "ropeB": ropeB,
        "ident": np.eye(128, dtype=np.float32),
    }
    for f in (1, 2):
        for k in ("w1", "w3", "w2"):
            shared[f"f{f}{k}"] = np.ascontiguousarray(inputs[f"ffn{f}_{k}"][0])
    for i in (1, 2, 3):
        shared[f"ln{i}_g"] = np.ascontiguousarray(inputs[f"ln{i}_g"]).reshape(1, D)
        shared[f"ln{i}_b"] = np.ascontiguousarray(inputs[f"ln{i}_b"]).reshape(1, D)
    for k in ("q_norm_a", "k_norm_a", "cq_norm", "ckv_norm", "out_norm_a", "out_norm_b"):
        shared[k] = np.ascontiguousarray(inputs[k]).reshape(1, -1)
    in_maps = []
    for i in range(n):
        m = dict(shared)
        m["x"] = x[i * NSEQ:(i + 1) * NSEQ]
        in_maps.append(m)
    res = run_bass_kernel_spmd(nc, in_maps, core_ids=list(range(n)))
    return np.concatenate([np.asarray(r["out"]) for r in res.results], axis=0).astype(np.float32)
```

```python
import numpy as np
from contextlib import ExitStack
import concourse.bass as bass
import concourse.mybir as mybir
from concourse.bass_utils import run_bass_kernel_spmd

F32 = mybir.dt.float32
BF16 = mybir.dt.bfloat16
AF = mybir.ActivationFunctionType
ALU = mybir.AluOpType
AX = mybir.AxisListType

D = 1024
FF = 2816
NFC = 22
SEQ = 2048
NMETA = 16
LK = SEQ + NMETA
NSEQ = 4
ALPHA = 2.0 ** 0.25
LN_EPS = 1e-5
RMS_EPS = 1e-6
SC_A = 64 ** -0.5
SC_B = 96 ** -0.5


class Buf:
    __slots__ = ("name", "w", "r", "psum")

    def __init__(self, name, psum=False):
        self.name = name
        self.w = None
        self.r = []
        self.psum = psum


class Op:
    __slots__ = ("eng", "fn", "waits", "needs_inc", "seq", "dma_sem")

    def __init__(self, eng, fn):
        self.eng = eng
        self.fn = fn
        self.waits = []
        self.needs_inc = False
        self.seq = None
        self.dma_sem = None


class Prog:
    def __init__(self, nc):
        self.nc = nc
        self.ops = {e: [] for e in ("pe", "act", "dve", "pool", "sp")}
        self.dma_sems = {}

    def _deps(self, eng, reads, writes, is_dma=False):
        deps = []
        for b in reads:
            if b.w is not None:
                deps.append((b.w, True))
            if b.psum:
                for t in b.r:
                    if t[0] == "c" and t[1].eng != eng:
                        deps.append((t, True))
        for b in writes:
            if b.w is not None:
                deps.append((b.w, False))
            for t in b.r:
                deps.append((t, False))
        out = []
        for t, raw in deps:
            if t[0] == "c":
                o = t[1]
                if o.eng == eng and not is_dma:
                    if eng == "pe":
                        continue
                o.needs_inc = True
            out.append(t)
        return out

    def _commit(self, tok, reads, writes):
        for b in reads:
            b.r.append(tok)
        for b in writes:
            b.w = tok
            b.r = []

    def op(self, eng, fn, reads=(), writes=()):
        o = Op(eng, fn)
        o.waits = self._deps(eng, reads, writes)
        self.ops[eng].append(o)
        self._commit(("c", o), reads, writes)
        return o

    def dma(self, q, fn, sem, reads=(), writes=(), final=False):
        o = Op(q, fn)
        o.waits = [t for t in self._deps(q, reads, writes, True) if not (t[0] == "d" and t[1] == sem and t[2] is None)]
        ent = self.dma_sems.setdefault(sem, [None, 0])
        ent[1] += 16
        o.dma_sem = sem
        self.ops[q].append(o)
        self._commit(("d", sem, None if final else ent[1]), reads, writes)
        return o

    def emit(self):
        nc = self.nc
        with ExitStack() as st:
            esem = {e: st.enter_context(nc.semaphore("s_" + e)) for e in self.ops}
            for name, ent in self.dma_sems.items():
                ent[0] = st.enter_context(nc.semaphore("d_" + name))
            for e, lst in self.ops.items():
                c = 0
                for o in lst:
                    if o.dma_sem is None and o.needs_inc:
                        c += 1
                        o.seq = c
            block = st.enter_context(nc.Block())
            starters = {"pe": block.tensor, "act": block.scalar, "dve": block.vector,
                        "pool": block.gpsimd, "sp": block.sync}

            def run_engine(e):
                lst = self.ops[e]

                def body(eng):
                    seen = {}
                    for o in lst:
                        for t in o.waits:
                            if t[0] == "c":
                                key, val, sem = t[1].eng, t[1].seq, esem[t[1].eng]
                            else:
                                ent = self.dma_sems[t[1]]
                                key, sem = "d:" + t[1], ent[0]
                                val = ent[1] if t[2] is None else t[2]
                            if seen.get(key, 0) >= val:
                                continue
                            seen[key] = val
                            eng.wait_ge(sem, val)
                        ins = o.fn(eng)
                        if o.dma_sem is not None:
                            ins.then_inc(self.dma_sems[o.dma_sem][0], 16)
                        elif o.needs_inc:
                            ins.then_inc(esem[e], 1)
                    if e == "sp":
                        for ent in self.dma_sems.values():
                            eng.wait_ge(ent[0], ent[1])
                starters[e](body)

            for e in ("sp", "pool", "act", "dve", "pe"):
                run_engine(e)


def build_program(nseq=NSEQ, nch=4, nfc=22, stop=99):
    SEQ_ = nch * 512
    LK_ = SEQ_ + 128
    NKT = nch * 4
    FF_ = nfc * 128
    nc = bass.Bass("TRN2", target_bir_lowering=False, dynamic_dma_scratch_size=8192)
    P = Prog(nc)

    def din(name, shape):
        return nc.dram_tensor(name, list(shape), F32, kind="ExternalInput").ap()

    x_d = din("x", [nseq, SEQ_, D])
    meta_d = din("meta", [NMETA, D])
    fw = {}
    for f in (1, 2):
        fw[f] = (din(f"f{f}w1", [D, FF_]), din(f"f{f}w3", [D, FF_]), din(f"f{f}w2", [FF_, D]))
    w_in_d = din("w_in", [D, 1184])
    w_uq_d = din("w_uq", [256, 768])
    w_ukv_d = din("w_ukv", [128, 1024])
    w_out_d = din("w_out", [D, D])
    ln_d = {i: (din(f"ln{i}_g", [1, D]), din(f"ln{i}_b", [1, D])) for i in (1, 2, 3)}
    qn_d = din("q_norm_a", [1, 64])
    kn_d = din("k_norm_a", [1, 64])
    cqn_d = din("cq_norm", [1, 256])
    ckvn_d = din("ckv_norm", [1, 128])
    ona_d = din("out_norm_a", [1, 512])
    onb_d = din("out_norm_b", [1, 512])
    ropeA_d = din("ropeA", [SEQ_, 64])
    ropeB_d = din("ropeB", [SEQ_, 32])
    ident_d = din("ident", [128, 128])
    out_d = nc.dram_tensor("out", [nseq, SEQ_, D], F32, kind="ExternalOutput").ap()

    def dscr(name, shape, dt=BF16):
        return nc.dram_tensor(name, list(shape), dt, kind="Internal").ap()

    w13s = {f: dscr(f"w13s{f}", [nfc, 128, 2, 8, 128]) for f in (1, 2)}
    w2s = {f: dscr(f"w2s{f}", [FF_, D]) for f in (1, 2)}
    winq_s = dscr("winq_s", [D, 768])
    wuq_s = dscr("wuq_s", [256, 768])
    wout_s = dscr("wout_s", [D, D])
    h1_s = dscr("h1_s", [SEQ_, D], F32)

    with ExitStack() as st:
        def sb(name, shape, dt):
            return st.enter_context(nc.sbuf_tensor(name, list(shape), dt))

        def ps(name, shape, dt):
            return st.enter_context(nc.psum_tensor(name, list(shape), dt))

        KT_A = sb("KT_A", [128, LK_], BF16)
        KT_B = sb("KT_B", [128, 8, LK_], BF16)
        V_A = sb("V_A", [128, NKT + 1, 2, 65], BF16)
        V_B = sb("V_B", [128, NKT + 1, 8, 65], BF16)
        hc = sb("hc", [128, 4, D], F32)
        aT = sb("aT", [128, 8, 512], BF16)
        G = sb("G", [128, 24, 512], BF16)
        o_tm = G[:].rearrange("p a b -> p (a b)")[:, 0:8192].bitcast(F32).rearrange("p (t c) -> p t c", t=4)
        on_tm = G[:].rearrange("p a b -> p (a b)")[:, 8192:12288].rearrange("p (t c) -> p t c", t=4)
        NR13, NR2 = 3, 4
        r13 = [sb(f"r13_{i}", [128, 2, 8, 128], BF16) for i in range(NR13)]
        r2 = [sb(f"r2_{i}", [128, 1024], BF16) for i in range(NR2)]
        QT_A = sb("QT_A", [128, 8, 512], BF16)
        QT_B = sb("QT_B", [128, 8, 512], BF16)
        NPT = 3
        PT = [sb(f"PT{i}", [128, 512], BF16) for i in range(NPT)]
        lnp = [sb(f"lnp{i}", [128, 2, D], F32) for i in range(2)]
        ropeA = sb("ropeA_t", [128, NKT, 64], F32)
        ropeB = sb("ropeB_t", [128, NKT, 32], F32)
        gq = sb("gq", [128, 64], F32)
        gk = sb("gk", [128, 64], F32)
        gcq = sb("gcq", [128, 256], F32)
        gckv = sb("gckv", [128, 128], F32)
        gout = sb("gout", [128, 1024], F32)
        identf = sb("identf", [128, 128], F32)
        identb = sb("identb", [128, 128], BF16)
        sel = sb("sel", [128, 96], BF16)
        epsL = sb("epsL", [128, 1], F32)
        epsR = sb("epsR", [128, 1], F32)
        winkv = sb("winkv", [128, 8, 416], BF16)
        wukp = sb("wukp", [128, 8, 96], BF16)
        wuv = sb("wuv", [128, 8, 64], BF16)
        sa = [sb(f"sa{i}", [128, 512], F32) for i in range(2)]
        OTs = [sb(f"OTs{i}", [128, 512], F32) for i in range(2)]
        sq = sb("sq", [128, 768], F32)
        nrm = sb("nrm", [128, 512], F32)
        rt = [sb(f"rt{i}", [128, 256], F32) for i in range(4)]
        rotb = sb("rotb", [128, 768], BF16)
        cnb = sb("cnb", [128, 256], BF16)
        ckvT = sb("ckvT", [128, 512], BF16)
        kpeT = sb("kpeT", [128, 512], BF16)
        cqT = sb("cqT", [128, 2, 512], BF16)
        stats = sb("stats", [128, 4, 2, 6], F32)
        mv = sb("mv", [128, 4, 2], F32)
        rstd = sb("rstd", [128, 4], F32)
        nmr = sb("nmr", [128, 4], F32)
        ss = sb("ss", [128, 16], F32)
        rs = sb("rs", [128, 16], F32)
        rcp = sb("rcp", [128, 4], F32)

        PS = [ps(f"ps{i}", [128, 1024], F32) for i in range(4)]

        def bank(i):
            return PS[i // 2][:, (i % 2) * 512:(i % 2) * 512 + 512]

        def bank_bf(i):
            return bank(i).bitcast(BF16)

        PJ = PS[3]
        pb = [Buf(f"bank{i}", psum=True) for i in range(8)]

        B = {}

        def bf(name):
            if name not in B:
                B[name] = Buf(name)
            return B[name]

        KA = [bf(f"KA{i}") for i in range(NKT + 1)]
        KB = [bf(f"KB{i}") for i in range(NKT + 1)]
        VA = [bf(f"VA{i}") for i in range(NKT + 1)]
        VB = [bf(f"VB{i}") for i in range(NKT + 1)]
        hcB = [bf(f"hc{i}") for i in range(4)]
        aTB = [bf(f"aT{i}") for i in range(4)]
        GB = [bf(f"G{i}") for i in range(24)]
        r13B = [bf(f"r13_{i}") for i in range(NR13)]
        r2B = [bf(f"r2_{i}") for i in range(NR2)]
        PTB = [bf(f"PT{i}") for i in range(NPT)]
        lnpB = [bf(f"lnp{i}") for i in range(2)]
        saB = [bf(f"sa{i}") for i in range(2)]
        OTsB = [bf(f"OTs{i}") for i in range(2)]
        h1sB = [bf(f"h1s{i}") for i in range(NKT)]
        cB = bf("consts")
        qaB, qbB = bf("QT_A"), bf("QT_B")
        cvA, cvB, cvC = bf("cvA"), bf("cvB"), bf("cvC")

        P.dma("sp", lambda e: e.dma_start(out=identf[:], in_=ident_d), "c0", writes=[cB], final=True)
        P.dma("sp", lambda e: e.dma_start(out=ropeA[:], in_=ropeA_d.rearrange("(t p) c -> p t c", p=128)), "c0", writes=[cB], final=True)
        P.dma("sp", lambda e: e.dma_start(out=ropeB[:], in_=ropeB_d.rearrange("(t p) c -> p t c", p=128)), "c0", writes=[cB], final=True)
        for tile_, src in ((gq, qn_d), (gk, kn_d), (gcq, cqn_d), (gckv, ckvn_d)):
            P.dma("sp", lambda e, tile_=tile_, src=src: e.dma_start(out=tile_[:], in_=src.partition_broadcast(128)), "c0", writes=[cB], final=True)
        P.dma("sp", lambda e: e.dma_start(out=gout[:, 0:512], in_=ona_d.partition_broadcast(128)), "c0", writes=[cB], final=True)
        P.dma("sp", lambda e: e.dma_start(out=gout[:, 512:1024], in_=onb_d.partition_broadcast(128)), "c0", writes=[cB], final=True)
        c2 = bf("consts2")
        P.op("dve", lambda e: e.tensor_copy(out=identb[:], in_=identf[:]), reads=[cB], writes=[c2])
        P.op("pool", lambda e: e.memset(sel[:], 0.0), writes=[c2])
        P.op("dve", lambda e: e.tensor_copy(out=sel[0:32, 64:96], in_=identf[0:32, 0:32]), reads=[cB, c2], writes=[c2])
        P.op("pool", lambda e: e.memset(kpeT[:], 0.0), writes=[bf("kpeT")])
        P.op("pool", lambda e: e.memset(rotb[:], 0.0), writes=[bf("rotb")])
        P.op("pool", lambda e: e.memset(epsL[:], LN_EPS), writes=[c2])
        P.op("pool", lambda e: e.memset(epsR[:], RMS_EPS), writes=[c2])
        P.op("pool", lambda e: e.memset(V_A[:], 1.0), writes=VA)
        P.op("pool", lambda e: e.memset(V_B[:], 1.0), writes=VB)
        P.op("pool", lambda e: e.memset(V_A[:, NKT], 0.0), writes=[VA[NKT]])
        P.op("pool", lambda e: e.memset(V_B[:, NKT], 0.0), writes=[VB[NKT]])
        P.op("pool", lambda e: e.memset(V_A[0:NMETA, NKT, :, 64:65], 1.0), writes=[VA[NKT]])
        P.op("pool", lambda e: e.memset(V_B[0:NMETA, NKT, :, 64:65], 1.0), writes=[VB[NKT]])
        P.op("pool", lambda e: e.memset(KT_A[:, SEQ_:SEQ_ + 128], 0.0), writes=[KA[NKT]])
        P.op("pool", lambda e: e.memset(KT_B[:, :, SEQ_:SEQ_ + 128], 0.0), writes=[KB[NKT]])
        P.op("pool", lambda e: e.memset(QT_A[:], 0.0), writes=[bf("QT_A")])
        P.op("pool", lambda e: e.memset(wukp[:], 0.0), writes=[cvA])
        P.dma("pool", lambda e: e.dma_start(out=winkv[:, :, 0:256], in_=w_in_d[:, 512:768].rearrange("(kc p) n -> p kc n", p=128)), "cvA", writes=[cvA], final=True)
        P.dma("pool", lambda e: e.dma_start(out=winkv[:, :, 256:416], in_=w_in_d[:, 1024:1184].rearrange("(kc p) n -> p kc n", p=128)), "cvA", writes=[cvA], final=True)
        ukv4 = w_ukv_d.rearrange("p (h two d) -> p h two d", h=8, two=2)
        P.dma("pool", lambda e: e.dma_start(out=wukp[:, :, 0:64], in_=ukv4[:, :, 0, :]), "cvA", writes=[cvA], final=True)
        P.dma("pool", lambda e: e.dma_start(out=wuv[:], in_=ukv4[:, :, 1, :]), "cvA", writes=[cvA], final=True)

        def conv_ffn(f, semname, buf):
            w1, w3, w2 = fw[f]
            for fc in range(nfc):
                for m, w in enumerate((w1, w3)):
                    src = w.rearrange("(kc p) (fc j) -> fc p kc j", p=128, j=128)[fc]
                    P.dma("pool", lambda e, src=src, fc=fc, m=m: e.dma_start(out=w13s[f][fc, :, m, :, :], in_=src),
                          semname, writes=[buf], final=True)
            for fc in range(nfc):
                P.dma("pool", lambda e, fc=fc: e.dma_start(out=w2s[f][fc * 128:(fc + 1) * 128, :], in_=w2[fc * 128:(fc + 1) * 128, :]),
                      semname, writes=[buf], final=True)

        conv_ffn(1, "cvA", cvA)
        for kc in range(8):
            P.dma("pool", lambda e, kc=kc: e.dma_start(out=winq_s[kc * 128:(kc + 1) * 128, 0:512], in_=w_in_d[kc * 128:(kc + 1) * 128, 0:512]), "cvB", writes=[cvB], final=True)
            P.dma("pool", lambda e, kc=kc: e.dma_start(out=winq_s[kc * 128:(kc + 1) * 128, 512:768], in_=w_in_d[kc * 128:(kc + 1) * 128, 768:1024]), "cvB", writes=[cvB], final=True)
        for kc in range(8):
            P.dma("pool", lambda e, kc=kc: e.dma_start(out=wout_s[kc * 128:(kc + 1) * 128, :], in_=w_out_d[kc * 128:(kc + 1) * 128, :]), "cvB", writes=[cvB], final=True)
        conv_ffn(2, "cvC", cvC)

        class Stream:
            def __init__(self, slots, bufs, semprefix):
                self.slots, self.bufs, self.pref = slots, bufs, semprefix
                self.items = []
                self.issued = 0
                self.taken = 0

            def _issue(self):
                i = self.issued
                fn, rd = self.items[i]
                s = i % len(self.slots)
                P.dma("sp", lambda e, fn=fn, s=s: fn(e, self.slots[s]), f"{self.pref}{s}", reads=[rd], writes=[self.bufs[s]])
                self.issued += 1

            def take(self):
                i = self.taken
                while self.issued < min(len(self.items), i + len(self.slots)):
                    self._issue()
                self.taken += 1
                s = i % len(self.slots)
                return self.slots[s], self.bufs[s]

        S13 = Stream(r13, r13B, "r13_")
        S2 = Stream(r2, r2B, "r2_")
        cvbuf = {1: cvA, 2: cvC}

        def sched_ffn(f):
            for fc in range(nfc):
                S13.items.append((lambda e, slot, fc=fc, f=f: e.dma_start(out=slot[:], in_=w13s[f][fc]), cvbuf[f]))
            for half in range(2):
                for fp in range(nfc // 2):
                    src = w2s[f][fp * 256:(fp + 1) * 256, half * 512:(half + 1) * 512].rearrange("(a p) n -> p a n", p=128)
                    S2.items.append((lambda e, slot, src=src: e.dma_start(out=slot[:].rearrange("p (a n) -> p a n", a=2), in_=src), cvbuf[f]))

        def sched_mixB_q():
            for tp in range(2):
                for kc in range(8):
                    S2.items.append((lambda e, slot, kc=kc: e.dma_start(out=slot[:, 0:768], in_=winq_s[kc * 128:(kc + 1) * 128, :]), cvB))

        def sched_mixB_o():
            for tp in range(2):
                for kc in range(8):
                    S2.items.append((lambda e, slot, kc=kc: e.dma_start(out=slot[:], in_=wout_s[kc * 128:(kc + 1) * 128, :]), cvB))

        sched_ffn(1)
        for s in range(nseq):
            for c in range(nch):
                sched_ffn(1)
            for c in range(nch):
                sched_mixB_q()
                sched_mixB_o()
                sched_ffn(2)

        ln_cur = [None, None]

        def ensure_ln(i, slot):
            if ln_cur[slot] == i:
                return
            ln_cur[slot] = i
            g_d, b_d = ln_d[i]
            P.dma("sp", lambda e: e.dma_start(out=lnp[slot][:, 0, :], in_=g_d.partition_broadcast(128)), f"lnp{slot}", writes=[lnpB[slot]])
            P.dma("sp", lambda e: e.dma_start(out=lnp[slot][:, 1, :], in_=b_d.partition_broadcast(128)), f"lnp{slot}", writes=[lnpB[slot]])

        evac_rr = [0]

        def evac(out, in_, reads, writes):
            evac_rr[0] ^= 1
            if evac_rr[0]:
                P.op("dve", lambda e: e.tensor_copy(out=out, in_=in_), reads=reads, writes=writes)
            else:
                P.op("act", lambda e: e.activation(out=out, in_=in_, func=AF.Copy), reads=reads, writes=writes)

        def transposes_to_aT(nt, R, tiles=None):
            for t in (range(nt) if tiles is None else tiles):
                for kc in range(8):
                    P.op("pe", lambda e, t=t, kc=kc: e.transpose(out=PJ[:, kc * 128:kc * 128 + R], in_=hc[0:R, t, kc * 128:(kc + 1) * 128], identity=identf[0:R, 0:R]),
                         reads=[hcB[t], cB], writes=[pb[6 + kc // 4]])
                pj3 = PJ[:].rearrange("p (k c) -> p k c", k=8)
                evac(aT[:, 0:4, t * 128:t * 128 + R], pj3[:, 0:4, 0:R], [pb[6]], [aTB[t]])
                evac(aT[:, 4:8, t * 128:t * 128 + R], pj3[:, 4:8, 0:R], [pb[7]], [aTB[t]])

        def ffn(f, T, R, nt, ln_fused=False):
            for t in range(nt):
                P.op("act", lambda e, t=t: e.activation(out=hc[0:R, t, :], in_=hc[0:R, t, :], func=AF.Copy, scale=ALPHA),
                     reads=[hcB[t]], writes=[hcB[t]])
            for fc in range(nfc):
                slot, sbuf_ = S13.take()
                ua, ub = (0, 1) if fc % 2 == 0 else (2, 3)
                for m, bk in ((0, ua), (1, ub)):
                    for kc in range(8):
                        P.op("pe", lambda e, slot=slot, m=m, bk=bk, kc=kc: e.matmul(bank(bk)[:, 0:T], lhsT=slot[:, m, kc, :], rhs=aT[:, kc, 0:T], start=(kc == 0), stop=(kc == 7)),
                             reads=[sbuf_] + aTB[0:nt], writes=[pb[bk]])
                si = fc % 2
                P.op("act", lambda e, si=si, ua=ua: e.activation(out=sa[si][:, 0:T], in_=bank(ua)[:, 0:T], func=AF.Silu), reads=[pb[ua]], writes=[saB[si]])
                P.op("dve", lambda e, si=si, ub=ub, fc=fc: e.tensor_tensor(out=G[:, fc, 0:T], in0=sa[si][:, 0:T], in1=bank(ub)[:, 0:T], op=ALU.mult),
                     reads=[saB[si], pb[ub]], writes=[GB[fc]])
            for half in range(2):
                for fp in range(nfc // 2):
                    slot, sbuf_ = S2.take()
                    for a in range(2):
                        fc = fp * 2 + a
                        for t in range(nt):
                            P.op("pe", lambda e, slot=slot, a=a, fc=fc, t=t: e.matmul(bank(4 + t)[0:R, :], lhsT=G[:, fc, t * 128:t * 128 + R], rhs=slot[:, a * 512:(a + 1) * 512], start=(fc == 0), stop=(fc == nfc - 1)),
                                 reads=[sbuf_, GB[fc]], writes=[pb[4 + t]])
                deferred = []
                for t in range(nt):
                    def ev(t=t, half=half):
                        P.op("dve", lambda e: e.scalar_tensor_tensor(out=hc[0:R, t, half * 512:(half + 1) * 512], in0=bank(4 + t)[0:R, :], scalar=0.5, in1=hc[0:R, t, half * 512:(half + 1) * 512], op0=ALU.mult, op1=ALU.add),
                             reads=[pb[4 + t], hcB[t]], writes=[hcB[t]])
                    if ln_fused and half == 1:
                        deferred.append(ev)
                    else:
                        ev()
                        if ln_fused:
                            P.op("dve", lambda e, t=t: e.bn_stats(out=stats[0:R, t, 0, :], in_=hc[0:R, t, 0:512]), reads=[hcB[t]], writes=[stB[t]])
            return deferred

        sB = {n: bf(n) for n in ("stats", "mv", "rstd", "nmr", "sq", "sq2", "ss", "ss2", "rs", "nrm", "rt0", "rt1", "rt2", "rt3", "rotb", "cnb", "ckvT", "kpeT", "cqT", "rcp")}

        def layernorm(slot, R, nt):
            for t in range(nt):
                for h in range(2):
                    P.op("dve", lambda e, t=t, h=h: e.bn_stats(out=stats[0:R, t, h, :], in_=hc[0:R, t, h * 512:(h + 1) * 512]), reads=[hcB[t]], writes=[sB["stats"]])
                P.op("dve", lambda e, t=t: e.bn_aggr(out=mv[0:R, t, :], in_=stats[0:R, t].rearrange("p a b -> p (a b)")), reads=[sB["stats"]], writes=[sB["mv"]])
            P.op("act", lambda e: e.activation(out=rstd[0:R, 0:nt], in_=mv[0:R, 0:nt, 1], func=AF.Sqrt, bias=epsL[0:R, :], scale=1.0), reads=[sB["mv"], c2], writes=[sB["rstd"]])
            P.op("dve", lambda e: e.reciprocal(out=rstd[0:R, 0:nt], in_=rstd[0:R, 0:nt]), reads=[sB["rstd"]], writes=[sB["rstd"]])
            P.op("dve", lambda e: e.scalar_tensor_tensor(out=nmr[0:R, 0:nt], in0=mv[0:R, 0:nt, 0], scalar=-1.0, in1=rstd[0:R, 0:nt], op0=ALU.mult, op1=ALU.mult),
                 reads=[sB["mv"], sB["rstd"]], writes=[sB["nmr"]])
            for t in range(nt):
                P.op("act", lambda e, t=t: e.activation(out=hc[0:R, t, :], in_=hc[0:R, t, :], func=AF.Identity, scale=rstd[0:R, t:t + 1], bias=nmr[0:R, t:t + 1]),
                     reads=[hcB[t], sB["rstd"], sB["nmr"]], writes=[hcB[t]])
                P.op("dve", lambda e, t=t: e.tensor_tensor(out=hc[0:R, t, :], in0=hc[0:R, t, :], in1=lnp[slot][0:R, 0, :], op=ALU.mult), reads=[hcB[t], lnpB[slot]], writes=[hcB[t]])
                P.op("dve", lambda e, t=t: e.tensor_tensor(out=hc[0:R, t, :], in0=hc[0:R, t, :], in1=lnp[slot][0:R, 1, :], op=ALU.add), reads=[hcB[t], lnpB[slot]], writes=[hcB[t]])

        stB = [bf(f"st{i}") for i in range(4)]
        mvB = [bf(f"mv{i}") for i in range(4)]
        rsB = [bf(f"rsd{i}") for i in range(4)]

        def ln_pipeline(slot, R, nt, pre=None, post=None, have_h0=False):
            def S1(t):
                if pre:
                    pre[t]()
                for h in ((1,) if have_h0 else (0, 1)):
                    P.op("dve", lambda e, h=h: e.bn_stats(out=stats[0:R, t, h, :], in_=hc[0:R, t, h * 512:(h + 1) * 512]), reads=[hcB[t]], writes=[stB[t]])
                P.op("dve", lambda e: e.bn_aggr(out=mv[0:R, t, :], in_=stats[0:R, t].rearrange("p a b -> p (a b)")), reads=[stB[t]], writes=[mvB[t]])
                P.op("act", lambda e: e.activation(out=rstd[0:R, t:t + 1], in_=mv[0:R, t, 1:2], func=AF.Sqrt, bias=epsL[0:R, :], scale=1.0), reads=[mvB[t], c2], writes=[rsB[t]])

            def S2(t):
                P.op("dve", lambda e: e.reciprocal(out=rstd[0:R, t:t + 1], in_=rstd[0:R, t:t + 1]), reads=[rsB[t]], writes=[rsB[t]])
                P.op("dve", lambda e: e.scalar_tensor_tensor(out=nmr[0:R, t:t + 1], in0=mv[0:R, t, 0:1], scalar=-1.0, in1=rstd[0:R, t:t + 1], op0=ALU.mult, op1=ALU.mult),
                     reads=[mvB[t], rsB[t]], writes=[rsB[t]])
                P.op("act", lambda e: e.activation(out=hc[0:R, t, :], in_=hc[0:R, t, :], func=AF.Identity, scale=rstd[0:R, t:t + 1], bias=nmr[0:R, t:t + 1]),
                     reads=[hcB[t], rsB[t]], writes=[hcB[t]])

            def S3(t):
                P.op("dve", lambda e: e.tensor_tensor(out=hc[0:R, t, :], in0=hc[0:R, t, :], in1=lnp[slot][0:R, 0, :], op=ALU.mult), reads=[hcB[t], lnpB[slot]], writes=[hcB[t]])
                P.op("dve", lambda e: e.tensor_tensor(out=hc[0:R, t, :], in0=hc[0:R, t, :], in1=lnp[slot][0:R, 1, :], op=ALU.add), reads=[hcB[t], lnpB[slot]], writes=[hcB[t]])

            for i in range(nt + 3):
                for k, st in enumerate((S1, S2, S3, post)):
                    t = i - k
                    if st is not None and 0 <= t < nt:
                        st(t)

        def rms_rstd(R, n, inv_n):
            P.op("act", lambda e: e.activation(out=rs[0:R, 0:n], in_=ss[0:R, 0:n], func=AF.Sqrt, bias=epsR[0:R, :], scale=inv_n), reads=[sB["ss"], sB["ss2"], c2], writes=[sB["rs"]])
            P.op("dve", lambda e: e.reciprocal(out=rs[0:R, 0:n], in_=rs[0:R, 0:n]), reads=[sB["rs"]], writes=[sB["rs"]])

        def passA(src_fn, T, R, nt, kbase, rope_t0, store_h1):
            kcol0 = kbase * 128
            for t in range(nt):
                P.dma("sp", lambda e, t=t: e.dma_start(out=hc[0:R, t, :], in_=src_fn(t)), f"ldhc{t}", writes=[hcB[t]])
            transposes_to_aT(nt, R)
            ensure_ln(1, 0)
            dfr = ffn(1, T, R, nt, ln_fused=True)

            def postA(t):
                if store_h1:
                    P.dma("sp", lambda e: e.dma_start(out=h1_s[(kbase + t) * 128:(kbase + t + 1) * 128, :], in_=hc[:, t, :]), f"sthc{t}", reads=[hcB[t]], writes=[h1sB[kbase + t]])
                transposes_to_aT(nt, R, tiles=[t])
            ln_pipeline(0, R, nt, pre=dfr, post=postA, have_h0=True)
            for t in range(nt):
                kt = kbase + t
                for kc in range(8):
                    P.op("pe", lambda e, t=t, kc=kc: e.matmul(PJ[0:R, 0:416], lhsT=aT[:, kc, t * 128:t * 128 + R], rhs=winkv[:, kc, :], start=(kc == 0), stop=(kc == 7)),
                         reads=[aTB[t], cvA], writes=[pb[6]])
                P.op("act", lambda e: e.activation(out=sq[0:R, 0:128], in_=PJ[0:R, 0:128], func=AF.Square), reads=[pb[6]], writes=[sB["sq"]])
                P.op("act", lambda e: e.activation(out=sq[0:R, 256:384], in_=PJ[0:R, 256:384], func=AF.Square, scale=0.5 ** 0.5, accum_out=ss[0:R, 2:3]), reads=[pb[6]], writes=[sB["sq2"], sB["ss2"]])
                P.op("dve", lambda e: e.tensor_reduce(out=ss[0:R, 0:2], in_=sq[0:R, 0:128].rearrange("p (h d) -> p h d", d=64), axis=AX.X, op=ALU.add), reads=[sB["sq"]], writes=[sB["ss"]])
                rms_rstd(R, 3, 1.0 / 64)
                for h in range(2):
                    P.op("dve", lambda e, h=h: e.scalar_tensor_tensor(out=nrm[0:R, h * 64:(h + 1) * 64], in0=PJ[0:R, h * 64:(h + 1) * 64], scalar=rs[0:R, h:h + 1], in1=gk[0:R, :], op0=ALU.mult, op1=ALU.mult),
                         reads=[pb[6], sB["rs"], cB], writes=[sB["nrm"]])
                s3 = nrm[0:R, 0:128].rearrange("p (h d) -> p h d", h=2)
                d3 = rotb[0:R, 0:128].rearrange("p (h d) -> p h d", h=2)
                if rope_t0 is None:
                    P.op("dve", lambda e, s3=s3, d3=d3: e.tensor_copy(out=d3, in_=s3), reads=[sB["nrm"]], writes=[sB["rotb"]])
                else:
                    tab = ropeA[0:R, rope_t0 + t, :]
                    rope4(s3[:, :, 0:32], s3[:, :, 32:64], tab[:, 0:32].unsqueeze(1).broadcast_to([R, 2, 32]), tab[:, 32:64].unsqueeze(1).broadcast_to([R, 2, 32]),
                          d3[:, :, 0:32], d3[:, :, 32:64], lambda i: rt[i][0:R, 0:64].rearrange("p (h d) -> p h d", h=2), [sB["nrm"]])
                P.op("pe", lambda e: e.transpose(out=bank_bf(5)[:, 0:R], in_=rotb[0:R, 0:128], identity=identb[0:R, 0:R]), reads=[sB["rotb"], c2], writes=[pb[5]])
                evac(KT_A[:, kt * 128:kt * 128 + R], bank_bf(5)[:, 0:R], [pb[5]], [KA[kt]])
                P.op("act", lambda e, kt=kt: e.activation(out=V_A[0:R, kt, :, 0:64], in_=PJ[0:R, 128:256].rearrange("p (h d) -> p h d", h=2), func=AF.Copy), reads=[pb[6]], writes=[VA[kt]])
                P.op("dve", lambda e: e.scalar_tensor_tensor(out=cnb[0:R, 0:128], in0=PJ[0:R, 256:384], scalar=rs[0:R, 2:3], in1=gckv[0:R, :], op0=ALU.mult, op1=ALU.mult),
                     reads=[pb[6], sB["rs"], cB], writes=[sB["cnb"]])
                P.op("pe", lambda e: e.transpose(out=bank_bf(5)[:, 128:128 + R], in_=cnb[0:R, 0:128], identity=identb[0:R, 0:R]), reads=[sB["cnb"], c2], writes=[pb[5]])
                evac(ckvT[:, t * 128:t * 128 + R], bank_bf(5)[:, 128:128 + R], [pb[5]], [sB["ckvT"]])
                P.op("dve", lambda e: e.tensor_copy(out=nrm[0:R, 128:160], in_=PJ[0:R, 384:416]), reads=[pb[6]], writes=[sB["nrm"]])
                if rope_t0 is None:
                    P.op("dve", lambda e: e.tensor_copy(out=rotb[0:R, 128:160], in_=nrm[0:R, 128:160]), reads=[sB["nrm"]], writes=[sB["rotb"]])
                else:
                    tab = ropeB[0:R, rope_t0 + t, :]
                    rope4(nrm[0:R, 128:144], nrm[0:R, 144:160], tab[:, 0:16], tab[:, 16:32], rotb[0:R, 128:144], rotb[0:R, 144:160],
                          lambda i: rt[i][0:R, 0:16], [sB["nrm"]])
                P.op("pe", lambda e: e.transpose(out=bank_bf(5)[:, 256:256 + R], in_=rotb[0:R, 128:256], identity=identb[0:R, 0:R]), reads=[sB["rotb"], c2], writes=[pb[5]])
                evac(kpeT[0:32, t * 128:t * 128 + R], bank_bf(5)[0:32, 256:256 + R], [pb[5]], [sB["kpeT"]])
            for h in range(8):
                bk = 4 + (h % 2)
                P.op("pe", lambda e, h=h, bk=bk: e.matmul(bank(bk)[0:96, 0:T], lhsT=wukp[:, h, :], rhs=ckvT[:, 0:T], start=True, stop=False), reads=[cvA, sB["ckvT"]], writes=[pb[bk]])
                P.op("pe", lambda e, h=h, bk=bk: e.matmul(bank(bk)[0:96, 0:T], lhsT=sel[:, :], rhs=kpeT[:, 0:T], start=False, stop=True), reads=[c2, sB["kpeT"]], writes=[pb[bk]])
                evac(KT_B[0:96, h, kcol0:kcol0 + T], bank(bk)[0:96, 0:T], [pb[bk]], KB[kbase:kbase + nt])
            for t in range(nt):
                kt = kbase + t
                P.op("pe", lambda e, t=t: e.matmul(PJ[0:R, 0:512], lhsT=ckvT[:, t * 128:t * 128 + R], rhs=wuv[:].rearrange("p h d -> p (h d)"), start=True, stop=True), reads=[sB["ckvT"], cvA], writes=[pb[6]])
                evac(V_B[0:R, kt, :, 0:64], PJ[0:R, 0:512].rearrange("p (h d) -> p h d", h=8), [pb[6]], [VB[kt]])

        def attention_head(QT_ap, KT_fn, V_fn, Kbufs, Vbufs, dk, scale, hb, ocol, qbufs, prev_tail):
            otb = 4 + hb
            NKC = NKT + 1
            LOOK = 2

            def st_mm(kc):
                bk = kc % 4
                P.op("pe", lambda e: e.matmul(bank(bk)[:, :], lhsT=KT_fn(kc), rhs=QT_ap, start=True, stop=True), reads=[Kbufs[kc]] + qbufs, writes=[pb[bk]])

            def exp_pv(kc):
                bk = kc % 4
                pi = kc % NPT
                P.op("act", lambda e: e.activation(out=PT[pi][:, :], in_=bank(bk)[:, :], func=AF.Exp, scale=scale), reads=[pb[bk]], writes=[PTB[pi]])
                P.op("pe", lambda e: e.matmul(bank(otb)[0:65, :], lhsT=V_fn(kc), rhs=PT[pi][:, :], start=(kc == 0), stop=(kc == NKC - 1)), reads=[Vbufs[kc], PTB[pi]], writes=[pb[otb]])

            for kc in range(min(LOOK, NKC)):
                st_mm(kc)
            for kc in range(NKC):
                if kc + LOOK < NKC:
                    st_mm(kc + LOOK)
                exp_pv(kc)
                if kc == 2 and prev_tail is not None:
                    prev_tail()

            def tail():
                P.op("dve", lambda e: e.tensor_copy(out=OTs[hb][0:65, :], in_=bank(otb)[0:65, :]), reads=[pb[otb]], writes=[OTsB[hb]])
                for t in range(4):
                    P.op("pe", lambda e, t=t: e.transpose(out=bank(6)[:, t * 128:t * 128 + 65], in_=OTs[hb][0:65, t * 128:(t + 1) * 128], identity=identf[0:65, 0:65]), reads=[OTsB[hb], cB], writes=[pb[6]])
                po = bank(6).rearrange("p (t c) -> p t c", t=4)
                P.op("dve", lambda e: e.reciprocal(out=rcp[:, 0:4], in_=po[:, :, 64]), reads=[pb[6]], writes=[sB["rcp"]])
                for t in range(4):
                    P.op("dve", lambda e, t=t: e.tensor_scalar(out=o_tm[:, t, ocol:ocol + 64], in0=po[:, t, 0:64], scalar1=rcp[:, t:t + 1], scalar2=None, op0=ALU.mult),
                         reads=[pb[6], sB["rcp"]], writes=GB[4 * t:4 * t + 4])
            return tail

        wuq = sb("wuq", [128, 2, 768], BF16)
        P.dma("pool", lambda e: e.dma_start(out=wuq[:], in_=w_uq_d.rearrange("(kc p) n -> p kc n", p=128)), "cvB", writes=[cvB], final=True)

        def passB(s, c):
            q0 = c * 512
            for t in range(4):
                P.dma("sp", lambda e, t=t: e.dma_start(out=hc[:, t, :], in_=h1_s[q0 + t * 128:q0 + (t + 1) * 128, :]), f"ldhc{t}", reads=[h1sB[c * 4 + t]], writes=[hcB[t]])
            transposes_to_aT(4, 128)
            for t in range(4):
                P.op("act", lambda e, t=t: e.activation(out=hc[:, t, :], in_=hc[:, t, :], func=AF.Copy, scale=ALPHA), reads=[hcB[t]], writes=[hcB[t]])
            if stop < 3.1:
                return
            for tp in range(2):
                for kc in range(8):
                    slot, sbuf_ = S2.take()
                    for j in range(2):
                        t = tp * 2 + j
                        P.op("pe", lambda e, slot=slot, t=t, j=j, kc=kc: e.matmul(PS[j][:, 0:512], lhsT=aT[:, kc, t * 128:(t + 1) * 128], rhs=slot[:, 0:512], start=(kc == 0), stop=(kc == 7)),
                             reads=[sbuf_, aTB[t]], writes=[pb[2 * j]])
                        P.op("pe", lambda e, slot=slot, t=t, j=j, kc=kc: e.matmul(PS[j][:, 512:768], lhsT=aT[:, kc, t * 128:(t + 1) * 128], rhs=slot[:, 512:768], start=(kc == 0), stop=(kc == 7)),
                             reads=[sbuf_, aTB[t]], writes=[pb[2 * j + 1]])
                if stop < 3.11:
                    continue
                for j in range(2):
                    t = tp * 2 + j
                    pq = PS[j]
                    pbs = [pb[2 * j], pb[2 * j + 1]]
                    P.op("act", lambda e, pq=pq: e.activation(out=sq[:, 0:512], in_=pq[:, 0:512], func=AF.Square), reads=[pbs[0]], writes=[sB["sq"]])
                    P.op("act", lambda e, pq=pq: e.activation(out=sq[:, 512:768], in_=pq[:, 512:768], func=AF.Square, scale=0.5, accum_out=ss[:, 8:9]), reads=[pbs[1]], writes=[sB["sq2"], sB["ss2"]])
                    P.op("dve", lambda e: e.tensor_reduce(out=ss[:, 0:8], in_=sq[:, 0:512].rearrange("p (h d) -> p h d", d=64), axis=AX.X, op=ALU.add), reads=[sB["sq"]], writes=[sB["ss"]])
                    if stop < 3.12:
                        continue
                    rms_rstd(128, 9, 1.0 / 64)
                    P.op("dve", lambda e, pq=pq: e.tensor_tensor(out=nrm[:, 0:512].rearrange("p (h d) -> p h d", h=8), in0=pq[:, 0:512].rearrange("p (h d) -> p h d", h=8),
                                                                 in1=rs[:, 0:8].unsqueeze(2).broadcast_to([128, 8, 64]), op=ALU.mult), reads=[pbs[0], sB["rs"]], writes=[sB["nrm"]])
                    P.op("dve", lambda e: e.tensor_tensor(out=nrm[:, 0:512].rearrange("p (h d) -> p h d", h=8), in0=nrm[:, 0:512].rearrange("p (h d) -> p h d", h=8),
                                                          in1=gq[:, :].unsqueeze(1).broadcast_to([128, 8, 64]), op=ALU.mult), reads=[sB["nrm"], cB], writes=[sB["nrm"]])
                    if stop < 3.13:
                        continue
                    src4 = nrm[:, 0:512].rearrange("p (j q d) -> p j q d", j=2, q=4)
                    dst4 = rotb[:, 0:512].rearrange("p (q j d) -> p j q d", q=4, j=2)
                    tab = ropeA[:, c * 4 + t, :]
                    cs = tab[:, 0:32].unsqueeze(1).unsqueeze(1).broadcast_to([128, 2, 4, 32])
                    sn = tab[:, 32:64].unsqueeze(1).unsqueeze(1).broadcast_to([128, 2, 4, 32])
                    rope4(src4[:, :, :, 0:32], src4[:, :, :, 32:64], cs, sn, dst4[:, :, :, 0:32], dst4[:, :, :, 32:64],
                          lambda i: rt[i][:, 0:256].rearrange("p (j q d) -> p j q d", j=2, q=4), [sB["nrm"]])
                    if stop < 3.14:
                        continue
                    for p4 in range(4):
                        P.op("pe", lambda e, p4=p4: e.transpose(out=bank_bf(5)[:, p4 * 128:(p4 + 1) * 128], in_=rotb[:, p4 * 128:(p4 + 1) * 128], identity=identb[:, :]), reads=[sB["rotb"], c2], writes=[pb[5]])
                    evac(QT_A[0:64, 0:4, t * 128:(t + 1) * 128], bank_bf(5)[0:64, 0:512].rearrange("p (q n) -> p q n", q=4), [pb[5]], [qaB])
                    evac(QT_A[64:128, 4:8, t * 128:(t + 1) * 128], bank_bf(5)[64:128, 0:512].rearrange("p (q n) -> p q n", q=4), [pb[5]], [qaB])
                    if stop < 3.15:
                        continue
                    P.op("dve", lambda e, pq=pq: e.scalar_tensor_tensor(out=cnb[:, 0:256], in0=pq[:, 512:768], scalar=rs[:, 8:9], in1=gcq[:, :], op0=ALU.mult, op1=ALU.mult),
                         reads=[pbs[1], sB["rs"], cB], writes=[sB["cnb"]])
                    if stop < 3.16:
                        continue
                    for k2 in range(2):
                        P.op("pe", lambda e, k2=k2: e.transpose(out=bank_bf(5)[:, 512 + k2 * 128:512 + (k2 + 1) * 128], in_=cnb[:, k2 * 128:(k2 + 1) * 128], identity=identb[:, :]), reads=[sB["cnb"], c2], writes=[pb[5]])
                    if stop < 3.17:
                        continue
                    if stop == 3.19:
                        P.op("dve", lambda e, t=t: e.tensor_copy(out=ckvT[:, t * 128:(t + 1) * 128], in_=cnb[:, 0:128]), reads=[sB["cnb"]], writes=[sB["ckvT"]])
                        continue
                    if stop == 3.18:
                        P.op("dve", lambda e, t=t: e.tensor_copy(out=cqT[:, 0, t * 128:(t + 1) * 128], in_=cnb[:, 0:128]), reads=[sB["cnb"]], writes=[sB["cqT"]])
                        continue
                    for k2 in range(2):
                        P.op("dve", lambda e, t=t, k2=k2: e.tensor_copy(out=cqT[:, k2, t * 128:(t + 1) * 128], in_=bank_bf(5)[:, 512 + k2 * 128:512 + (k2 + 1) * 128]), reads=[pb[5]], writes=[sB["cqT"]])
            if stop < 3.2:
                return
            for t in range(4):
                for (lo, hi, bk) in ((0, 512, 6), (512, 768, 7)):
                    for k2 in range(2):
                        P.op("pe", lambda e, t=t, lo=lo, hi=hi, k2=k2: e.matmul(PJ[:, lo:hi], lhsT=cqT[:, k2, t * 128:(t + 1) * 128], rhs=wuq[:, k2, lo:hi], start=(k2 == 0), stop=(k2 == 1)),
                             reads=[sB["cqT"], cvB], writes=[pb[bk]])
                if stop < 3.21:
                    continue
                P.op("act", lambda e: e.activation(out=sq[:, 0:768], in_=PJ[:, 0:768], func=AF.Copy), reads=[pb[6], pb[7]], writes=[sB["sq"], sB["sq2"]])
                qb3 = sq[:, 0:768].rearrange("p (h d) -> p h d", h=8)
                rb3 = rotb[:, 0:768].rearrange("p (h d) -> p h d", h=8)
                P.op("dve", lambda e, qb3=qb3, rb3=rb3: e.tensor_copy(out=rb3[:, :, 0:64], in_=qb3[:, :, 0:64]), reads=[sB["sq"], sB["sq2"]], writes=[sB["rotb"]])
                if stop < 3.22:
                    continue
                tab = ropeB[:, c * 4 + t, :]
                cs = tab[:, 0:16].unsqueeze(1).broadcast_to([128, 8, 16])
                sn = tab[:, 16:32].unsqueeze(1).broadcast_to([128, 8, 16])
                rope4(qb3[:, :, 64:80], qb3[:, :, 80:96], cs, sn, rb3[:, :, 64:80], rb3[:, :, 80:96],
                      lambda i: rt[i][:, 0:128].rearrange("p (h d) -> p h d", h=8), [sB["sq"], sB["sq2"]], pool_ok=False)
                if stop < 3.23:
                    continue
                for h in range(8):
                    P.op("pe", lambda e, h=h: e.transpose(out=bank_bf(5)[0:96, h * 128:(h + 1) * 128], in_=rotb[:, h * 96:(h + 1) * 96], identity=identb[:, :]), reads=[sB["rotb"], c2], writes=[pb[5]])
                if stop < 3.24:
                    continue
                evac(QT_B[0:96, :, t * 128:(t + 1) * 128], bank_bf(5)[0:96, :].rearrange("p (h n) -> p h n", h=8), [pb[5]], [qbB])
            if stop < 3.3:
                return
            hbc = [0]
            tail = None
            for h in range(8):
                tail = attention_head(QT_A[:, h, :],
                                      lambda kc: KT_A[:, kc * 128:(kc + 1) * 128],
                                      lambda kc, g=h // 4: V_A[:, kc, g, :], KA, VA, 64, SC_A, hbc[0], h * 64, [qaB], tail)
                hbc[0] ^= 1
            for h in range(8):
                tail = attention_head(QT_B[0:96, h, :],
                                      lambda kc, h=h: KT_B[0:96, h, kc * 128:(kc + 1) * 128],
                                      lambda kc, h=h: V_B[:, kc, h, :], KB, VB, 96, SC_B, hbc[0], 512 + h * 64, [qbB], tail)
                hbc[0] ^= 1
            tail()
            if stop < 3.4:
                return
            for t in range(4):
                gbt = GB[4 * t:4 * t + 4]
                P.op("act", lambda e, t=t: e.activation(out=sq[:, 0:512], in_=o_tm[:, t, 0:512], func=AF.Square, accum_out=ss[:, 2 * t:2 * t + 1]), reads=gbt, writes=[sB["sq"], sB["ss"]])
                P.op("act", lambda e, t=t: e.activation(out=nrm[:, 0:512], in_=o_tm[:, t, 512:1024], func=AF.Square, accum_out=ss[:, 2 * t + 1:2 * t + 2]), reads=gbt, writes=[sB["nrm"], sB["ss"]])
            rms_rstd(128, 8, 1.0 / 512)
            for t in range(4):
                gbt = GB[4 * t:4 * t + 4]
                for g in range(2):
                    P.op("dve", lambda e, t=t, g=g: e.scalar_tensor_tensor(out=on_tm[:, t, g * 512:(g + 1) * 512], in0=o_tm[:, t, g * 512:(g + 1) * 512], scalar=rs[:, 2 * t + g:2 * t + g + 1], in1=gout[:, g * 512:(g + 1) * 512], op0=ALU.mult, op1=ALU.mult),
                         reads=gbt + [sB["rs"], cB], writes=GB[16 + 2 * t:18 + 2 * t])
                for kc in range(8):
                    P.op("pe", lambda e, t=t, kc=kc: e.transpose(out=bank_bf(5 + (t % 2))[:, kc * 128:(kc + 1) * 128], in_=on_tm[:, t, kc * 128:(kc + 1) * 128], identity=identb[:, :]), reads=GB[16 + 2 * t:18 + 2 * t] + [c2], writes=[pb[5 + (t % 2)]])
                evac(aT[:, :, t * 128:(t + 1) * 128], bank_bf(5 + (t % 2))[:, :].rearrange("p (k n) -> p k n", k=8), [pb[5 + (t % 2)]], [aTB[t]])
            if stop < 3.5:
                return
            for tp in range(2):
                for kc in range(8):
                    slot, sbuf_ = S2.take()
                    for j in range(2):
                        t = tp * 2 + j
                        for half in range(2):
                            P.op("pe", lambda e, slot=slot, t=t, j=j, kc=kc, half=half: e.matmul(PS[j][:, half * 512:(half + 1) * 512], lhsT=aT[:, kc, t * 128:(t + 1) * 128], rhs=slot[:, half * 512:(half + 1) * 512], start=(kc == 0), stop=(kc == 7)),
                                 reads=[sbuf_, aTB[t]], writes=[pb[2 * j + half]])
                for j in range(2):
                    t = tp * 2 + j
                    for half in range(2):
                        P.op("dve", lambda e, t=t, j=j, half=half: e.tensor_tensor(out=hc[:, t, half * 512:(half + 1) * 512], in0=PS[j][:, half * 512:(half + 1) * 512], in1=hc[:, t, half * 512:(half + 1) * 512], op=ALU.add),
                             reads=[pb[2 * j + half], hcB[t]], writes=[hcB[t]])
            if stop < 3.6:
                return
            ensure_ln(2, 0)
            ln_pipeline(0, 128, 4, post=lambda t: transposes_to_aT(4, 128, tiles=[t]))
            ensure_ln(3, 1)
            dfr = ffn(2, 512, 128, 4, ln_fused=True)

            def postB(t):
                P.dma("sp", lambda e: e.dma_start(out=out_d[s, q0 + t * 128:q0 + (t + 1) * 128, :], in_=hc[:, t, :]), f"sthc{t}", reads=[hcB[t]])
            ln_pipeline(1, 128, 4, pre=dfr, post=postB, have_h0=True)

        pass

        def rope4(x1, x2, cs, sn, d1, d2, tvf, rds, pool_ok=False):
            tv = [tvf(i) for i in range(4)]
            e2 = "pool" if pool_ok else "dve"
            P.op("dve", lambda e: e.tensor_tensor(out=tv[0], in0=x1, in1=cs, op=ALU.mult), reads=rds + [cB], writes=[sB["rt0"]])
            P.op(e2, lambda e: e.tensor_tensor(out=tv[1], in0=x2, in1=sn, op=ALU.mult), reads=rds + [cB], writes=[sB["rt1"]])
            P.op("dve", lambda e: e.tensor_tensor(out=tv[2], in0=x1, in1=sn, op=ALU.mult), reads=rds + [cB], writes=[sB["rt2"]])
            P.op(e2, lambda e: e.tensor_tensor(out=tv[3], in0=x2, in1=cs, op=ALU.mult), reads=rds + [cB], writes=[sB["rt3"]])
            P.op("dve", lambda e: e.tensor_tensor(out=d1, in0=tv[0], in1=tv[1], op=ALU.subtract), reads=[sB["rt0"], sB["rt1"]], writes=[sB["rotb"]])
            P.op("dve", lambda e: e.tensor_tensor(out=d2, in0=tv[2], in1=tv[3], op=ALU.add), reads=[sB["rt2"], sB["rt3"]], writes=[sB["rotb"]])

        if stop >= 1:
            passA(lambda t: meta_d, NMETA, NMETA, 1, NKT, None, False)
        for s in range(nseq):
            for c in range(nch):
                if stop >= 2:
                    passA(lambda t, s=s, c=c: x_d[s, c * 512 + t * 128:c * 512 + (t + 1) * 128, :], 512, 128, 4, c * 4, c * 4, True)
            for c in range(nch):
                if stop >= 3:
                    passB(s, c)
        if stop >= 99:
            assert S13.taken == len(S13.items) and S2.taken == len(S2.items), (S13.taken, len(S13.items), S2.taken, len(S2.items))
        P.emit()
    return nc


_CACHE = {}


def _rope_tables(SEQ=SEQ):
    def tab(rot_dim):
        axis_dim = rot_dim // 2
        inv = (10000.0 ** (-np.arange(0, axis_dim, 2, dtype=np.float32) / np.float32(axis_dim))).astype(np.float32)
        rows = np.repeat(np.arange(SEQ // 64, dtype=np.float32), 64)
        cols = np.tile(np.arange(64, dtype=np.float32), SEQ // 64)
        ang = np.concatenate([rows[:, None] * inv[None, :], cols[:, None] * inv[None, :]], axis=-1).astype(np.float32)
        return np.concatenate([np.cos(ang), np.sin(ang)], axis=-1).astype(np.float32)
    return tab(64), tab(32)


def kernel(**inputs):
    n = 8
    if "nc" not in _CACHE:
        _CACHE["nc"] = build_program()
    nc = _CACHE["nc"]
    x = np.ascontiguousarray(inputs["x"], dtype=np.float32)
    ropeA, ropeB = _rope_tables()
    shared = {
        "meta": np.ascontiguousarray(inputs["meta_tokens"], dtype=np.float32),
        "w_in": np.ascontiguousarray(inputs["w_in"][0]),
        "w_uq": np.ascontiguousarray(inputs["w_uq"][0]),
        "w_ukv": np.ascontiguousarray(inputs["w_ukv"][0]),
        "w_out": np.ascontiguousarray(inputs["w_out"][0]),
        "ropeA": ropeA, "ropeB": ropeB,
        "ident": np.eye(128, dtype=np.float32),
    }
    for f in (1, 2):
        for k in ("w1", "w3", "w2"):
            shared[f"f{f}{k}"] = np.ascontiguousarray(inputs[f"ffn{f}_{k}"][0])
    for i in (1, 2, 3):
        shared[f"ln{i}_g"] = np.ascontiguousarray(inputs[f"ln{i}_g"]).reshape(1, D)
        shared[f"ln{i}_b"] = np.ascontiguousarray(inputs[f"ln{i}_b"]).reshape(1, D)
    for k in ("q_norm_a", "k_norm_a", "cq_norm", "ckv_norm", "out_norm_a", "out_norm_b"):
        shared[k] = np.ascontiguousarray(inputs[k]).reshape(1, -1)
    in_maps = []
    for i in range(n):
        m = dict(shared)
        m["x"] = x[i * NSEQ:(i + 1) * NSEQ]
        in_maps.append(m)
    res = run_bass_kernel_spmd(nc, in_maps, core_ids=list(range(n)))
    return np.concatenate([np.asarray(r["out"]) for r in res.results], axis=0).astype(np.float32)
```

```python
import numpy as np
from contextlib import ExitStack
import concourse.bass as bass
import concourse.mybir as mybir
from concourse.bass_utils import run_bass_kernel_spmd

F32 = mybir.dt.float32
BF16 = mybir.dt.bfloat16
AF = mybir.ActivationFunctionType
ALU = mybir.AluOpType
AX = mybir.AxisListType

D = 1024
FF = 2816
NFC = 22
SEQ = 2048
NMETA = 16
LK = SEQ + NMETA
NSEQ = 4
ALPHA = 2.0 ** 0.25
LN_EPS = 1e-5
RMS_EPS = 1e-6
SC_A = 64 ** -0.5
SC_B = 96 ** -0.5


class Buf:
    __slots__ = ("name", "w", "r", "psum")

    def __init__(self, name, psum=False):
        self.name = name
        self.w = None
        self.r = []
        self.psum = psum


class Op:
    __slots__ = ("eng", "fn", "waits", "needs_inc", "seq", "dma_sem")

    def __init__(self, eng, fn):
        self.eng = eng
        self.fn = fn
        self.waits = []
        self.needs_inc = False
        self.seq = None
        self.dma_sem = None


class Prog:
    def __init__(self, nc):
        self.nc = nc
        self.ops = {e: [] for e in ("pe", "act", "dve", "pool", "sp")}
        self.dma_sems = {}

    def _deps(self, eng, reads, writes, is_dma=False):
        deps = []
        for b in reads:
            if b.w is not None:
                deps.append((b.w, True))
            if b.psum:
                for t in b.r:
                    if t[0] == "c" and t[1].eng != eng:
                        deps.append((t, True))
        for b in writes:
            if b.w is not None:
                deps.append((b.w, False))
            for t in b.r:
                deps.append((t, False))
        out = []
        for t, raw in deps:
            if t[0] == "c":
                o = t[1]
                if o.eng == eng and not is_dma:
                    if eng == "pe":
                        continue
                o.needs_inc = True
            out.append(t)
        return out

    def _commit(self, tok, reads, writes):
        for b in reads:
            b.r.append(tok)
        for b in writes:
            b.w = tok
            b.r = []

    def op(self, eng, fn, reads=(), writes=()):
        o = Op(eng, fn)
        o.waits = self._deps(eng, reads, writes)
        self.ops[eng].append(o)
        self._commit(("c", o), reads, writes)
        return o

    def dma(self, q, fn, sem, reads=(), writes=(), final=False):
        o = Op(q, fn)
        o.waits = [t for t in self._deps(q, reads, writes, True) if not (t[0] == "d" and t[1] == sem and t[2] is None)]
        ent = self.dma_sems.setdefault(sem, [None, 0])
        ent[1] += 16
        o.dma_sem = sem
        self.ops[q].append(o)
        self._commit(("d", sem, None if final else ent[1]), reads, writes)
        return o

    def emit(self):
        nc = self.nc
        with ExitStack() as st:
            esem = {e: st.enter_context(nc.semaphore("s_" + e)) for e in self.ops}
            for name, ent in self.dma_sems.items():
                ent[0] = st.enter_context(nc.semaphore("d_" + name))
            for e, lst in self.ops.items():
                c = 0
                for o in lst:
                    if o.dma_sem is None and o.needs_inc:
                        c += 1
                        o.seq = c
            block = st.enter_context(nc.Block())
            starters = {"pe": block.tensor, "act": block.scalar, "dve": block.vector,
                        "pool": block.gpsimd, "sp": block.sync}

            def run_engine(e):
                lst = self.ops[e]

                def body(eng):
                    seen = {}
                    for o in lst:
                        for t in o.waits:
                            if t[0] == "c":
                                key, val, sem = t[1].eng, t[1].seq, esem[t[1].eng]
                            else:
                                ent = self.dma_sems[t[1]]
                                key, sem = "d:" + t[1], ent[0]
                                val = ent[1] if t[2] is None else t[2]
                            if seen.get(key, 0) >= val:
                                continue
                            seen[key] = val
                            eng.wait_ge(sem, val)
                        ins = o.fn(eng)
                        if o.dma_sem is not None:
                            ins.then_inc(self.dma_sems[o.dma_sem][0], 16)
                        elif o.needs_inc:
                            ins.then_inc(esem[e], 1)
                    if e == "sp":
                        for ent in self.dma_sems.values():
                            eng.wait_ge(ent[0], ent[1])
                starters[e](body)

            for e in ("sp", "pool", "act", "dve", "pe"):
                run_engine(e)


def build_program(nseq=NSEQ, nch=4, nfc=22, stop=99):
    SEQ_ = nch * 512
    LK_ = SEQ_ + 128
    NKT = nch * 4
    FF_ = nfc * 128
    nc = bass.Bass("TRN2", target_bir_lowering=False, dynamic_dma_scratch_size=8192)
    P = Prog(nc)

    def din(name, shape):
        return nc.dram_tensor(name, list(shape), F32, kind="ExternalInput").ap()

    x_d = din("x", [nseq, SEQ_, D])
    meta_d = din("meta", [NMETA, D])
    fw = {}
    for f in (1, 2):
        fw[f] = (din(f"f{f}w1", [D, FF_]), din(f"f{f}w3", [D, FF_]), din(f"f{f}w2", [FF_, D]))
    w_in_d = din("w_in", [D, 1184])
    w_uq_d = din("w_uq", [256, 768])
    w_ukv_d = din("w_ukv", [128, 1024])
    w_out_d = din("w_out", [D, D])
    ln_d = {i: (din(f"ln{i}_g", [1, D]), din(f"ln{i}_b", [1, D])) for i in (1, 2, 3)}
    qn_d = din("q_norm_a", [1, 64])
    kn_d = din("k_norm_a", [1, 64])
    cqn_d = din("cq_norm", [1, 256])
    ckvn_d = din("ckv_norm", [1, 128])
    ona_d = din("out_norm_a", [1, 512])
    onb_d = din("out_norm_b", [1, 512])
    ropeA_d = din("ropeA", [SEQ_, 64])
    ropeB_d = din("ropeB", [SEQ_, 32])
    ident_d = din("ident", [128, 128])
    out_d = nc.dram_tensor("out", [nseq, SEQ_, D], F32, kind="ExternalOutput").ap()

    def dscr(name, shape, dt=BF16):
        return nc.dram_tensor(name, list(shape), dt, kind="Internal").ap()

    w13s = {f: dscr(f"w13s{f}", [nfc, 128, 2, 8, 128]) for f in (1, 2)}
    w2s = {f: dscr(f"w2s{f}", [FF_, D]) for f in (1, 2)}
    winq_s = dscr("winq_s", [D, 768])
    wuq_s = dscr("wuq_s", [256, 768])
    wout_s = dscr("wout_s", [D, D])
    h1_s = dscr("h1_s", [SEQ_, D], F32)

    with ExitStack() as st:
        def sb(name, shape, dt):
            return st.enter_context(nc.sbuf_tensor(name, list(shape), dt))

        def ps(name, shape, dt):
            return st.enter_context(nc.psum_tensor(name, list(shape), dt))

        KT_A = sb("KT_A", [128, LK_], BF16)
        KT_B = sb("KT_B", [128, 8, LK_], BF16)
        V_A = sb("V_A", [128, NKT + 1, 2, 65], BF16)
        V_B = sb("V_B", [128, NKT + 1, 8, 65], BF16)
        hc = sb("hc", [128, 4, D], F32)
        aT = sb("aT", [128, 8, 512], BF16)
        G = sb("G", [128, 24, 512], BF16)
        o_tm = G[:].rearrange("p a b -> p (a b)")[:, 0:8192].bitcast(F32).rearrange("p (t c) -> p t c", t=4)
        on_tm = G[:].rearrange("p a b -> p (a b)")[:, 8192:12288].rearrange("p (t c) -> p t c", t=4)
        NR13, NR2 = 3, 4
        r13 = [sb(f"r13_{i}", [128, 2, 8, 128], BF16) for i in range(NR13)]
        r2 = [sb(f"r2_{i}", [128, 1024], BF16) for i in range(NR2)]
        QT_A = sb("QT_A", [128, 8, 512], BF16)
        QT_B = sb("QT_B", [128, 8, 512], BF16)
        NPT = 3
        PT = [sb(f"PT{i}", [128, 512], BF16) for i in range(NPT)]
        lnp = [sb(f"lnp{i}", [128, 2, D], F32) for i in range(2)]
        ropeA = sb("ropeA_t", [128, NKT, 64], F32)
        ropeB = sb("ropeB_t", [128, NKT, 32], F32)
        gq = sb("gq", [128, 64], F32)
        gk = sb("gk", [128, 64], F32)
        gcq = sb("gcq", [128, 256], F32)
        gckv = sb("gckv", [128, 128], F32)
        gout = sb("gout", [128, 1024], F32)
        identf = sb("identf", [128, 128], F32)
        identb = sb("identb", [128, 128], BF16)
        sel = sb("sel", [128, 96], BF16)
        epsL = sb("epsL", [128, 1], F32)
        epsR = sb("epsR", [128, 1], F32)
        winkv = sb("winkv", [128, 8, 416], BF16)
        wukp = sb("wukp", [128, 8, 96], BF16)
        wuv = sb("wuv", [128, 8, 64], BF16)
        sa = [sb(f"sa{i}", [128, 512], F32) for i in range(2)]
        OTs = [sb(f"OTs{i}", [128, 512], F32) for i in range(2)]
        sq = sb("sq", [128, 768], F32)
        nrm = sb("nrm", [128, 512], F32)
        rt = [sb(f"rt{i}", [128, 256], F32) for i in range(4)]
        rotb = sb("rotb", [128, 768], BF16)
        cnb = sb("cnb", [128, 256], BF16)
        ckvT = sb("ckvT", [128, 512], BF16)
        kpeT = sb("kpeT", [128, 512], BF16)
        cqT = sb("cqT", [128, 2, 512], BF16)
        stats = sb("stats", [128, 4, 2, 6], F32)
        mv = sb("mv", [128, 4, 2], F32)
        rstd = sb("rstd", [128, 4], F32)
        nmr = sb("nmr", [128, 4], F32)
        ss = sb("ss", [128, 16], F32)
        rs = sb("rs", [128, 16], F32)
        rcp = sb("rcp", [128, 4], F32)

        PS = [ps(f"ps{i}", [128, 1024], F32) for i in range(4)]

        def bank(i):
            return PS[i // 2][:, (i % 2) * 512:(i % 2) * 512 + 512]

        def bank_bf(i):
            return bank(i).bitcast(BF16)

        PJ = PS[3]
        pb = [Buf(f"bank{i}", psum=True) for i in range(8)]

        B = {}

        def bf(name):
            if name not in B:
                B[name] = Buf(name)
            return B[name]

        KA = [bf(f"KA{i}") for i in range(NKT + 1)]
        KB = [bf(f"KB{i}") for i in range(NKT + 1)]
        VA = [bf(f"VA{i}") for i in range(NKT + 1)]
        VB = [bf(f"VB{i}") for i in range(NKT + 1)]
        hcB = [bf(f"hc{i}") for i in range(4)]
        aTB = [bf(f"aT{i}") for i in range(4)]
        GB = [bf(f"G{i}") for i in range(24)]
        r13B = [bf(f"r13_{i}") for i in range(NR13)]
        r2B = [bf(f"r2_{i}") for i in range(NR2)]
        PTB = [bf(f"PT{i}") for i in range(NPT)]
        lnpB = [bf(f"lnp{i}") for i in range(2)]
        saB = [bf(f"sa{i}") for i in range(2)]
        OTsB = [bf(f"OTs{i}") for i in range(2)]
        h1sB = [bf(f"h1s{i}") for i in range(NKT)]
        cB = bf("consts")
        qaB, qbB = bf("QT_A"), bf("QT_B")
        cvA, cvB, cvC = bf("cvA"), bf("cvB"), bf("cvC")

        P.dma("sp", lambda e: e.dma_start(out=identf[:], in_=ident_d), "c0", writes=[cB], final=True)
        P.dma("sp", lambda e: e.dma_start(out=ropeA[:], in_=ropeA_d.rearrange("(t p) c -> p t c", p=128)), "c0", writes=[cB], final=True)
        P.dma("sp", lambda e: e.dma_start(out=ropeB[:], in_=ropeB_d.rearrange("(t p) c -> p t c", p=128)), "c0", writes=[cB], final=True)
        for tile_, src in ((gq, qn_d), (gk, kn_d), (gcq, cqn_d), (gckv, ckvn_d)):
            P.dma("sp", lambda e, tile_=tile_, src=src: e.dma_start(out=tile_[:], in_=src.partition_broadcast(128)), "c0", writes=[cB], final=True)
        P.dma("sp", lambda e: e.dma_start(out=gout[:, 0:512], in_=ona_d.partition_broadcast(128)), "c0", writes=[cB], final=True)
        P.dma("sp", lambda e: e.dma_start(out=gout[:, 512:1024], in_=onb_d.partition_broadcast(128)), "c0", writes=[cB], final=True)
        c2 = bf("consts2")
        P.op("dve", lambda e: e.tensor_copy(out=identb[:], in_=identf[:]), reads=[cB], writes=[c2])
        P.op("pool", lambda e: e.memset(sel[:], 0.0), writes=[c2])
        P.op("dve", lambda e: e.tensor_copy(out=sel[0:32, 64:96], in_=identf[0:32, 0:32]), reads=[cB, c2], writes=[c2])
        P.op("pool", lambda e: e.memset(kpeT[:], 0.0), writes=[bf("kpeT")])
        P.op("pool", lambda e: e.memset(epsL[:], LN_EPS), writes=[c2])
        P.op("pool", lambda e: e.memset(epsR[:], RMS_EPS), writes=[c2])
        P.op("pool", lambda e: e.memset(V_A[:], 1.0), writes=VA)
        P.op("pool", lambda e: e.memset(V_B[:], 1.0), writes=VB)
        P.op("pool", lambda e: e.memset(V_A[:, NKT], 0.0), writes=[VA[NKT]])
        P.op("pool", lambda e: e.memset(V_B[:, NKT], 0.0), writes=[VB[NKT]])
        P.op("pool", lambda e: e.memset(V_A[0:NMETA, NKT, :, 64:65], 1.0), writes=[VA[NKT]])
        P.op("pool", lambda e: e.memset(V_B[0:NMETA, NKT, :, 64:65], 1.0), writes=[VB[NKT]])
        P.op("pool", lambda e: e.memset(KT_A[:, SEQ_:SEQ_ + 128], 0.0), writes=[KA[NKT]])
        P.op("pool", lambda e: e.memset(KT_B[:, :, SEQ_:SEQ_ + 128], 0.0), writes=[KB[NKT]])
        P.op("pool", lambda e: e.memset(QT_A[:], 0.0), writes=[bf("QT_A")])
        P.op("pool", lambda e: e.memset(wukp[:], 0.0), writes=[cvA])
        P.dma("pool", lambda e: e.dma_start(out=winkv[:, :, 0:256], in_=w_in_d[:, 512:768].rearrange("(kc p) n -> p kc n", p=128)), "cvA", writes=[cvA], final=True)
        P.dma("pool", lambda e: e.dma_start(out=winkv[:, :, 256:416], in_=w_in_d[:, 1024:1184].rearrange("(kc p) n -> p kc n", p=128)), "cvA", writes=[cvA], final=True)
        ukv4 = w_ukv_d.rearrange("p (h two d) -> p h two d", h=8, two=2)
        P.dma("pool", lambda e: e.dma_start(out=wukp[:, :, 0:64], in_=ukv4[:, :, 0, :]), "cvA", writes=[cvA], final=True)
        P.dma("pool", lambda e: e.dma_start(out=wuv[:], in_=ukv4[:, :, 1, :]), "cvA", writes=[cvA], final=True)

        def conv_ffn(f, semname, buf, semname2=None, buf2=None):
            w1, w3, w2 = fw[f]
            semname2 = semname2 or semname
            buf2 = buf2 or buf
            for fc in range(nfc):
                for m, w in enumerate((w1, w3)):
                    src = w.rearrange("(kc p) (fc j) -> fc p kc j", p=128, j=128)[fc]
                    P.dma("pool", lambda e, src=src, fc=fc, m=m: e.dma_start(out=w13s[f][fc, :, m, :, :], in_=src),
                          semname, writes=[buf], final=True)
            for fc in range(nfc):
                P.dma("pool", lambda e, fc=fc: e.dma_start(out=w2s[f][fc * 128:(fc + 1) * 128, :], in_=w2[fc * 128:(fc + 1) * 128, :]),
                      semname2, writes=[buf2], final=True)

        cvA2 = bf("cvA2")
        conv_ffn(1, "cvA", cvA, "cvA2", cvA2)
        for kc in range(8):
            P.dma("pool", lambda e, kc=kc: e.dma_start(out=winq_s[kc * 128:(kc + 1) * 128, 0:512], in_=w_in_d[kc * 128:(kc + 1) * 128, 0:512]), "cvB", writes=[cvB], final=True)
            P.dma("pool", lambda e, kc=kc: e.dma_start(out=winq_s[kc * 128:(kc + 1) * 128, 512:768], in_=w_in_d[kc * 128:(kc + 1) * 128, 768:1024]), "cvB", writes=[cvB], final=True)
        for kc in range(8):
            P.dma("pool", lambda e, kc=kc: e.dma_start(out=wout_s[kc * 128:(kc + 1) * 128, :], in_=w_out_d[kc * 128:(kc + 1) * 128, :]), "cvB", writes=[cvB], final=True)
        conv_ffn(2, "cvC", cvC)

        class Stream:
            def __init__(self, slots, bufs, semprefix):
                self.slots, self.bufs, self.pref = slots, bufs, semprefix
                self.items = []
                self.issued = 0
                self.taken = 0

            def _issue(self):
                i = self.issued
                fn, rd = self.items[i]
                s = i % len(self.slots)
                P.dma("sp", lambda e, fn=fn, s=s: fn(e, self.slots[s]), f"{self.pref}{s}", reads=[rd], writes=[self.bufs[s]])
                self.issued += 1

            def take(self):
                i = self.taken
                while self.issued < min(len(self.items), i + len(self.slots)):
                    self._issue()
                self.taken += 1
                s = i % len(self.slots)
                return self.slots[s], self.bufs[s]

        S13 = Stream(r13, r13B, "r13_")
        S2 = Stream(r2, r2B, "r2_")
        cvbuf = {1: cvA, 2: cvC}
        cvbuf2 = {1: cvA2, 2: cvC}

        def sched_ffn(f):
            for fc in range(nfc):
                S13.items.append((lambda e, slot, fc=fc, f=f: e.dma_start(out=slot[:], in_=w13s[f][fc]), cvbuf[f]))
            for half in range(2):
                for fp in range(nfc // 2):
                    src = w2s[f][fp * 256:(fp + 1) * 256, half * 512:(half + 1) * 512].rearrange("(a p) n -> p a n", p=128)
                    S2.items.append((lambda e, slot, src=src: e.dma_start(out=slot[:].rearrange("p (a n) -> p a n", a=2), in_=src), cvbuf2[f]))

        def sched_mixB_q():
            for tp in range(2):
                for kc in range(8):
                    S2.items.append((lambda e, slot, kc=kc: e.dma_start(out=slot[:, 0:768], in_=winq_s[kc * 128:(kc + 1) * 128, :]), cvB))

        def sched_mixB_o():
            for tp in range(2):
                for kc in range(8):
                    S2.items.append((lambda e, slot, kc=kc: e.dma_start(out=slot[:], in_=wout_s[kc * 128:(kc + 1) * 128, :]), cvB))

        sched_ffn(1)
        for s in range(nseq):
            for c in range(nch):
                sched_ffn(1)
            for c in range(nch):
                sched_mixB_q()
                sched_mixB_o()
                sched_ffn(2)

        ln_cur = [None, None]

        def ensure_ln(i, slot):
            if ln_cur[slot] == i:
                return
            ln_cur[slot] = i
            g_d, b_d = ln_d[i]
            P.dma("sp", lambda e: e.dma_start(out=lnp[slot][:, 0, :], in_=g_d.partition_broadcast(128)), f"lnp{slot}", writes=[lnpB[slot]])
            P.dma("sp", lambda e: e.dma_start(out=lnp[slot][:, 1, :], in_=b_d.partition_broadcast(128)), f"lnp{slot}", writes=[lnpB[slot]])

        evac_rr = [0]

        def evac(out, in_, reads, writes):
            evac_rr[0] ^= 1
            if evac_rr[0]:
                P.op("dve", lambda e: e.tensor_copy(out=out, in_=in_), reads=reads, writes=writes)
            else:
                P.op("act", lambda e: e.activation(out=out, in_=in_, func=AF.Copy), reads=reads, writes=writes)

        def transposes_to_aT(nt, R, tiles=None):
            for t in (range(nt) if tiles is None else tiles):
                for kc in range(8):
                    P.op("pe", lambda e, t=t, kc=kc: e.transpose(out=PJ[:, kc * 128:kc * 128 + R], in_=hc[0:R, t, kc * 128:(kc + 1) * 128], identity=identf[0:R, 0:R]),
                         reads=[hcB[t], cB], writes=[pb[6 + kc // 4]])
                pj3 = PJ[:].rearrange("p (k c) -> p k c", k=8)
                evac(aT[:, 0:4, t * 128:t * 128 + R], pj3[:, 0:4, 0:R], [pb[6]], [aTB[t]])
                evac(aT[:, 4:8, t * 128:t * 128 + R], pj3[:, 4:8, 0:R], [pb[7]], [aTB[t]])

        def ffn(f, T, R, nt, ln_fused=False):
            for t in range(nt):
                P.op("act", lambda e, t=t: e.activation(out=hc[0:R, t, :], in_=hc[0:R, t, :], func=AF.Copy, scale=ALPHA),
                     reads=[hcB[t]], writes=[hcB[t]])
            for fc in range(nfc):
                slot, sbuf_ = S13.take()
                ua, ub = (0, 1) if fc % 2 == 0 else (2, 3)
                for m, bk in ((0, ua), (1, ub)):
                    for kc in range(8):
                        P.op("pe", lambda e, slot=slot, m=m, bk=bk, kc=kc: e.matmul(bank(bk)[:, 0:T], lhsT=slot[:, m, kc, :], rhs=aT[:, kc, 0:T], start=(kc == 0), stop=(kc == 7)),
                             reads=[sbuf_] + aTB[0:nt], writes=[pb[bk]])
                si = fc % 2
                P.op("act", lambda e, si=si, ua=ua: e.activation(out=sa[si][:, 0:T], in_=bank(ua)[:, 0:T], func=AF.Silu), reads=[pb[ua]], writes=[saB[si]])
                P.op("dve", lambda e, si=si, ub=ub, fc=fc: e.tensor_tensor(out=G[:, fc, 0:T], in0=sa[si][:, 0:T], in1=bank(ub)[:, 0:T], op=ALU.mult),
                     reads=[saB[si], pb[ub]], writes=[GB[fc]])
            for half in range(2):
                for fp in range(nfc // 2):
                    slot, sbuf_ = S2.take()
                    for a in range(2):
                        fc = fp * 2 + a
                        for t in range(nt):
                            P.op("pe", lambda e, slot=slot, a=a, fc=fc, t=t: e.matmul(bank(4 + t)[0:R, :], lhsT=G[:, fc, t * 128:t * 128 + R], rhs=slot[:, a * 512:(a + 1) * 512], start=(fc == 0), stop=(fc == nfc - 1)),
                                 reads=[sbuf_, GB[fc]], writes=[pb[4 + t]])
                deferred = []
                for t in range(nt):
                    def ev(t=t, half=half):
                        P.op("dve", lambda e: e.scalar_tensor_tensor(out=hc[0:R, t, half * 512:(half + 1) * 512], in0=bank(4 + t)[0:R, :], scalar=0.5, in1=hc[0:R, t, half * 512:(half + 1) * 512], op0=ALU.mult, op1=ALU.add),
                             reads=[pb[4 + t], hcB[t]], writes=[hcB[t]])
                    if ln_fused and half == 1:
                        deferred.append(ev)
                    else:
                        ev()
                        if ln_fused:
                            P.op("dve", lambda e, t=t: e.bn_stats(out=stats[0:R, t, 0, :], in_=hc[0:R, t, 0:512]), reads=[hcB[t]], writes=[stB[t]])
            return deferred

        sB = {n: bf(n) for n in ("stats", "mv", "rstd", "nmr", "sq", "sq2", "ss", "ss2", "rs", "nrm", "rt0", "rt1", "rt2", "rt3", "rotb", "cnb", "ckvT", "kpeT", "cqT", "rcp")}

        def layernorm(slot, R, nt):
            for t in range(nt):
                for h in range(2):
                    P.op("dve", lambda e, t=t, h=h: e.bn_stats(out=stats[0:R, t, h, :], in_=hc[0:R, t, h * 512:(h + 1) * 512]), reads=[hcB[t]], writes=[sB["stats"]])
                P.op("dve", lambda e, t=t: e.bn_aggr(out=mv[0:R, t, :], in_=stats[0:R, t].rearrange("p a b -> p (a b)")), reads=[sB["stats"]], writes=[sB["mv"]])
            P.op("act", lambda e: e.activation(out=rstd[0:R, 0:nt], in_=mv[0:R, 0:nt, 1], func=AF.Sqrt, bias=epsL[0:R, :], scale=1.0), reads=[sB["mv"], c2], writes=[sB["rstd"]])
            P.op("dve", lambda e: e.reciprocal(out=rstd[0:R, 0:nt], in_=rstd[0:R, 0:nt]), reads=[sB["rstd"]], writes=[sB["rstd"]])
            P.op("dve", lambda e: e.scalar_tensor_tensor(out=nmr[0:R, 0:nt], in0=mv[0:R, 0:nt, 0], scalar=-1.0, in1=rstd[0:R, 0:nt], op0=ALU.mult, op1=ALU.mult),
                 reads=[sB["mv"], sB["rstd"]], writes=[sB["nmr"]])
            for t in range(nt):
                P.op("act", lambda e, t=t: e.activation(out=hc[0:R, t, :], in_=hc[0:R, t, :], func=AF.Identity, scale=rstd[0:R, t:t + 1], bias=nmr[0:R, t:t + 1]),
                     reads=[hcB[t], sB["rstd"], sB["nmr"]], writes=[hcB[t]])
                P.op("dve", lambda e, t=t: e.tensor_tensor(out=hc[0:R, t, :], in0=hc[0:R, t, :], in1=lnp[slot][0:R, 0, :], op=ALU.mult), reads=[hcB[t], lnpB[slot]], writes=[hcB[t]])
                P.op("dve", lambda e, t=t: e.tensor_tensor(out=hc[0:R, t, :], in0=hc[0:R, t, :], in1=lnp[slot][0:R, 1, :], op=ALU.add), reads=[hcB[t], lnpB[slot]], writes=[hcB[t]])

        stB = [bf(f"st{i}") for i in range(4)]
        mvB = [bf(f"mv{i}") for i in range(4)]
        rsB = [bf(f"rsd{i}") for i in range(4)]

        def ln_pipeline(slot, R, nt, pre=None, post=None, have_h0=False):
            def S1(t):
                if pre:
                    pre[t]()
                for h in ((1,) if have_h0 else (0, 1)):
                    P.op("dve", lambda e, h=h: e.bn_stats(out=stats[0:R, t, h, :], in_=hc[0:R, t, h * 512:(h + 1) * 512]), reads=[hcB[t]], writes=[stB[t]])
                P.op("dve", lambda e: e.bn_aggr(out=mv[0:R, t, :], in_=stats[0:R, t].rearrange("p a b -> p (a b)")), reads=[stB[t]], writes=[mvB[t]])
                P.op("act", lambda e: e.activation(out=rstd[0:R, t:t + 1], in_=mv[0:R, t, 1:2], func=AF.Sqrt, bias=epsL[0:R, :], scale=1.0), reads=[mvB[t], c2], writes=[rsB[t]])

            def S2(t):
                P.op("dve", lambda e: e.reciprocal(out=rstd[0:R, t:t + 1], in_=rstd[0:R, t:t + 1]), reads=[rsB[t]], writes=[rsB[t]])
                P.op("dve", lambda e: e.scalar_tensor_tensor(out=nmr[0:R, t:t + 1], in0=mv[0:R, t, 0:1], scalar=-1.0, in1=rstd[0:R, t:t + 1], op0=ALU.mult, op1=ALU.mult),
                     reads=[mvB[t], rsB[t]], writes=[rsB[t]])
                P.op("act", lambda e: e.activation(out=hc[0:R, t, :], in_=hc[0:R, t, :], func=AF.Identity, scale=rstd[0:R, t:t + 1], bias=nmr[0:R, t:t + 1]),
                     reads=[hcB[t], rsB[t]], writes=[hcB[t]])

            def S3(t):
                P.op("dve", lambda e: e.tensor_tensor(out=hc[0:R, t, :], in0=hc[0:R, t, :], in1=lnp[slot][0:R, 0, :], op=ALU.mult), reads=[hcB[t], lnpB[slot]], writes=[hcB[t]])
                P.op("dve", lambda e: e.tensor_tensor(out=hc[0:R, t, :], in0=hc[0:R, t, :], in1=lnp[slot][0:R, 1, :], op=ALU.add), reads=[hcB[t], lnpB[slot]], writes=[hcB[t]])

            for i in range(nt + 3):
                for k, st in enumerate((S1, S2, S3, post)):
                    t = i - k
                    if st is not None and 0 <= t < nt:
                        st(t)

        def rms_rstd(R, n, inv_n):
            P.op("act", lambda e: e.activation(out=rs[0:R, 0:n], in_=ss[0:R, 0:n], func=AF.Sqrt, bias=epsR[0:R, :], scale=inv_n), reads=[sB["ss"], sB["ss2"], c2], writes=[sB["rs"]])
            P.op("dve", lambda e: e.reciprocal(out=rs[0:R, 0:n], in_=rs[0:R, 0:n]), reads=[sB["rs"]], writes=[sB["rs"]])

        def passA(src_fn, T, R, nt, kbase, rope_t0, store_h1):
            kcol0 = kbase * 128
            for t in range(nt):
                P.dma("sp", lambda e, t=t: e.dma_start(out=hc[0:R, t, :], in_=src_fn(t)), f"ldhc{t}", writes=[hcB[t]])
            transposes_to_aT(nt, R)
            ensure_ln(1, 0)
            dfr = ffn(1, T, R, nt, ln_fused=True)

            def postA(t):
                if store_h1:
                    P.dma("sp", lambda e: e.dma_start(out=h1_s[(kbase + t) * 128:(kbase + t + 1) * 128, :], in_=hc[:, t, :]), f"sthc{t}", reads=[hcB[t]], writes=[h1sB[kbase + t]])
                transposes_to_aT(nt, R, tiles=[t])
            ln_pipeline(0, R, nt, pre=dfr, post=postA, have_h0=True)
            for t in range(nt):
                kt = kbase + t
                for kc in range(8):
                    P.op("pe", lambda e, t=t, kc=kc: e.matmul(PJ[0:R, 0:416], lhsT=aT[:, kc, t * 128:t * 128 + R], rhs=winkv[:, kc, :], start=(kc == 0), stop=(kc == 7)),
                         reads=[aTB[t], cvA], writes=[pb[6]])
                P.op("act", lambda e: e.activation(out=sq[0:R, 0:128], in_=PJ[0:R, 0:128], func=AF.Square), reads=[pb[6]], writes=[sB["sq"]])
                P.op("act", lambda e: e.activation(out=sq[0:R, 256:384], in_=PJ[0:R, 256:384], func=AF.Square, scale=0.5 ** 0.5, accum_out=ss[0:R, 2:3]), reads=[pb[6]], writes=[sB["sq2"], sB["ss2"]])
                P.op("dve", lambda e: e.tensor_reduce(out=ss[0:R, 0:2], in_=sq[0:R, 0:128].rearrange("p (h d) -> p h d", d=64), axis=AX.X, op=ALU.add), reads=[sB["sq"]], writes=[sB["ss"]])
                rms_rstd(R, 3, 1.0 / 64)
                for h in range(2):
                    P.op("dve", lambda e, h=h: e.scalar_tensor_tensor(out=nrm[0:R, h * 64:(h + 1) * 64], in0=PJ[0:R, h * 64:(h + 1) * 64], scalar=rs[0:R, h:h + 1], in1=gk[0:R, :], op0=ALU.mult, op1=ALU.mult),
                         reads=[pb[6], sB["rs"], cB], writes=[sB["nrm"]])
                s3 = nrm[0:R, 0:128].rearrange("p (h d) -> p h d", h=2)
                d3 = rotb[0:R, 0:128].rearrange("p (h d) -> p h d", h=2)
                if rope_t0 is None:
                    P.op("dve", lambda e, s3=s3, d3=d3: e.tensor_copy(out=d3, in_=s3), reads=[sB["nrm"]], writes=[sB["rotb"]])
                else:
                    tab = ropeA[0:R, rope_t0 + t, :]
                    rope4(s3[:, :, 0:32], s3[:, :, 32:64], tab[:, 0:32].unsqueeze(1).broadcast_to([R, 2, 32]), tab[:, 32:64].unsqueeze(1).broadcast_to([R, 2, 32]),
                          d3[:, :, 0:32], d3[:, :, 32:64], lambda i: rt[i][0:R, 0:64].rearrange("p (h d) -> p h d", h=2), [sB["nrm"]])
                P.op("pe", lambda e: e.transpose(out=bank_bf(5)[:, 0:R], in_=rotb[0:R, 0:128], identity=identb[0:R, 0:R]), reads=[sB["rotb"], c2], writes=[pb[5]])
                evac(KT_A[:, kt * 128:kt * 128 + R], bank_bf(5)[:, 0:R], [pb[5]], [KA[kt]])
                P.op("act", lambda e, kt=kt: e.activation(out=V_A[0:R, kt, :, 0:64], in_=PJ[0:R, 128:256].rearrange("p (h d) -> p h d", h=2), func=AF.Copy), reads=[pb[6]], writes=[VA[kt]])
                P.op("dve", lambda e: e.scalar_tensor_tensor(out=cnb[0:R, 0:128], in0=PJ[0:R, 256:384], scalar=rs[0:R, 2:3], in1=gckv[0:R, :], op0=ALU.mult, op1=ALU.mult),
                     reads=[pb[6], sB["rs"], cB], writes=[sB["cnb"]])
                P.op("pe", lambda e: e.transpose(out=bank_bf(5)[:, 128:128 + R], in_=cnb[0:R, 0:128], identity=identb[0:R, 0:R]), reads=[sB["cnb"], c2], writes=[pb[5]])
                evac(ckvT[:, t * 128:t * 128 + R], bank_bf(5)[:, 128:128 + R], [pb[5]], [sB["ckvT"]])
                P.op("dve", lambda e: e.tensor_copy(out=nrm[0:R, 128:160], in_=PJ[0:R, 384:416]), reads=[pb[6]], writes=[sB["nrm"]])
                if rope_t0 is None:
                    P.op("dve", lambda e: e.tensor_copy(out=rotb[0:R, 128:160], in_=nrm[0:R, 128:160]), reads=[sB["nrm"]], writes=[sB["rotb"]])
                else:
                    tab = ropeB[0:R, rope_t0 + t, :]
                    rope4(nrm[0:R, 128:144], nrm[0:R, 144:160], tab[:, 0:16], tab[:, 16:32], rotb[0:R, 128:144], rotb[0:R, 144:160],
                          lambda i: rt[i][0:R, 0:16], [sB["nrm"]])
                P.op("pe", lambda e: e.transpose(out=bank_bf(5)[0:32, 256:256 + R], in_=rotb[0:R, 128:160], identity=identb[0:R, 0:R]), reads=[sB["rotb"], c2], writes=[pb[5]])
                evac(kpeT[0:32, t * 128:t * 128 + R], bank_bf(5)[0:32, 256:256 + R], [pb[5]], [sB["kpeT"]])
            for h in range(8):
                bk = 4 + (h % 2)
                P.op("pe", lambda e, h=h, bk=bk: e.matmul(bank(bk)[0:96, 0:T], lhsT=wukp[:, h, :], rhs=ckvT[:, 0:T], start=True, stop=False), reads=[cvA, sB["ckvT"]], writes=[pb[bk]])
                P.op("pe", lambda e, h=h, bk=bk: e.matmul(bank(bk)[0:96, 0:T], lhsT=sel[:, :], rhs=kpeT[:, 0:T], start=False, stop=True), reads=[c2, sB["kpeT"]], writes=[pb[bk]])
                evac(KT_B[0:96, h, kcol0:kcol0 + T], bank(bk)[0:96, 0:T], [pb[bk]], KB[kbase:kbase + nt])
            for t in range(nt):
                kt = kbase + t
                P.op("pe", lambda e, t=t: e.matmul(PJ[0:R, 0:512], lhsT=ckvT[:, t * 128:t * 128 + R], rhs=wuv[:].rearrange("p h d -> p (h d)"), start=True, stop=True), reads=[sB["ckvT"], cvA], writes=[pb[6]])
                evac(V_B[0:R, kt, :, 0:64], PJ[0:R, 0:512].rearrange("p (h d) -> p h d", h=8), [pb[6]], [VB[kt]])

        def attention_head(QT_ap, KT_fn, V_fn, Kbufs, Vbufs, dk, scale, hb, ocol, qbufs, prev_tail):
            otb = 4 + hb
            NKC = NKT + 1
            LOOK = 2

            def st_mm(kc):
                bk = kc % 4
                P.op("pe", lambda e: e.matmul(bank(bk)[:, :], lhsT=KT_fn(kc), rhs=QT_ap, start=True, stop=True), reads=[Kbufs[kc]] + qbufs, writes=[pb[bk]])

            def exp_pv(kc):
                bk = kc % 4
                pi = kc % NPT
                P.op("act", lambda e: e.activation(out=PT[pi][:, :], in_=bank(bk)[:, :], func=AF.Exp, scale=scale), reads=[pb[bk]], writes=[PTB[pi]])
                P.op("pe", lambda e: e.matmul(bank(otb)[0:65, :], lhsT=V_fn(kc), rhs=PT[pi][:, :], start=(kc == 0), stop=(kc == NKC - 1)), reads=[Vbufs[kc], PTB[pi]], writes=[pb[otb]])

            for kc in range(min(LOOK, NKC)):
                st_mm(kc)
            for kc in range(NKC):
                if kc + LOOK < NKC:
                    st_mm(kc + LOOK)
                exp_pv(kc)
                if kc == 2 and prev_tail is not None:
                    prev_tail()

            def tail():
                P.op("dve", lambda e: e.tensor_copy(out=OTs[hb][0:65, :], in_=bank(otb)[0:65, :]), reads=[pb[otb]], writes=[OTsB[hb]])
                for t in range(4):
                    P.op("pe", lambda e, t=t: e.transpose(out=bank(6)[:, t * 128:t * 128 + 65], in_=OTs[hb][0:65, t * 128:(t + 1) * 128], identity=identf[0:65, 0:65]), reads=[OTsB[hb], cB], writes=[pb[6]])
                po = bank(6).rearrange("p (t c) -> p t c", t=4)
                P.op("dve", lambda e: e.reciprocal(out=rcp[:, 0:4], in_=po[:, :, 64]), reads=[pb[6]], writes=[sB["rcp"]])
                for t in range(4):
                    P.op("dve", lambda e, t=t: e.tensor_scalar(out=o_tm[:, t, ocol:ocol + 64], in0=po[:, t, 0:64], scalar1=rcp[:, t:t + 1], scalar2=None, op0=ALU.mult),
                         reads=[pb[6], sB["rcp"]], writes=GB[4 * t:4 * t + 4])
            return tail

        wuq = sb("wuq", [128, 2, 768], BF16)
        P.dma("pool", lambda e: e.dma_start(out=wuq[:], in_=w_uq_d.rearrange("(kc p) n -> p kc n", p=128)), "cvB", writes=[cvB], final=True)

        def passB(s, c):
            q0 = c * 512
            for t in range(4):
                P.dma("sp", lambda e, t=t: e.dma_start(out=hc[:, t, :], in_=h1_s[q0 + t * 128:q0 + (t + 1) * 128, :]), f"ldhc{t}", reads=[h1sB[c * 4 + t]], writes=[hcB[t]])
            transposes_to_aT(4, 128)
            for t in range(4):
                P.op("act", lambda e, t=t: e.activation(out=hc[:, t, :], in_=hc[:, t, :], func=AF.Copy, scale=ALPHA), reads=[hcB[t]], writes=[hcB[t]])
            if stop < 3.1:
                return
            for tp in range(2):
                for kc in range(8):
                    slot, sbuf_ = S2.take()
                    for j in range(2):
                        t = tp * 2 + j
                        P.op("pe", lambda e, slot=slot, t=t, j=j, kc=kc: e.matmul(PS[j][:, 0:512], lhsT=aT[:, kc, t * 128:(t + 1) * 128], rhs=slot[:, 0:512], start=(kc == 0), stop=(kc == 7)),
                             reads=[sbuf_, aTB[t]], writes=[pb[2 * j]])
                        P.op("pe", lambda e, slot=slot, t=t, j=j, kc=kc: e.matmul(PS[j][:, 512:768], lhsT=aT[:, kc, t * 128:(t + 1) * 128], rhs=slot[:, 512:768], start=(kc == 0), stop=(kc == 7)),
                             reads=[sbuf_, aTB[t]], writes=[pb[2 * j + 1]])
                if stop < 3.11:
                    continue
                for j in range(2):
                    t = tp * 2 + j
                    pq = PS[j]
                    pbs = [pb[2 * j], pb[2 * j + 1]]
                    P.op("act", lambda e, pq=pq: e.activation(out=sq[:, 0:512], in_=pq[:, 0:512], func=AF.Square), reads=[pbs[0]], writes=[sB["sq"]])
                    P.op("act", lambda e, pq=pq: e.activation(out=sq[:, 512:768], in_=pq[:, 512:768], func=AF.Square, scale=0.5, accum_out=ss[:, 8:9]), reads=[pbs[1]], writes=[sB["sq2"], sB["ss2"]])
                    P.op("dve", lambda e: e.tensor_reduce(out=ss[:, 0:8], in_=sq[:, 0:512].rearrange("p (h d) -> p h d", d=64), axis=AX.X, op=ALU.add), reads=[sB["sq"]], writes=[sB["ss"]])
                    if stop < 3.12:
                        continue
                    rms_rstd(128, 9, 1.0 / 64)
                    P.op("dve", lambda e, pq=pq: e.tensor_tensor(out=nrm[:, 0:512].rearrange("p (h d) -> p h d", h=8), in0=pq[:, 0:512].rearrange("p (h d) -> p h d", h=8),
                                                                 in1=rs[:, 0:8].unsqueeze(2).broadcast_to([128, 8, 64]), op=ALU.mult), reads=[pbs[0], sB["rs"]], writes=[sB["nrm"]])
                    P.op("dve", lambda e: e.tensor_tensor(out=nrm[:, 0:512].rearrange("p (h d) -> p h d", h=8), in0=nrm[:, 0:512].rearrange("p (h d) -> p h d", h=8),
                                                          in1=gq[:, :].unsqueeze(1).broadcast_to([128, 8, 64]), op=ALU.mult), reads=[sB["nrm"], cB], writes=[sB["nrm"]])
                    if stop < 3.13:
                        continue
                    src4 = nrm[:, 0:512].rearrange("p (j q d) -> p j q d", j=2, q=4)
                    dst4 = rotb[:, 0:512].rearrange("p (q j d) -> p j q d", q=4, j=2)
                    tab = ropeA[:, c * 4 + t, :]
                    cs = tab[:, 0:32].unsqueeze(1).unsqueeze(1).broadcast_to([128, 2, 4, 32])
                    sn = tab[:, 32:64].unsqueeze(1).unsqueeze(1).broadcast_to([128, 2, 4, 32])
                    rope4(src4[:, :, :, 0:32], src4[:, :, :, 32:64], cs, sn, dst4[:, :, :, 0:32], dst4[:, :, :, 32:64],
                          lambda i: rt[i][:, 0:256].rearrange("p (j q d) -> p j q d", j=2, q=4), [sB["nrm"]])
                    if stop < 3.14:
                        continue
                    for p4 in range(4):
                        P.op("pe", lambda e, p4=p4: e.transpose(out=bank_bf(5)[:, p4 * 128:(p4 + 1) * 128], in_=rotb[:, p4 * 128:(p4 + 1) * 128], identity=identb[:, :]), reads=[sB["rotb"], c2], writes=[pb[5]])
                    evac(QT_A[0:64, 0:4, t * 128:(t + 1) * 128], bank_bf(5)[0:64, 0:512].rearrange("p (q n) -> p q n", q=4), [pb[5]], [qaB])
                    evac(QT_A[64:128, 4:8, t * 128:(t + 1) * 128], bank_bf(5)[64:128, 0:512].rearrange("p (q n) -> p q n", q=4), [pb[5]], [qaB])
                    if stop < 3.15:
                        continue
                    P.op("dve", lambda e, pq=pq: e.scalar_tensor_tensor(out=cnb[:, 0:256], in0=pq[:, 512:768], scalar=rs[:, 8:9], in1=gcq[:, :], op0=ALU.mult, op1=ALU.mult),
                         reads=[pbs[1], sB["rs"], cB], writes=[sB["cnb"]])
                    if stop < 3.16:
                        continue
                    for k2 in range(2):
                        P.op("pe", lambda e, k2=k2: e.transpose(out=bank_bf(5)[:, 512 + k2 * 128:512 + (k2 + 1) * 128], in_=cnb[:, k2 * 128:(k2 + 1) * 128], identity=identb[:, :]), reads=[sB["cnb"], c2], writes=[pb[5]])
                    if stop < 3.17:
                        continue
                    if stop == 3.19:
                        P.op("dve", lambda e, t=t: e.tensor_copy(out=ckvT[:, t * 128:(t + 1) * 128], in_=cnb[:, 0:128]), reads=[sB["cnb"]], writes=[sB["ckvT"]])
                        continue
                    if stop == 3.18:
                        P.op("dve", lambda e, t=t: e.tensor_copy(out=cqT[:, 0, t * 128:(t + 1) * 128], in_=cnb[:, 0:128]), reads=[sB["cnb"]], writes=[sB["cqT"]])
                        continue
                    for k2 in range(2):
                        P.op("dve", lambda e, t=t, k2=k2: e.tensor_copy(out=cqT[:, k2, t * 128:(t + 1) * 128], in_=bank_bf(5)[:, 512 + k2 * 128:512 + (k2 + 1) * 128]), reads=[pb[5]], writes=[sB["cqT"]])
            if stop < 3.2:
                return
            for t in range(4):
                for (lo, hi, bk) in ((0, 512, 6), (512, 768, 7)):
                    for k2 in range(2):
                        P.op("pe", lambda e, t=t, lo=lo, hi=hi, k2=k2: e.matmul(PJ[:, lo:hi], lhsT=cqT[:, k2, t * 128:(t + 1) * 128], rhs=wuq[:, k2, lo:hi], start=(k2 == 0), stop=(k2 == 1)),
                             reads=[sB["cqT"], cvB], writes=[pb[bk]])
                if stop < 3.21:
                    continue
                P.op("act", lambda e: e.activation(out=sq[:, 0:768], in_=PJ[:, 0:768], func=AF.Copy), reads=[pb[6], pb[7]], writes=[sB["sq"], sB["sq2"]])
                qb3 = sq[:, 0:768].rearrange("p (h d) -> p h d", h=8)
                rb3 = rotb[:, 0:768].rearrange("p (h d) -> p h d", h=8)
                P.op("dve", lambda e, qb3=qb3, rb3=rb3: e.tensor_copy(out=rb3[:, :, 0:64], in_=qb3[:, :, 0:64]), reads=[sB["sq"], sB["sq2"]], writes=[sB["rotb"]])
                if stop < 3.22:
                    continue
                tab = ropeB[:, c * 4 + t, :]
                cs = tab[:, 0:16].unsqueeze(1).broadcast_to([128, 8, 16])
                sn = tab[:, 16:32].unsqueeze(1).broadcast_to([128, 8, 16])
                rope4(qb3[:, :, 64:80], qb3[:, :, 80:96], cs, sn, rb3[:, :, 64:80], rb3[:, :, 80:96],
                      lambda i: rt[i][:, 0:128].rearrange("p (h d) -> p h d", h=8), [sB["sq"], sB["sq2"]], pool_ok=False)
                if stop < 3.23:
                    continue
                for h in range(8):
                    P.op("pe", lambda e, h=h: e.transpose(out=bank_bf(5)[0:96, h * 128:(h + 1) * 128], in_=rotb[:, h * 96:(h + 1) * 96], identity=identb[:, :]), reads=[sB["rotb"], c2], writes=[pb[5]])
                if stop < 3.24:
                    continue
                evac(QT_B[0:96, :, t * 128:(t + 1) * 128], bank_bf(5)[0:96, :].rearrange("p (h n) -> p h n", h=8), [pb[5]], [qbB])
            if stop < 3.3:
                return
            hbc = [0]
            tail = None
            for h in range(8):
                tail = attention_head(QT_A[:, h, :],
                                      lambda kc: KT_A[:, kc * 128:(kc + 1) * 128],
                                      lambda kc, g=h // 4: V_A[:, kc, g, :], KA, VA, 64, SC_A, hbc[0], h * 64, [qaB], tail)
                hbc[0] ^= 1
            for h in range(8):
                tail = attention_head(QT_B[0:96, h, :],
                                      lambda kc, h=h: KT_B[0:96, h, kc * 128:(kc + 1) * 128],
                                      lambda kc, h=h: V_B[:, kc, h, :], KB, VB, 96, SC_B, hbc[0], 512 + h * 64, [qbB], tail)
                hbc[0] ^= 1
            tail()
            if stop < 3.4:
                return
            for t in range(4):
                gbt = GB[4 * t:4 * t + 4]
                P.op("act", lambda e, t=t: e.activation(out=sq[:, 0:512], in_=o_tm[:, t, 0:512], func=AF.Square, accum_out=ss[:, 2 * t:2 * t + 1]), reads=gbt, writes=[sB["sq"], sB["ss"]])
                P.op("act", lambda e, t=t: e.activation(out=nrm[:, 0:512], in_=o_tm[:, t, 512:1024], func=AF.Square, accum_out=ss[:, 2 * t + 1:2 * t + 2]), reads=gbt, writes=[sB["nrm"], sB["ss"]])
            rms_rstd(128, 8, 1.0 / 512)
            for t in range(4):
                gbt = GB[4 * t:4 * t + 4]
                for g in range(2):
                    P.op("dve", lambda e, t=t, g=g: e.scalar_tensor_tensor(out=on_tm[:, t, g * 512:(g + 1) * 512], in0=o_tm[:, t, g * 512:(g + 1) * 512], scalar=rs[:, 2 * t + g:2 * t + g + 1], in1=gout[:, g * 512:(g + 1) * 512], op0=ALU.mult, op1=ALU.mult),
                         reads=gbt + [sB["rs"], cB], writes=GB[16 + 2 * t:18 + 2 * t])
                for kc in range(8):
                    P.op("pe", lambda e, t=t, kc=kc: e.transpose(out=bank_bf(5 + (t % 2))[:, kc * 128:(kc + 1) * 128], in_=on_tm[:, t, kc * 128:(kc + 1) * 128], identity=identb[:, :]), reads=GB[16 + 2 * t:18 + 2 * t] + [c2], writes=[pb[5 + (t % 2)]])
                evac(aT[:, :, t * 128:(t + 1) * 128], bank_bf(5 + (t % 2))[:, :].rearrange("p (k n) -> p k n", k=8), [pb[5 + (t % 2)]], [aTB[t]])
            if stop < 3.5:
                return
            for tp in range(2):
                for kc in range(8):
                    slot, sbuf_ = S2.take()
                    for j in range(2):
                        t = tp * 2 + j
                        for half in range(2):
                            P.op("pe", lambda e, slot=slot, t=t, j=j, kc=kc, half=half: e.matmul(PS[j][:, half * 512:(half + 1) * 512], lhsT=aT[:, kc, t * 128:(t + 1) * 128], rhs=slot[:, half * 512:(half + 1) * 512], start=(kc == 0), stop=(kc == 7)),
                                 reads=[sbuf_, aTB[t]], writes=[pb[2 * j + half]])
                for j in range(2):
                    t = tp * 2 + j
                    for half in range(2):
                        P.op("dve", lambda e, t=t, j=j, half=half: e.tensor_tensor(out=hc[:, t, half * 512:(half + 1) * 512], in0=PS[j][:, half * 512:(half + 1) * 512], in1=hc[:, t, half * 512:(half + 1) * 512], op=ALU.add),
                             reads=[pb[2 * j + half], hcB[t]], writes=[hcB[t]])
            if stop < 3.6:
                return
            ensure_ln(2, 0)
            ln_pipeline(0, 128, 4, post=lambda t: transposes_to_aT(4, 128, tiles=[t]))
            ensure_ln(3, 1)
            dfr = ffn(2, 512, 128, 4, ln_fused=True)

            def postB(t):
                P.dma("sp", lambda e: e.dma_start(out=out_d[s, q0 + t * 128:q0 + (t + 1) * 128, :], in_=hc[:, t, :]), f"sthc{t}", reads=[hcB[t]])
            ln_pipeline(1, 128, 4, pre=dfr, post=postB, have_h0=True)

        pass

        def rope4(x1, x2, cs, sn, d1, d2, tvf, rds, pool_ok=False):
            tv = [tvf(i) for i in range(4)]
            e2 = "pool" if pool_ok else "dve"
            P.op("dve", lambda e: e.tensor_tensor(out=tv[0], in0=x1, in1=cs, op=ALU.mult), reads=rds + [cB], writes=[sB["rt0"]])
            P.op(e2, lambda e: e.tensor_tensor(out=tv[1], in0=x2, in1=sn, op=ALU.mult), reads=rds + [cB], writes=[sB["rt1"]])
            P.op("dve", lambda e: e.tensor_tensor(out=tv[2], in0=x1, in1=sn, op=ALU.mult), reads=rds + [cB], writes=[sB["rt2"]])
            P.op(e2, lambda e: e.tensor_tensor(out=tv[3], in0=x2, in1=cs, op=ALU.mult), reads=rds + [cB], writes=[sB["rt3"]])
            P.op("dve", lambda e: e.tensor_tensor(out=d1, in0=tv[0], in1=tv[1], op=ALU.subtract), reads=[sB["rt0"], sB["rt1"]], writes=[sB["rotb"]])
            P.op("dve", lambda e: e.tensor_tensor(out=d2, in0=tv[2], in1=tv[3], op=ALU.add), reads=[sB["rt2"], sB["rt3"]], writes=[sB["rotb"]])

        if stop >= 1:
            passA(lambda t: meta_d, NMETA, NMETA, 1, NKT, None, False)
        for s in range(nseq):
            for c in range(nch):
                if stop >= 2:
                    passA(lambda t, s=s, c=c: x_d[s, c * 512 + t * 128:c * 512 + (t + 1) * 128, :], 512, 128, 4, c * 4, c * 4, True)
            for c in range(nch):
                if stop >= 3:
                    passB(s, c)
        if stop >= 99:
            assert S13.taken == len(S13.items) and S2.taken == len(S2.items), (S13.taken, len(S13.items), S2.taken, len(S2.items))
        P.emit()
    return nc


_CACHE = {}


def _rope_tables(SEQ=SEQ):
    def tab(rot_dim):
        axis_dim = rot_dim // 2
        inv = (10000.0 ** (-np.arange(0, axis_dim, 2, dtype=np.float32) / np.float32(axis_dim))).astype(np.float32)
        rows = np.repeat(np.arange(SEQ // 64, dtype=np.float32), 64)
        cols = np.tile(np.arange(64, dtype=np.float32), SEQ // 64)
        ang = np.concatenate([rows[:, None] * inv[None, :], cols[:, None] * inv[None, :]], axis=-1).astype(np.float32)
        return np.concatenate([np.cos(ang), np.sin(ang)], axis=-1).astype(np.float32)
    return tab(64), tab(32)


def kernel(**inputs):
    n = 8
    if "nc" not in _CACHE:
        _CACHE["nc"] = build_program()
    nc = _CACHE["nc"]
    x = np.ascontiguousarray(inputs["x"], dtype=np.float32)
    ropeA, ropeB = _rope_tables()
    shared = {
        "meta": np.ascontiguousarray(inputs["meta_tokens"], dtype=np.float32),
        "w_in": np.ascontiguousarray(inputs["w_in"][0]),
        "w_uq": np.ascontiguousarray(inputs["w_uq"][0]),
        "w_ukv": np.ascontiguousarray(inputs["w_ukv"][0]),
        "w_out": np.ascontiguousarray(inputs["w_out"][0]),
        "ropeA": ropeA, "ropeB": ropeB,
        "ident": np.eye(128, dtype=np.float32),
    }
    for f in (1, 2):
        for k in ("w1", "w3", "w2"):
            shared[f"f{f}{k}"] = np.ascontiguousarray(inputs[f"ffn{f}_{k}"][0])
    for i in (1, 2, 3):
        shared[f"ln{i}_g"] = np.ascontiguousarray(inputs[f"ln{i}_g"]).reshape(1, D)
        shared[f"ln{i}_b"] = np.ascontiguousarray(inputs[f"ln{i}_b"]).reshape(1, D)
    for k in ("q_norm_a", "k_norm_a", "cq_norm", "ckv_norm", "out_norm_a", "out_norm_b"):
        shared[k] = np.ascontiguousarray(inputs[k]).reshape(1, -1)
    in_maps = []
    for i in range(n):
        m = dict(shared)
        m["x"] = x[i * NSEQ:(i + 1) * NSEQ]
        in_maps.append(m)
    res = run_bass_kernel_spmd(nc, in_maps, core_ids=list(range(n)))
    return np.concatenate([np.asarray(r["out"]) for r in res.results], axis=0).astype(np.float32)
```

```python
import numpy as np
from contextlib import ExitStack
import concourse.bass as bass
import concourse.mybir as mybir
from concourse.bass_utils import run_bass_kernel_spmd

F32 = mybir.dt.float32
BF16 = mybir.dt.bfloat16
AF = mybir.ActivationFunctionType
ALU = mybir.AluOpType
AX = mybir.AxisListType

D = 1024
FF = 2816
NFC = 22
SEQ = 2048
NMETA = 16
LK = SEQ + NMETA
NSEQ = 4
ALPHA = 2.0 ** 0.25
LN_EPS = 1e-5
RMS_EPS = 1e-6
SC_A = 64 ** -0.5
SC_B = 96 ** -0.5


class Buf:
    __slots__ = ("name", "w", "r", "psum")

    def __init__(self, name, psum=False):
        self.name = name
        self.w = None
        self.r = []
        self.psum = psum


class Op:
    __slots__ = ("eng", "fn", "waits", "needs_inc", "seq", "dma_sem")

    def __init__(self, eng, fn):
        self.eng = eng
        self.fn = fn
        self.waits = []
        self.needs_inc = False
        self.seq = None
        self.dma_sem = None


class Prog:
    def __init__(self, nc):
        self.nc = nc
        self.ops = {e: [] for e in ("pe", "act", "dve", "pool", "sp")}
        self.dma_sems = {}

    def _deps(self, eng, reads, writes, is_dma=False):
        deps = []
        for b in reads:
            if b.w is not None:
                deps.append((b.w, True))
            if b.psum:
                for t in b.r:
                    if t[0] == "c" and t[1].eng != eng:
                        deps.append((t, True))
        for b in writes:
            if b.w is not None:
                deps.append((b.w, False))
            for t in b.r:
                deps.append((t, False))
        out = []
        for t, raw in deps:
            if t[0] == "c":
                o = t[1]
                if o.eng == eng and not is_dma:
                    if eng == "pe":
                        continue
                o.needs_inc = True
            out.append(t)
        return out

    def _commit(self, tok, reads, writes):
        for b in reads:
            b.r.append(tok)
        for b in writes:
            b.w = tok
            b.r = []

    def op(self, eng, fn, reads=(), writes=()):
        o = Op(eng, fn)
        o.waits = self._deps(eng, reads, writes)
        self.ops[eng].append(o)
        self._commit(("c", o), reads, writes)
        return o

    def dma(self, q, fn, sem, reads=(), writes=(), final=False):
        o = Op(q, fn)
        o.waits = [t for t in self._deps(q, reads, writes, True) if not (t[0] == "d" and t[1] == sem and t[2] is None)]
        ent = self.dma_sems.setdefault(sem, [None, 0])
        ent[1] += 16
        o.dma_sem = sem
        self.ops[q].append(o)
        self._commit(("d", sem, None if final else ent[1]), reads, writes)
        return o

    def emit(self):
        nc = self.nc
        with ExitStack() as st:
            esem = {e: st.enter_context(nc.semaphore("s_" + e)) for e in self.ops}
            for name, ent in self.dma_sems.items():
                ent[0] = st.enter_context(nc.semaphore("d_" + name))
            for e, lst in self.ops.items():
                c = 0
                for o in lst:
                    if o.dma_sem is None and o.needs_inc:
                        c += 1
                        o.seq = c
            block = st.enter_context(nc.Block())
            starters = {"pe": block.tensor, "act": block.scalar, "dve": block.vector,
                        "pool": block.gpsimd, "sp": block.sync}

            def run_engine(e):
                lst = self.ops[e]

                def body(eng):
                    seen = {}
                    for o in lst:
                        for t in o.waits:
                            if t[0] == "c":
                                key, val, sem = t[1].eng, t[1].seq, esem[t[1].eng]
                            else:
                                ent = self.dma_sems[t[1]]
                                key, sem = "d:" + t[1], ent[0]
                                val = ent[1] if t[2] is None else t[2]
                            if seen.get(key, 0) >= val:
                                continue
                            seen[key] = val
                            eng.wait_ge(sem, val)
                        ins = o.fn(eng)
                        if o.dma_sem is not None:
                            ins.then_inc(self.dma_sems[o.dma_sem][0], 16)
                        elif o.needs_inc:
                            ins.then_inc(esem[e], 1)
                    if e == "sp":
                        for ent in self.dma_sems.values():
                            eng.wait_ge(ent[0], ent[1])
                starters[e](body)

            for e in ("sp", "pool", "act", "dve", "pe"):
                run_engine(e)


def build_program(nseq=NSEQ, nch=4, nfc=22, stop=99):
    SEQ_ = nch * 512
    LK_ = SEQ_ + 128
    NKT = nch * 4
    FF_ = nfc * 128
    nc = bass.Bass("TRN2", target_bir_lowering=False, dynamic_dma_scratch_size=8192)
    P = Prog(nc)

    def din(name, shape):
        return nc.dram_tensor(name, list(shape), F32, kind="ExternalInput").ap()

    x_d = din("x", [nseq, SEQ_, D])
    meta_d = din("meta", [NMETA, D])
    fw = {}
    for f in (1, 2):
        fw[f] = (din(f"f{f}w1", [D, FF_]), din(f"f{f}w3", [D, FF_]), din(f"f{f}w2", [FF_, D]))
    w_in_d = din("w_in", [D, 1184])
    w_uq_d = din("w_uq", [256, 768])
    w_ukv_d = din("w_ukv", [128, 1024])
    w_out_d = din("w_out", [D, D])
    ln_d = {i: (din(f"ln{i}_g", [1, D]), din(f"ln{i}_b", [1, D])) for i in (1, 2, 3)}
    qn_d = din("q_norm_a", [1, 64])
    kn_d = din("k_norm_a", [1, 64])
    cqn_d = din("cq_norm", [1, 256])
    ckvn_d = din("ckv_norm", [1, 128])
    ona_d = din("out_norm_a", [1, 512])
    onb_d = din("out_norm_b", [1, 512])
    ropeA_d = din("ropeA", [SEQ_, 64])
    ropeB_d = din("ropeB", [SEQ_, 32])
    ident_d = din("ident", [128, 128])
    out_d = nc.dram_tensor("out", [nseq, SEQ_, D], F32, kind="ExternalOutput").ap()

    def dscr(name, shape, dt=BF16):
        return nc.dram_tensor(name, list(shape), dt, kind="Internal").ap()

    w13s = {f: dscr(f"w13s{f}", [nfc, 128, 2, 8, 128]) for f in (1, 2)}
    w2s = {f: dscr(f"w2s{f}", [2, nfc // 2, 128, 2, 512]) for f in (1, 2)}
    winq_s = dscr("winq_s", [D, 768])
    wuq_s = dscr("wuq_s", [256, 768])
    wout_s = dscr("wout_s", [D, D])
    h1_s = dscr("h1_s", [SEQ_, D], F32)

    with ExitStack() as st:
        def sb(name, shape, dt):
            return st.enter_context(nc.sbuf_tensor(name, list(shape), dt))

        def ps(name, shape, dt):
            return st.enter_context(nc.psum_tensor(name, list(shape), dt))

        KT_A = sb("KT_A", [128, LK_], BF16)
        KT_B = sb("KT_B", [128, 8, LK_], BF16)
        V_A = sb("V_A", [128, NKT + 1, 2, 65], BF16)
        V_B = sb("V_B", [128, NKT + 1, 8, 65], BF16)
        hc = sb("hc", [128, 4, D], F32)
        aT = sb("aT", [128, 8, 512], BF16)
        G = sb("G", [128, 24, 512], BF16)
        o_tm = G[:].rearrange("p a b -> p (a b)")[:, 0:8192].bitcast(F32).rearrange("p (t c) -> p t c", t=4)
        on_tm = G[:].rearrange("p a b -> p (a b)")[:, 8192:12288].rearrange("p (t c) -> p t c", t=4)
        NR13, NR2 = 3, 4
        r13 = [sb(f"r13_{i}", [128, 2, 8, 128], BF16) for i in range(NR13)]
        r2 = [sb(f"r2_{i}", [128, 1024], BF16) for i in range(NR2)]
        QT_A = sb("QT_A", [128, 8, 512], BF16)
        QT_B = sb("QT_B", [128, 8, 512], BF16)
        NPT = 3
        PT = [sb(f"PT{i}", [128, 512], BF16) for i in range(NPT)]
        lnp = [sb(f"lnp{i}", [128, 2, D], F32) for i in range(2)]
        ropeA = sb("ropeA_t", [128, NKT, 64], F32)
        ropeB = sb("ropeB_t", [128, NKT, 32], F32)
        gq = sb("gq", [128, 64], F32)
        gk = sb("gk", [128, 64], F32)
        gcq = sb("gcq", [128, 256], F32)
        gckv = sb("gckv", [128, 128], F32)
        gout = sb("gout", [128, 1024], F32)
        identf = sb("identf", [128, 128], F32)
        identb = sb("identb", [128, 128], BF16)
        sel = sb("sel", [128, 96], BF16)
        epsL = sb("epsL", [128, 1], F32)
        epsR = sb("epsR", [128, 1], F32)
        winkv = sb("winkv", [128, 8, 416], BF16)
        wukp = sb("wukp", [128, 8, 96], BF16)
        wuv = sb("wuv", [128, 8, 64], BF16)
        sa = [sb(f"sa{i}", [128, 512], F32) for i in range(2)]
        OTs = [sb(f"OTs{i}", [128, 512], F32) for i in range(2)]
        sq = sb("sq", [128, 768], F32)
        nrm = sb("nrm", [128, 512], F32)
        rt = [sb(f"rt{i}", [128, 256], F32) for i in range(4)]
        rotb = sb("rotb", [128, 768], BF16)
        cnb = sb("cnb", [128, 256], BF16)
        ckvT = sb("ckvT", [128, 512], BF16)
        kpeT = sb("kpeT", [128, 512], BF16)
        cqT = sb("cqT", [128, 2, 512], BF16)
        stats = sb("stats", [128, 4, 2, 6], F32)
        mv = sb("mv", [128, 4, 2], F32)
        rstd = sb("rstd", [128, 4], F32)
        nmr = sb("nmr", [128, 4], F32)
        ss = sb("ss", [128, 16], F32)
        rs = sb("rs", [128, 16], F32)
        rcp = sb("rcp", [128, 4], F32)

        PS = [ps(f"ps{i}", [128, 1024], F32) for i in range(4)]

        def bank(i):
            return PS[i // 2][:, (i % 2) * 512:(i % 2) * 512 + 512]

        def bank_bf(i):
            return bank(i).bitcast(BF16)

        PJ = PS[3]
        pb = [Buf(f"bank{i}", psum=True) for i in range(8)]

        B = {}

        def bf(name):
            if name not in B:
                B[name] = Buf(name)
            return B[name]

        KA = [bf(f"KA{i}") for i in range(NKT + 1)]
        KB = [bf(f"KB{i}") for i in range(NKT + 1)]
        VA = [bf(f"VA{i}") for i in range(NKT + 1)]
        VB = [bf(f"VB{i}") for i in range(NKT + 1)]
        hcB = [bf(f"hc{i}") for i in range(4)]
        aTB = [bf(f"aT{i}") for i in range(4)]
        GB = [bf(f"G{i}") for i in range(24)]
        r13B = [bf(f"r13_{i}") for i in range(NR13)]
        r2B = [bf(f"r2_{i}") for i in range(NR2)]
        PTB = [bf(f"PT{i}") for i in range(NPT)]
        lnpB = [bf(f"lnp{i}") for i in range(2)]
        saB = [bf(f"sa{i}") for i in range(2)]
        OTsB = [bf(f"OTs{i}") for i in range(2)]
        h1sB = [bf(f"h1s{i}") for i in range(NKT)]
        cB = bf("consts")
        qaB, qbB = bf("QT_A"), bf("QT_B")
        cvA, cvB, cvC = bf("cvA"), bf("cvB"), bf("cvC")

        P.dma("sp", lambda e: e.dma_start(out=identf[:], in_=ident_d), "c0", writes=[cB], final=True)
        P.dma("sp", lambda e: e.dma_start(out=ropeA[:], in_=ropeA_d.rearrange("(t p) c -> p t c", p=128)), "c0", writes=[cB], final=True)
        P.dma("sp", lambda e: e.dma_start(out=ropeB[:], in_=ropeB_d.rearrange("(t p) c -> p t c", p=128)), "c0", writes=[cB], final=True)
        for tile_, src in ((gq, qn_d), (gk, kn_d), (gcq, cqn_d), (gckv, ckvn_d)):
            P.dma("sp", lambda e, tile_=tile_, src=src: e.dma_start(out=tile_[:], in_=src.partition_broadcast(128)), "c0", writes=[cB], final=True)
        P.dma("sp", lambda e: e.dma_start(out=gout[:, 0:512], in_=ona_d.partition_broadcast(128)), "c0", writes=[cB], final=True)
        P.dma("sp", lambda e: e.dma_start(out=gout[:, 512:1024], in_=onb_d.partition_broadcast(128)), "c0", writes=[cB], final=True)
        c2 = bf("consts2")
        P.op("dve", lambda e: e.tensor_copy(out=identb[:], in_=identf[:]), reads=[cB], writes=[c2])
        P.op("pool", lambda e: e.memset(sel[:], 0.0), writes=[c2])
        P.op("dve", lambda e: e.tensor_copy(out=sel[0:32, 64:96], in_=identf[0:32, 0:32]), reads=[cB, c2], writes=[c2])
        P.op("pool", lambda e: e.memset(kpeT[:], 0.0), writes=[bf("kpeT")])
        P.op("pool", lambda e: e.memset(epsL[:], LN_EPS), writes=[c2])
        P.op("pool", lambda e: e.memset(epsR[:], RMS_EPS), writes=[c2])
        P.op("pool", lambda e: e.memset(V_A[:], 1.0), writes=VA)
        P.op("pool", lambda e: e.memset(V_B[:], 1.0), writes=VB)
        P.op("pool", lambda e: e.memset(V_A[:, NKT], 0.0), writes=[VA[NKT]])
        P.op("pool", lambda e: e.memset(V_B[:, NKT], 0.0), writes=[VB[NKT]])
        P.op("pool", lambda e: e.memset(V_A[0:NMETA, NKT, :, 64:65], 1.0), writes=[VA[NKT]])
        P.op("pool", lambda e: e.memset(V_B[0:NMETA, NKT, :, 64:65], 1.0), writes=[VB[NKT]])
        P.op("pool", lambda e: e.memset(KT_A[:, SEQ_:SEQ_ + 128], 0.0), writes=[KA[NKT]])
        P.op("pool", lambda e: e.memset(KT_B[:, :, SEQ_:SEQ_ + 128], 0.0), writes=[KB[NKT]])
        P.op("pool", lambda e: e.memset(QT_A[:], 0.0), writes=[bf("QT_A")])
        P.op("pool", lambda e: e.memset(wukp[:], 0.0), writes=[cvA])
        P.dma("pool", lambda e: e.dma_start(out=winkv[:, :, 0:256], in_=w_in_d[:, 512:768].rearrange("(kc p) n -> p kc n", p=128)), "cvA", writes=[cvA], final=True)
        P.dma("pool", lambda e: e.dma_start(out=winkv[:, :, 256:416], in_=w_in_d[:, 1024:1184].rearrange("(kc p) n -> p kc n", p=128)), "cvA", writes=[cvA], final=True)
        ukv4 = w_ukv_d.rearrange("p (h two d) -> p h two d", h=8, two=2)
        P.dma("pool", lambda e: e.dma_start(out=wukp[:, :, 0:64], in_=ukv4[:, :, 0, :]), "cvA", writes=[cvA], final=True)
        P.dma("pool", lambda e: e.dma_start(out=wuv[:], in_=ukv4[:, :, 1, :]), "cvA", writes=[cvA], final=True)

        def conv_ffn(f, semname, buf):
            w1, w3, w2 = fw[f]
            for fc in range(nfc):
                for m, w in enumerate((w1, w3)):
                    src = w.rearrange("(kc p) (fc j) -> fc p kc j", p=128, j=128)[fc]
                    P.dma("pool", lambda e, src=src, fc=fc, m=m: e.dma_start(out=w13s[f][fc, :, m, :, :], in_=src),
                          semname, writes=[buf], final=True)
            for fc in range(nfc):
                for half in range(2):
                    P.dma("pool", lambda e, fc=fc, half=half: e.dma_start(out=w2s[f][half, fc // 2, :, fc % 2, :], in_=w2[fc * 128:(fc + 1) * 128, half * 512:(half + 1) * 512]),
                          semname, writes=[buf], final=True)

        conv_ffn(1, "cvA", cvA)
        for kc in range(8):
            P.dma("pool", lambda e, kc=kc: e.dma_start(out=winq_s[kc * 128:(kc + 1) * 128, 0:512], in_=w_in_d[kc * 128:(kc + 1) * 128, 0:512]), "cvB", writes=[cvB], final=True)
            P.dma("pool", lambda e, kc=kc: e.dma_start(out=winq_s[kc * 128:(kc + 1) * 128, 512:768], in_=w_in_d[kc * 128:(kc + 1) * 128, 768:1024]), "cvB", writes=[cvB], final=True)
        for kc in range(8):
            P.dma("pool", lambda e, kc=kc: e.dma_start(out=wout_s[kc * 128:(kc + 1) * 128, :], in_=w_out_d[kc * 128:(kc + 1) * 128, :]), "cvB", writes=[cvB], final=True)
        conv_ffn(2, "cvC", cvC)

        class Stream:
            def __init__(self, slots, bufs, semprefix):
                self.slots, self.bufs, self.pref = slots, bufs, semprefix
                self.items = []
                self.issued = 0
                self.taken = 0

            def _issue(self):
                i = self.issued
                fn, rd = self.items[i]
                s = i % len(self.slots)
                P.dma("sp", lambda e, fn=fn, s=s: fn(e, self.slots[s]), f"{self.pref}{s}", reads=[rd], writes=[self.bufs[s]])
                self.issued += 1

            def take(self):
                i = self.taken
                while self.issued < min(len(self.items), i + len(self.slots)):
                    self._issue()
                self.taken += 1
                s = i % len(self.slots)
                return self.slots[s], self.bufs[s]

        S13 = Stream(r13, r13B, "r13_")
        S2 = Stream(r2, r2B, "r2_")
        cvbuf = {1: cvA, 2: cvC}

        def sched_ffn(f):
            for fc in range(nfc):
                S13.items.append((lambda e, slot, fc=fc, f=f: e.dma_start(out=slot[:], in_=w13s[f][fc]), cvbuf[f]))
            for half in range(2):
                for fp in range(nfc // 2):
                    src = w2s[f][half, fp].rearrange("p a n -> p (a n)")
                    S2.items.append((lambda e, slot, src=src: e.dma_start(out=slot[:], in_=src), cvbuf[f]))

        def sched_mixB_q():
            for tp in range(2):
                for kc in range(8):
                    S2.items.append((lambda e, slot, kc=kc: e.dma_start(out=slot[:, 0:768], in_=winq_s[kc * 128:(kc + 1) * 128, :]), cvB))

        def sched_mixB_o():
            for tp in range(2):
                for kc in range(8):
                    S2.items.append((lambda e, slot, kc=kc: e.dma_start(out=slot[:], in_=wout_s[kc * 128:(kc + 1) * 128, :]), cvB))

        sched_ffn(1)
        for s in range(nseq):
            for c in range(nch):
                sched_ffn(1)
            for c in range(nch):
                sched_mixB_q()
                sched_mixB_o()
                sched_ffn(2)

        ln_cur = [None, None]

        def ensure_ln(i, slot):
            if ln_cur[slot] == i:
                return
            ln_cur[slot] = i
            g_d, b_d = ln_d[i]
            P.dma("sp", lambda e: e.dma_start(out=lnp[slot][:, 0, :], in_=g_d.partition_broadcast(128)), f"lnp{slot}", writes=[lnpB[slot]])
            P.dma("sp", lambda e: e.dma_start(out=lnp[slot][:, 1, :], in_=b_d.partition_broadcast(128)), f"lnp{slot}", writes=[lnpB[slot]])

        evac_rr = [0]

        def evac(out, in_, reads, writes):
            evac_rr[0] ^= 1
            if evac_rr[0]:
                P.op("dve", lambda e: e.tensor_copy(out=out, in_=in_), reads=reads, writes=writes)
            else:
                P.op("act", lambda e: e.activation(out=out, in_=in_, func=AF.Copy), reads=reads, writes=writes)

        def transposes_to_aT(nt, R, tiles=None):
            for t in (range(nt) if tiles is None else tiles):
                for kc in range(8):
                    P.op("pe", lambda e, t=t, kc=kc: e.transpose(out=PJ[:, kc * 128:kc * 128 + R], in_=hc[0:R, t, kc * 128:(kc + 1) * 128], identity=identf[0:R, 0:R]),
                         reads=[hcB[t], cB], writes=[pb[6 + kc // 4]])
                pj3 = PJ[:].rearrange("p (k c) -> p k c", k=8)
                evac(aT[:, 0:4, t * 128:t * 128 + R], pj3[:, 0:4, 0:R], [pb[6]], [aTB[t]])
                evac(aT[:, 4:8, t * 128:t * 128 + R], pj3[:, 4:8, 0:R], [pb[7]], [aTB[t]])

        def ffn(f, T, R, nt, ln_fused=False):
            for t in range(nt):
                P.op("act", lambda e, t=t: e.activation(out=hc[0:R, t, :], in_=hc[0:R, t, :], func=AF.Copy, scale=ALPHA),
                     reads=[hcB[t]], writes=[hcB[t]])
            for fc in range(nfc):
                slot, sbuf_ = S13.take()
                ua, ub = (0, 1) if fc % 2 == 0 else (2, 3)
                for m, bk in ((0, ua), (1, ub)):
                    for kc in range(8):
                        P.op("pe", lambda e, slot=slot, m=m, bk=bk, kc=kc: e.matmul(bank(bk)[:, 0:T], lhsT=slot[:, m, kc, :], rhs=aT[:, kc, 0:T], start=(kc == 0), stop=(kc == 7)),
                             reads=[sbuf_] + aTB[0:nt], writes=[pb[bk]])
                si = fc % 2
                P.op("act", lambda e, si=si, ua=ua: e.activation(out=sa[si][:, 0:T], in_=bank(ua)[:, 0:T], func=AF.Silu), reads=[pb[ua]], writes=[saB[si]])
                P.op("dve", lambda e, si=si, ub=ub, fc=fc: e.tensor_tensor(out=G[:, fc, 0:T], in0=sa[si][:, 0:T], in1=bank(ub)[:, 0:T], op=ALU.mult),
                     reads=[saB[si], pb[ub]], writes=[GB[fc]])
            for half in range(2):
                for fp in range(nfc // 2):
                    slot, sbuf_ = S2.take()
                    for a in range(2):
                        fc = fp * 2 + a
                        for t in range(nt):
                            P.op("pe", lambda e, slot=slot, a=a, fc=fc, t=t: e.matmul(bank(4 + t)[0:R, :], lhsT=G[:, fc, t * 128:t * 128 + R], rhs=slot[:, a * 512:(a + 1) * 512], start=(fc == 0), stop=(fc == nfc - 1)),
                                 reads=[sbuf_, GB[fc]], writes=[pb[4 + t]])
                deferred = []
                for t in range(nt):
                    def ev(t=t, half=half):
                        P.op("dve", lambda e: e.scalar_tensor_tensor(out=hc[0:R, t, half * 512:(half + 1) * 512], in0=bank(4 + t)[0:R, :], scalar=0.5, in1=hc[0:R, t, half * 512:(half + 1) * 512], op0=ALU.mult, op1=ALU.add),
                             reads=[pb[4 + t], hcB[t]], writes=[hcB[t]])
                    if ln_fused and half == 1:
                        deferred.append(ev)
                    else:
                        ev()
                        if ln_fused:
                            P.op("dve", lambda e, t=t: e.bn_stats(out=stats[0:R, t, 0, :], in_=hc[0:R, t, 0:512]), reads=[hcB[t]], writes=[stB[t]])
            return deferred

        sB = {n: bf(n) for n in ("stats", "mv", "rstd", "nmr", "sq", "sq2", "ss", "ss2", "rs", "nrm", "rt0", "rt1", "rt2", "rt3", "rotb", "cnb", "ckvT", "kpeT", "cqT", "rcp")}

        def layernorm(slot, R, nt):
            for t in range(nt):
                for h in range(2):
                    P.op("dve", lambda e, t=t, h=h: e.bn_stats(out=stats[0:R, t, h, :], in_=hc[0:R, t, h * 512:(h + 1) * 512]), reads=[hcB[t]], writes=[sB["stats"]])
                P.op("dve", lambda e, t=t: e.bn_aggr(out=mv[0:R, t, :], in_=stats[0:R, t].rearrange("p a b -> p (a b)")), reads=[sB["stats"]], writes=[sB["mv"]])
            P.op("act", lambda e: e.activation(out=rstd[0:R, 0:nt], in_=mv[0:R, 0:nt, 1], func=AF.Sqrt, bias=epsL[0:R, :], scale=1.0), reads=[sB["mv"], c2], writes=[sB["rstd"]])
            P.op("dve", lambda e: e.reciprocal(out=rstd[0:R, 0:nt], in_=rstd[0:R, 0:nt]), reads=[sB["rstd"]], writes=[sB["rstd"]])
            P.op("dve", lambda e: e.scalar_tensor_tensor(out=nmr[0:R, 0:nt], in0=mv[0:R, 0:nt, 0], scalar=-1.0, in1=rstd[0:R, 0:nt], op0=ALU.mult, op1=ALU.mult),
                 reads=[sB["mv"], sB["rstd"]], writes=[sB["nmr"]])
            for t in range(nt):
                P.op("act", lambda e, t=t: e.activation(out=hc[0:R, t, :], in_=hc[0:R, t, :], func=AF.Identity, scale=rstd[0:R, t:t + 1], bias=nmr[0:R, t:t + 1]),
                     reads=[hcB[t], sB["rstd"], sB["nmr"]], writes=[hcB[t]])
                P.op("dve", lambda e, t=t: e.tensor_tensor(out=hc[0:R, t, :], in0=hc[0:R, t, :], in1=lnp[slot][0:R, 0, :], op=ALU.mult), reads=[hcB[t], lnpB[slot]], writes=[hcB[t]])
                P.op("dve", lambda e, t=t: e.tensor_tensor(out=hc[0:R, t, :], in0=hc[0:R, t, :], in1=lnp[slot][0:R, 1, :], op=ALU.add), reads=[hcB[t], lnpB[slot]], writes=[hcB[t]])

        stB = [bf(f"st{i}") for i in range(4)]
        mvB = [bf(f"mv{i}") for i in range(4)]
        rsB = [bf(f"rsd{i}") for i in range(4)]

        def ln_pipeline(slot, R, nt, pre=None, post=None, have_h0=False):
            def S1(t):
                if pre:
                    pre[t]()
                for h in ((1,) if have_h0 else (0, 1)):
                    P.op("dve", lambda e, h=h: e.bn_stats(out=stats[0:R, t, h, :], in_=hc[0:R, t, h * 512:(h + 1) * 512]), reads=[hcB[t]], writes=[stB[t]])
                P.op("dve", lambda e: e.bn_aggr(out=mv[0:R, t, :], in_=stats[0:R, t].rearrange("p a b -> p (a b)")), reads=[stB[t]], writes=[mvB[t]])
                P.op("act", lambda e: e.activation(out=rstd[0:R, t:t + 1], in_=mv[0:R, t, 1:2], func=AF.Sqrt, bias=epsL[0:R, :], scale=1.0), reads=[mvB[t], c2], writes=[rsB[t]])

            def S2(t):
                P.op("dve", lambda e: e.reciprocal(out=rstd[0:R, t:t + 1], in_=rstd[0:R, t:t + 1]), reads=[rsB[t]], writes=[rsB[t]])
                P.op("dve", lambda e: e.scalar_tensor_tensor(out=nmr[0:R, t:t + 1], in0=mv[0:R, t, 0:1], scalar=-1.0, in1=rstd[0:R, t:t + 1], op0=ALU.mult, op1=ALU.mult),
                     reads=[mvB[t], rsB[t]], writes=[rsB[t]])
                P.op("act", lambda e: e.activation(out=hc[0:R, t, :], in_=hc[0:R, t, :], func=AF.Identity, scale=rstd[0:R, t:t + 1], bias=nmr[0:R, t:t + 1]),
                     reads=[hcB[t], rsB[t]], writes=[hcB[t]])

            def S3(t):
                P.op("dve", lambda e: e.tensor_tensor(out=hc[0:R, t, :], in0=hc[0:R, t, :], in1=lnp[slot][0:R, 0, :], op=ALU.mult), reads=[hcB[t], lnpB[slot]], writes=[hcB[t]])
                P.op("dve", lambda e: e.tensor_tensor(out=hc[0:R, t, :], in0=hc[0:R, t, :], in1=lnp[slot][0:R, 1, :], op=ALU.add), reads=[hcB[t], lnpB[slot]], writes=[hcB[t]])

            for i in range(nt + 3):
                for k, st in enumerate((S1, S2, S3, post)):
                    t = i - k
                    if st is not None and 0 <= t < nt:
                        st(t)

        def rms_rstd(R, n, inv_n):
            P.op("act", lambda e: e.activation(out=rs[0:R, 0:n], in_=ss[0:R, 0:n], func=AF.Sqrt, bias=epsR[0:R, :], scale=inv_n), reads=[sB["ss"], sB["ss2"], c2], writes=[sB["rs"]])
            P.op("dve", lambda e: e.reciprocal(out=rs[0:R, 0:n], in_=rs[0:R, 0:n]), reads=[sB["rs"]], writes=[sB["rs"]])

        def passA(src_fn, T, R, nt, kbase, rope_t0, store_h1):
            kcol0 = kbase * 128
            for t in range(nt):
                P.dma("sp", lambda e, t=t: e.dma_start(out=hc[0:R, t, :], in_=src_fn(t)), f"ldhc{t}", writes=[hcB[t]])
            transposes_to_aT(nt, R)
            ensure_ln(1, 0)
            dfr = ffn(1, T, R, nt, ln_fused=True)

            def postA(t):
                if store_h1:
                    P.dma("sp", lambda e: e.dma_start(out=h1_s[(kbase + t) * 128:(kbase + t + 1) * 128, :], in_=hc[:, t, :]), f"sthc{t}", reads=[hcB[t]], writes=[h1sB[kbase + t]])
                transposes_to_aT(nt, R, tiles=[t])
            ln_pipeline(0, R, nt, pre=dfr, post=postA, have_h0=True)
            for t in range(nt):
                kt = kbase + t
                for kc in range(8):
                    P.op("pe", lambda e, t=t, kc=kc: e.matmul(PJ[0:R, 0:416], lhsT=aT[:, kc, t * 128:t * 128 + R], rhs=winkv[:, kc, :], start=(kc == 0), stop=(kc == 7)),
                         reads=[aTB[t], cvA], writes=[pb[6]])
                P.op("act", lambda e: e.activation(out=sq[0:R, 0:128], in_=PJ[0:R, 0:128], func=AF.Square), reads=[pb[6]], writes=[sB["sq"]])
                P.op("act", lambda e: e.activation(out=sq[0:R, 256:384], in_=PJ[0:R, 256:384], func=AF.Square, scale=0.5 ** 0.5, accum_out=ss[0:R, 2:3]), reads=[pb[6]], writes=[sB["sq2"], sB["ss2"]])
                P.op("dve", lambda e: e.tensor_reduce(out=ss[0:R, 0:2], in_=sq[0:R, 0:128].rearrange("p (h d) -> p h d", d=64), axis=AX.X, op=ALU.add), reads=[sB["sq"]], writes=[sB["ss"]])
                rms_rstd(R, 3, 1.0 / 64)
                for h in range(2):
                    P.op("dve", lambda e, h=h: e.scalar_tensor_tensor(out=nrm[0:R, h * 64:(h + 1) * 64], in0=PJ[0:R, h * 64:(h + 1) * 64], scalar=rs[0:R, h:h + 1], in1=gk[0:R, :], op0=ALU.mult, op1=ALU.mult),
                         reads=[pb[6], sB["rs"], cB], writes=[sB["nrm"]])
                s3 = nrm[0:R, 0:128].rearrange("p (h d) -> p h d", h=2)
                d3 = rotb[0:R, 0:128].rearrange("p (h d) -> p h d", h=2)
                if rope_t0 is None:
                    P.op("dve", lambda e, s3=s3, d3=d3: e.tensor_copy(out=d3, in_=s3), reads=[sB["nrm"]], writes=[sB["rotb"]])
                else:
                    tab = ropeA[0:R, rope_t0 + t, :]
                    rope4(s3[:, :, 0:32], s3[:, :, 32:64], tab[:, 0:32].unsqueeze(1).broadcast_to([R, 2, 32]), tab[:, 32:64].unsqueeze(1).broadcast_to([R, 2, 32]),
                          d3[:, :, 0:32], d3[:, :, 32:64], lambda i: rt[i][0:R, 0:64].rearrange("p (h d) -> p h d", h=2), [sB["nrm"]])
                P.op("pe", lambda e: e.transpose(out=bank_bf(5)[:, 0:R], in_=rotb[0:R, 0:128], identity=identb[0:R, 0:R]), reads=[sB["rotb"], c2], writes=[pb[5]])
                evac(KT_A[:, kt * 128:kt * 128 + R], bank_bf(5)[:, 0:R], [pb[5]], [KA[kt]])
                P.op("act", lambda e, kt=kt: e.activation(out=V_A[0:R, kt, :, 0:64], in_=PJ[0:R, 128:256].rearrange("p (h d) -> p h d", h=2), func=AF.Copy), reads=[pb[6]], writes=[VA[kt]])
                P.op("dve", lambda e: e.scalar_tensor_tensor(out=cnb[0:R, 0:128], in0=PJ[0:R, 256:384], scalar=rs[0:R, 2:3], in1=gckv[0:R, :], op0=ALU.mult, op1=ALU.mult),
                     reads=[pb[6], sB["rs"], cB], writes=[sB["cnb"]])
                P.op("pe", lambda e: e.transpose(out=bank_bf(5)[:, 128:128 + R], in_=cnb[0:R, 0:128], identity=identb[0:R, 0:R]), reads=[sB["cnb"], c2], writes=[pb[5]])
                evac(ckvT[:, t * 128:t * 128 + R], bank_bf(5)[:, 128:128 + R], [pb[5]], [sB["ckvT"]])
                P.op("dve", lambda e: e.tensor_copy(out=nrm[0:R, 128:160], in_=PJ[0:R, 384:416]), reads=[pb[6]], writes=[sB["nrm"]])
                if rope_t0 is None:
                    P.op("dve", lambda e: e.tensor_copy(out=rotb[0:R, 128:160], in_=nrm[0:R, 128:160]), reads=[sB["nrm"]], writes=[sB["rotb"]])
                else:
                    tab = ropeB[0:R, rope_t0 + t, :]
                    rope4(nrm[0:R, 128:144], nrm[0:R, 144:160], tab[:, 0:16], tab[:, 16:32], rotb[0:R, 128:144], rotb[0:R, 144:160],
                          lambda i: rt[i][0:R, 0:16], [sB["nrm"]])
                P.op("pe", lambda e: e.transpose(out=bank_bf(5)[0:32, 256:256 + R], in_=rotb[0:R, 128:160], identity=identb[0:R, 0:R]), reads=[sB["rotb"], c2], writes=[pb[5]])
                evac(kpeT[0:32, t * 128:t * 128 + R], bank_bf(5)[0:32, 256:256 + R], [pb[5]], [sB["kpeT"]])
            for h in range(8):
                bk = 4 + (h % 2)
                P.op("pe", lambda e, h=h, bk=bk: e.matmul(bank(bk)[0:96, 0:T], lhsT=wukp[:, h, :], rhs=ckvT[:, 0:T], start=True, stop=False), reads=[cvA, sB["ckvT"]], writes=[pb[bk]])
                P.op("pe", lambda e, h=h, bk=bk: e.matmul(bank(bk)[0:96, 0:T], lhsT=sel[:, :], rhs=kpeT[:, 0:T], start=False, stop=True), reads=[c2, sB["kpeT"]], writes=[pb[bk]])
                evac(KT_B[0:96, h, kcol0:kcol0 + T], bank(bk)[0:96, 0:T], [pb[bk]], KB[kbase:kbase + nt])
            for t in range(nt):
                kt = kbase + t
                P.op("pe", lambda e, t=t: e.matmul(PJ[0:R, 0:512], lhsT=ckvT[:, t * 128:t * 128 + R], rhs=wuv[:].rearrange("p h d -> p (h d)"), start=True, stop=True), reads=[sB["ckvT"], cvA], writes=[pb[6]])
                evac(V_B[0:R, kt, :, 0:64], PJ[0:R, 0:512].rearrange("p (h d) -> p h d", h=8), [pb[6]], [VB[kt]])

        def attention_head(QT_ap, KT_fn, V_fn, Kbufs, Vbufs, dk, scale, hb, ocol, qbufs, prev_tail):
            otb = 4 + hb
            NKC = NKT + 1
            LOOK = 2

            def st_mm(kc):
                bk = kc % 4
                P.op("pe", lambda e: e.matmul(bank(bk)[:, :], lhsT=KT_fn(kc), rhs=QT_ap, start=True, stop=True), reads=[Kbufs[kc]] + qbufs, writes=[pb[bk]])

            def exp_pv(kc):
                bk = kc % 4
                pi = kc % NPT
                P.op("act", lambda e: e.activation(out=PT[pi][:, :], in_=bank(bk)[:, :], func=AF.Exp, scale=scale), reads=[pb[bk]], writes=[PTB[pi]])
                P.op("pe", lambda e: e.matmul(bank(otb)[0:65, :], lhsT=V_fn(kc), rhs=PT[pi][:, :], start=(kc == 0), stop=(kc == NKC - 1)), reads=[Vbufs[kc], PTB[pi]], writes=[pb[otb]])

            for kc in range(min(LOOK, NKC)):
                st_mm(kc)
            for kc in range(NKC):
                if kc + LOOK < NKC:
                    st_mm(kc + LOOK)
                exp_pv(kc)
                if kc == 2 and prev_tail is not None:
                    prev_tail()

            def tail():
                P.op("dve", lambda e: e.tensor_copy(out=OTs[hb][0:65, :], in_=bank(otb)[0:65, :]), reads=[pb[otb]], writes=[OTsB[hb]])
                for t in range(4):
                    P.op("pe", lambda e, t=t: e.transpose(out=bank(6)[:, t * 128:t * 128 + 65], in_=OTs[hb][0:65, t * 128:(t + 1) * 128], identity=identf[0:65, 0:65]), reads=[OTsB[hb], cB], writes=[pb[6]])
                po = bank(6).rearrange("p (t c) -> p t c", t=4)
                P.op("dve", lambda e: e.reciprocal(out=rcp[:, 0:4], in_=po[:, :, 64]), reads=[pb[6]], writes=[sB["rcp"]])
                for t in range(4):
                    P.op("dve", lambda e, t=t: e.tensor_scalar(out=o_tm[:, t, ocol:ocol + 64], in0=po[:, t, 0:64], scalar1=rcp[:, t:t + 1], scalar2=None, op0=ALU.mult),
                         reads=[pb[6], sB["rcp"]], writes=GB[4 * t:4 * t + 4])
            return tail

        wuq = sb("wuq", [128, 2, 768], BF16)
        P.dma("pool", lambda e: e.dma_start(out=wuq[:], in_=w_uq_d.rearrange("(kc p) n -> p kc n", p=128)), "cvB", writes=[cvB], final=True)

        def passB(s, c):
            q0 = c * 512
            for t in range(4):
                P.dma("sp", lambda e, t=t: e.dma_start(out=hc[:, t, :], in_=h1_s[q0 + t * 128:q0 + (t + 1) * 128, :]), f"ldhc{t}", reads=[h1sB[c * 4 + t]], writes=[hcB[t]])
            transposes_to_aT(4, 128)
            for t in range(4):
                P.op("act", lambda e, t=t: e.activation(out=hc[:, t, :], in_=hc[:, t, :], func=AF.Copy, scale=ALPHA), reads=[hcB[t]], writes=[hcB[t]])
            if stop < 3.1:
                return
            for tp in range(2):
                for kc in range(8):
                    slot, sbuf_ = S2.take()
                    for j in range(2):
                        t = tp * 2 + j
                        P.op("pe", lambda e, slot=slot, t=t, j=j, kc=kc: e.matmul(PS[j][:, 0:512], lhsT=aT[:, kc, t * 128:(t + 1) * 128], rhs=slot[:, 0:512], start=(kc == 0), stop=(kc == 7)),
                             reads=[sbuf_, aTB[t]], writes=[pb[2 * j]])
                        P.op("pe", lambda e, slot=slot, t=t, j=j, kc=kc: e.matmul(PS[j][:, 512:768], lhsT=aT[:, kc, t * 128:(t + 1) * 128], rhs=slot[:, 512:768], start=(kc == 0), stop=(kc == 7)),
                             reads=[sbuf_, aTB[t]], writes=[pb[2 * j + 1]])
                if stop < 3.11:
                    continue
                for j in range(2):
                    t = tp * 2 + j
                    pq = PS[j]
                    pbs = [pb[2 * j], pb[2 * j + 1]]
                    P.op("act", lambda e, pq=pq: e.activation(out=sq[:, 0:512], in_=pq[:, 0:512], func=AF.Square), reads=[pbs[0]], writes=[sB["sq"]])
                    P.op("act", lambda e, pq=pq: e.activation(out=sq[:, 512:768], in_=pq[:, 512:768], func=AF.Square, scale=0.5, accum_out=ss[:, 8:9]), reads=[pbs[1]], writes=[sB["sq2"], sB["ss2"]])
                    P.op("dve", lambda e: e.tensor_reduce(out=ss[:, 0:8], in_=sq[:, 0:512].rearrange("p (h d) -> p h d", d=64), axis=AX.X, op=ALU.add), reads=[sB["sq"]], writes=[sB["ss"]])
                    if stop < 3.12:
                        continue
                    rms_rstd(128, 9, 1.0 / 64)
                    P.op("dve", lambda e, pq=pq: e.tensor_tensor(out=nrm[:, 0:512].rearrange("p (h d) -> p h d", h=8), in0=pq[:, 0:512].rearrange("p (h d) -> p h d", h=8),
                                                                 in1=rs[:, 0:8].unsqueeze(2).broadcast_to([128, 8, 64]), op=ALU.mult), reads=[pbs[0], sB["rs"]], writes=[sB["nrm"]])
                    P.op("dve", lambda e: e.tensor_tensor(out=nrm[:, 0:512].rearrange("p (h d) -> p h d", h=8), in0=nrm[:, 0:512].rearrange("p (h d) -> p h d", h=8),
                                                          in1=gq[:, :].unsqueeze(1).broadcast_to([128, 8, 64]), op=ALU.mult), reads=[sB["nrm"], cB], writes=[sB["nrm"]])
                    if stop < 3.13:
                        continue
                    src4 = nrm[:, 0:512].rearrange("p (j q d) -> p j q d", j=2, q=4)
                    dst4 = rotb[:, 0:512].rearrange("p (q j d) -> p j q d", q=4, j=2)
                    tab = ropeA[:, c * 4 + t, :]
                    cs = tab[:, 0:32].unsqueeze(1).unsqueeze(1).broadcast_to([128, 2, 4, 32])
                    sn = tab[:, 32:64].unsqueeze(1).unsqueeze(1).broadcast_to([128, 2, 4, 32])
                    rope4(src4[:, :, :, 0:32], src4[:, :, :, 32:64], cs, sn, dst4[:, :, :, 0:32], dst4[:, :, :, 32:64],
                          lambda i: rt[i][:, 0:256].rearrange("p (j q d) -> p j q d", j=2, q=4), [sB["nrm"]])
                    if stop < 3.14:
                        continue
                    for p4 in range(4):
                        P.op("pe", lambda e, p4=p4: e.transpose(out=bank_bf(5)[:, p4 * 128:(p4 + 1) * 128], in_=rotb[:, p4 * 128:(p4 + 1) * 128], identity=identb[:, :]), reads=[sB["rotb"], c2], writes=[pb[5]])
                    evac(QT_A[0:64, 0:4, t * 128:(t + 1) * 128], bank_bf(5)[0:64, 0:512].rearrange("p (q n) -> p q n", q=4), [pb[5]], [qaB])
                    evac(QT_A[64:128, 4:8, t * 128:(t + 1) * 128], bank_bf(5)[64:128, 0:512].rearrange("p (q n) -> p q n", q=4), [pb[5]], [qaB])
                    if stop < 3.15:
                        continue
                    P.op("dve", lambda e, pq=pq: e.scalar_tensor_tensor(out=cnb[:, 0:256], in0=pq[:, 512:768], scalar=rs[:, 8:9], in1=gcq[:, :], op0=ALU.mult, op1=ALU.mult),
                         reads=[pbs[1], sB["rs"], cB], writes=[sB["cnb"]])
                    if stop < 3.16:
                        continue
                    for k2 in range(2):
                        P.op("pe", lambda e, k2=k2: e.transpose(out=bank_bf(5)[:, 512 + k2 * 128:512 + (k2 + 1) * 128], in_=cnb[:, k2 * 128:(k2 + 1) * 128], identity=identb[:, :]), reads=[sB["cnb"], c2], writes=[pb[5]])
                    if stop < 3.17:
                        continue
                    if stop == 3.19:
                        P.op("dve", lambda e, t=t: e.tensor_copy(out=ckvT[:, t * 128:(t + 1) * 128], in_=cnb[:, 0:128]), reads=[sB["cnb"]], writes=[sB["ckvT"]])
                        continue
                    if stop == 3.18:
                        P.op("dve", lambda e, t=t: e.tensor_copy(out=cqT[:, 0, t * 128:(t + 1) * 128], in_=cnb[:, 0:128]), reads=[sB["cnb"]], writes=[sB["cqT"]])
                        continue
                    for k2 in range(2):
                        P.op("dve", lambda e, t=t, k2=k2: e.tensor_copy(out=cqT[:, k2, t * 128:(t + 1) * 128], in_=bank_bf(5)[:, 512 + k2 * 128:512 + (k2 + 1) * 128]), reads=[pb[5]], writes=[sB["cqT"]])
            if stop < 3.2:
                return
            for t in range(4):
                for (lo, hi, bk) in ((0, 512, 6), (512, 768, 7)):
                    for k2 in range(2):
                        P.op("pe", lambda e, t=t, lo=lo, hi=hi, k2=k2: e.matmul(PJ[:, lo:hi], lhsT=cqT[:, k2, t * 128:(t + 1) * 128], rhs=wuq[:, k2, lo:hi], start=(k2 == 0), stop=(k2 == 1)),
                             reads=[sB["cqT"], cvB], writes=[pb[bk]])
                if stop < 3.21:
                    continue
                P.op("act", lambda e: e.activation(out=sq[:, 0:768], in_=PJ[:, 0:768], func=AF.Copy), reads=[pb[6], pb[7]], writes=[sB["sq"], sB["sq2"]])
                qb3 = sq[:, 0:768].rearrange("p (h d) -> p h d", h=8)
                rb3 = rotb[:, 0:768].rearrange("p (h d) -> p h d", h=8)
                P.op("dve", lambda e, qb3=qb3, rb3=rb3: e.tensor_copy(out=rb3[:, :, 0:64], in_=qb3[:, :, 0:64]), reads=[sB["sq"], sB["sq2"]], writes=[sB["rotb"]])
                if stop < 3.22:
                    continue
                tab = ropeB[:, c * 4 + t, :]
                cs = tab[:, 0:16].unsqueeze(1).broadcast_to([128, 8, 16])
                sn = tab[:, 16:32].unsqueeze(1).broadcast_to([128, 8, 16])
                rope4(qb3[:, :, 64:80], qb3[:, :, 80:96], cs, sn, rb3[:, :, 64:80], rb3[:, :, 80:96],
                      lambda i: rt[i][:, 0:128].rearrange("p (h d) -> p h d", h=8), [sB["sq"], sB["sq2"]], pool_ok=False)
                if stop < 3.23:
                    continue
                for h in range(8):
                    P.op("pe", lambda e, h=h: e.transpose(out=bank_bf(5)[0:96, h * 128:(h + 1) * 128], in_=rotb[:, h * 96:(h + 1) * 96], identity=identb[:, :]), reads=[sB["rotb"], c2], writes=[pb[5]])
                if stop < 3.24:
                    continue
                evac(QT_B[0:96, :, t * 128:(t + 1) * 128], bank_bf(5)[0:96, :].rearrange("p (h n) -> p h n", h=8), [pb[5]], [qbB])
            if stop < 3.3:
                return
            hbc = [0]
            tail = None
            for h in range(8):
                tail = attention_head(QT_A[:, h, :],
                                      lambda kc: KT_A[:, kc * 128:(kc + 1) * 128],
                                      lambda kc, g=h // 4: V_A[:, kc, g, :], KA, VA, 64, SC_A, hbc[0], h * 64, [qaB], tail)
                hbc[0] ^= 1
            for h in range(8):
                tail = attention_head(QT_B[0:96, h, :],
                                      lambda kc, h=h: KT_B[0:96, h, kc * 128:(kc + 1) * 128],
                                      lambda kc, h=h: V_B[:, kc, h, :], KB, VB, 96, SC_B, hbc[0], 512 + h * 64, [qbB], tail)
                hbc[0] ^= 1
            tail()
            if stop < 3.4:
                return
            for t in range(4):
                gbt = GB[4 * t:4 * t + 4]
                P.op("act", lambda e, t=t: e.activation(out=sq[:, 0:512], in_=o_tm[:, t, 0:512], func=AF.Square, accum_out=ss[:, 2 * t:2 * t + 1]), reads=gbt, writes=[sB["sq"], sB["ss"]])
                P.op("act", lambda e, t=t: e.activation(out=nrm[:, 0:512], in_=o_tm[:, t, 512:1024], func=AF.Square, accum_out=ss[:, 2 * t + 1:2 * t + 2]), reads=gbt, writes=[sB["nrm"], sB["ss"]])
            rms_rstd(128, 8, 1.0 / 512)
            for t in range(4):
                gbt = GB[4 * t:4 * t + 4]
                for g in range(2):
                    P.op("dve", lambda e, t=t, g=g: e.scalar_tensor_tensor(out=on_tm[:, t, g * 512:(g + 1) * 512], in0=o_tm[:, t, g * 512:(g + 1) * 512], scalar=rs[:, 2 * t + g:2 * t + g + 1], in1=gout[:, g * 512:(g + 1) * 512], op0=ALU.mult, op1=ALU.mult),
                         reads=gbt + [sB["rs"], cB], writes=GB[16 + 2 * t:18 + 2 * t])
                for kc in range(8):
                    P.op("pe", lambda e, t=t, kc=kc: e.transpose(out=bank_bf(5 + (t % 2))[:, kc * 128:(kc + 1) * 128], in_=on_tm[:, t, kc * 128:(kc + 1) * 128], identity=identb[:, :]), reads=GB[16 + 2 * t:18 + 2 * t] + [c2], writes=[pb[5 + (t % 2)]])
                evac(aT[:, :, t * 128:(t + 1) * 128], bank_bf(5 + (t % 2))[:, :].rearrange("p (k n) -> p k n", k=8), [pb[5 + (t % 2)]], [aTB[t]])
            if stop < 3.5:
                return
            for tp in range(2):
                for kc in range(8):
                    slot, sbuf_ = S2.take()
                    for j in range(2):
                        t = tp * 2 + j
                        for half in range(2):
                            P.op("pe", lambda e, slot=slot, t=t, j=j, kc=kc, half=half: e.matmul(PS[j][:, half * 512:(half + 1) * 512], lhsT=aT[:, kc, t * 128:(t + 1) * 128], rhs=slot[:, half * 512:(half + 1) * 512], start=(kc == 0), stop=(kc == 7)),
                                 reads=[sbuf_, aTB[t]], writes=[pb[2 * j + half]])
                for j in range(2):
                    t = tp * 2 + j
                    for half in range(2):
                        P.op("dve", lambda e, t=t, j=j, half=half: e.tensor_tensor(out=hc[:, t, half * 512:(half + 1) * 512], in0=PS[j][:, half * 512:(half + 1) * 512], in1=hc[:, t, half * 512:(half + 1) * 512], op=ALU.add),
                             reads=[pb[2 * j + half], hcB[t]], writes=[hcB[t]])
            if stop < 3.6:
                return
            ensure_ln(2, 0)
            ln_pipeline(0, 128, 4, post=lambda t: transposes_to_aT(4, 128, tiles=[t]))
            ensure_ln(3, 1)
            dfr = ffn(2, 512, 128, 4, ln_fused=True)

            def postB(t):
                P.dma("sp", lambda e: e.dma_start(out=out_d[s, q0 + t * 128:q0 + (t + 1) * 128, :], in_=hc[:, t, :]), f"sthc{t}", reads=[hcB[t]])
            ln_pipeline(1, 128, 4, pre=dfr, post=postB, have_h0=True)

        pass

        def rope4(x1, x2, cs, sn, d1, d2, tvf, rds, pool_ok=False):
            tv = [tvf(i) for i in range(4)]
            e2 = "pool" if pool_ok else "dve"
            P.op("dve", lambda e: e.tensor_tensor(out=tv[0], in0=x1, in1=cs, op=ALU.mult), reads=rds + [cB], writes=[sB["rt0"]])
            P.op(e2, lambda e: e.tensor_tensor(out=tv[1], in0=x2, in1=sn, op=ALU.mult), reads=rds + [cB], writes=[sB["rt1"]])
            P.op("dve", lambda e: e.tensor_tensor(out=tv[2], in0=x1, in1=sn, op=ALU.mult), reads=rds + [cB], writes=[sB["rt2"]])
            P.op(e2, lambda e: e.tensor_tensor(out=tv[3], in0=x2, in1=cs, op=ALU.mult), reads=rds + [cB], writes=[sB["rt3"]])
            P.op("dve", lambda e: e.tensor_tensor(out=d1, in0=tv[0], in1=tv[1], op=ALU.subtract), reads=[sB["rt0"], sB["rt1"]], writes=[sB["rotb"]])
            P.op("dve", lambda e: e.tensor_tensor(out=d2, in0=tv[2], in1=tv[3], op=ALU.add), reads=[sB["rt2"], sB["rt3"]], writes=[sB["rotb"]])

        if stop >= 1:
            passA(lambda t: meta_d, NMETA, NMETA, 1, NKT, None, False)
        for s in range(nseq):
            for c in range(nch):
                if stop >= 2:
                    passA(lambda t, s=s, c=c: x_d[s, c * 512 + t * 128:c * 512 + (t + 1) * 128, :], 512, 128, 4, c * 4, c * 4, True)
            for c in range(nch):
                if stop >= 3:
                    passB(s, c)
        if stop >= 99:
            assert S13.taken == len(S13.items) and S2.taken == len(S2.items), (S13.taken, len(S13.items), S2.taken, len(S2.items))
        P.emit()
    return nc


_CACHE = {}


def _rope_tables(SEQ=SEQ):
    def tab(rot_dim):
        axis_dim = rot_dim // 2
        inv = (10000.0 ** (-np.arange(0, axis_dim, 2, dtype=np.float32) / np.float32(axis_dim))).astype(np.float32)
        rows = np.repeat(np.arange(SEQ // 64, dtype=np.float32), 64)
        cols = np.tile(np.arange(64, dtype=np.float32), SEQ // 64)
        ang = np.concatenate([rows[:, None] * inv[None, :], cols[:, None] * inv[None, :]], axis=-1).astype(np.float32)
        return np.concatenate([np.cos(ang), np.sin(ang)], axis=-1).astype(np.float32)
    return tab(64), tab(32)


def kernel(**inputs):
    n = 8
    if "nc" not in _CACHE:
        _CACHE["nc"] = build_program()
    nc = _CACHE["nc"]
    x = np.ascontiguousarray(inputs["x"], dtype=np.float32)
    ropeA, ropeB = _rope_tables()
    shared = {
        "meta": np.ascontiguousarray(inputs["meta_tokens"], dtype=np.float32),
        "w_in": np.ascontiguousarray(inputs["w_in"][0]),
        "w_uq": np.ascontiguousarray(inputs["w_uq"][0]),
        "w_ukv": np.ascontiguousarray(inputs["w_ukv"][0]),
        "w_out": np.ascontiguousarray(inputs["w_out"][0]),
        "ropeA": ropeA, "ropeB": ropeB,
        "ident": np.eye(128, dtype=np.float32),
    }
    for f in (1, 2):
        for k in ("w1", "w3", "w2"):
            shared[f"f{f}{k}"] = np.ascontiguousarray(inputs[f"ffn{f}_{k}"][0])
    for i in (1, 2, 3):
        shared[f"ln{i}_g"] = np.ascontiguousarray(inputs[f"ln{i}_g"]).reshape(1, D)
        shared[f"ln{i}_b"] = np.ascontiguousarray(inputs[f"ln{i}_b"]).reshape(1, D)
    for k in ("q_norm_a", "k_norm_a", "cq_norm", "ckv_norm", "out_norm_a", "out_norm_b"):
        shared[k] = np.ascontiguousarray(inputs[k]).reshape(1, -1)
    in_maps = []
    for i in range(n):
        m = dict(shared)
        m["x"] = x[i * NSEQ:(i + 1) * NSEQ]
        in_maps.append(m)
    res = run_bass_kernel_spmd(nc, in_maps, core_ids=list(range(n)))
    return np.concatenate([np.asarray(r["out"]) for r in res.results], axis=0).astype(np.float32)
```
